# Optimizing a Trainium2 kernel written in Bass

```python
import math
import jax
import jax.numpy as jnp
from jax import lax
import numpy as np

D_MODEL = 1024
BATCH = 8
SEQ = 2048
DEPTH = 2
DEC_BATCH = 32
DEC_SEQ = 1
PAST_LEN = 8192
PAGE_SIZE = 128

H_A = 4
DH_A = 64
DV_A = 2 * DH_A
D_A = H_A * DV_A
H_B = 8
P_B = 64
D_B = H_B * P_B
G_B = 2
N_B = 128
CONV_B = D_B + 2 * G_B * N_B
H_C = 4
DK_C = 128
DV_C = 128
D_C = H_C * DV_C
CONV_C = 2 * H_C * DK_C + D_C
CONV_W = 4
D_MIX = D_A + D_B + D_C
IN_SPLITS = (2 * H_A * DH_A, 2 * H_A * DH_A, D_A, CONV_B, H_B, CONV_C, H_C, H_C, D_MIX)
N_IN = sum(IN_SPLITS)
REL_BUCKETS = 32
REL_MAX_DIST = 128
Q_BLOCK = 128
CHUNK = 64
EPS = 1e-6
POOL_NUM = 5
POOL_DEN = 4

kernel_name = 'hybrid_diffattn_ssd_gdn_decoder_step'


def rms_norm(x):
    xf = x.astype(jnp.float32)
    return (xf * lax.rsqrt(jnp.mean(xf * xf, axis=-1, keepdims=True) + EPS)).astype(x.dtype)


def l2_norm(x):
    return x * lax.rsqrt(jnp.sum(x * x, axis=-1, keepdims=True) + EPS)


def causal_conv(x, buf, w):
    L = x.shape[1]
    xp = jnp.concatenate([buf.astype(x.dtype), x], axis=1)
    y = xp[:, 0:L] * w[0]
    for i in range(1, CONV_W):
        y = y + xp[:, i:i + L] * w[i]
    return y, xp[:, L:]


def rel_pos_bias(q_pos, k_pos, table):
    n = jnp.maximum(q_pos[:, None] - k_pos[None, :], 0)
    exact = REL_BUCKETS // 2
    large = exact + (jnp.log(jnp.maximum(n, 1).astype(jnp.float32) / exact)
                     / math.log(REL_MAX_DIST / exact) * (REL_BUCKETS - exact)).astype(jnp.int32)
    bucket = jnp.where(n < exact, n, jnp.minimum(large, REL_BUCKETS - 1))
    return jnp.moveaxis(table[bucket].astype(jnp.float32), -1, 0)


def diff_attn_block(q, k, v, q_pos, k_pos, table, lam):
    logits = jnp.einsum('bqhmd,bkhmd->bmhqk', q, k).astype(jnp.float32) * (DH_A ** -0.5)
    logits = logits + rel_pos_bias(q_pos, k_pos, table)
    causal = k_pos[None, :] <= q_pos[:, None]
    logits = jnp.where(causal, logits, jnp.finfo(jnp.float32).min)
    p = jax.nn.softmax(logits, axis=-1)
    wts = p[:, 0] - lam * p[:, 1]
    return jnp.einsum('bhqk,bkhd->bqhd', wts.astype(v.dtype), v)


def diff_attention(q, k, v, q_pos, k_pos, table, lam):
    b, lq = q.shape[:2]
    qb = Q_BLOCK if lq % Q_BLOCK == 0 else lq
    nb = lq // qb
    q_blocks = jnp.moveaxis(q.reshape(b, nb, qb, H_A, 2, DH_A), 1, 0)
    pos_blocks = q_pos.reshape(nb, qb)
    o = lax.map(lambda a: diff_attn_block(a[0], k, v, a[1], k_pos, table, lam), (q_blocks, pos_blocks))
    return jnp.moveaxis(o, 0, 1).reshape(b, lq, H_A, DV_A)


def ssd_chunked(x, dt, A, Bm, Cm, h0):
    b, L = x.shape[:2]
    q = CHUNK if L % CHUNK == 0 else L
    nc = L // q
    a_cum = jnp.cumsum((dt * A).reshape(b, nc, q, H_B), axis=2)
    xc = (x * dt[..., None]).reshape(b, nc, q, H_B, P_B)
    Bc = Bm.reshape(b, nc, q, H_B, N_B)
    Cc = Cm.reshape(b, nc, q, H_B, N_B)
    causal = jnp.tril(jnp.ones((q, q), bool))[None, None, :, :, None]
    seg = a_cum[:, :, :, None, :] - a_cum[:, :, None, :, :]
    decay = jnp.exp(jnp.where(causal, seg, -jnp.inf))
    scores = jnp.einsum('bclhn,bcshn->bclsh', Cc, Bc) * decay
    y_diag = jnp.einsum('bclsh,bcshp->bclhp', scores, xc)
    chunk_states = jnp.einsum('bclhn,bclh,bclhp->bchpn', Bc, jnp.exp(a_cum[:, :, -1:] - a_cum), xc)
    chunk_decay = jnp.exp(a_cum[:, :, -1])

    def step(h, inp):
        s, d = inp
        return h * d[:, :, None, None] + s, h

    h1, h_in = lax.scan(step, h0, (jnp.moveaxis(chunk_states, 1, 0), jnp.moveaxis(chunk_decay, 1, 0)))
    y_off = jnp.einsum('bclhn,cbhpn,bclh->bclhp', Cc, h_in, jnp.exp(a_cum))
    return (y_diag + y_off).reshape(b, L, H_B, P_B), h1


def gated_delta_chunked(q, k, v, g, beta, S0):
    b, L = q.shape[:2]
    cs = CHUNK if L % CHUNK == 0 else L
    nc = L // cs

    def blk(t):
        t = jnp.moveaxis(t, 2, 1)
        return t.reshape(t.shape[:2] + (nc, cs) + t.shape[3:])

    q, k, v, g, beta = [blk(t) for t in (q, k, v, g, beta)]
    g_cum = jnp.cumsum(g, axis=-1)
    incl = jnp.tril(jnp.ones((cs, cs), bool))
    strict = jnp.tril(jnp.ones((cs, cs), bool), -1)
    decay = jnp.exp(jnp.where(incl, g_cum[..., :, None] - g_cum[..., None, :], -jnp.inf))
    kk = jnp.einsum('bhcld,bhcsd->bhcls', k, k)
    lower = jnp.where(strict, beta[..., :, None] * kk * decay, 0.0)
    eye = jnp.eye(cs, dtype=q.dtype)
    rhs = jnp.concatenate([v * beta[..., None], k * (beta * jnp.exp(g_cum))[..., None]], axis=-1)
    sol = lax.linalg.triangular_solve(lower + eye, rhs, left_side=True, lower=True, unit_diagonal=True)
    u, w = sol[..., :DV_C], sol[..., DV_C:]
    attn = jnp.einsum('bhcld,bhcsd->bhcls', q, k) * decay
    qg = q * jnp.exp(g_cum)[..., None]
    kd = k * jnp.exp(g_cum[..., -1:] - g_cum)[..., None]
    g_last = jnp.exp(g_cum[..., -1])

    def step(S, inp):
        qg_c, kd_c, u_c, w_c, attn_c, gl_c = inp
        v_new = u_c - jnp.einsum('bhlk,bhkv->bhlv', w_c, S)
        o = jnp.einsum('bhlk,bhkv->bhlv', qg_c, S) + jnp.einsum('bhls,bhsv->bhlv', attn_c, v_new)
        S = S * gl_c[..., None, None] + jnp.einsum('bhlk,bhlv->bhkv', kd_c, v_new)
        return S, o

    seq_in = tuple(jnp.moveaxis(t, 2, 0) for t in (qg, kd, u, w, attn, g_last))
    S1, o = lax.scan(step, S0, seq_in)
    o = jnp.transpose(o, (1, 0, 3, 2, 4)).reshape(b, L, H_C, DV_C)
    return o, S1


def mamba2_branch(xbc, dt_raw, z, conv0, h0, conv_w, conv_b, dt_bias, A_log, D_skip, norm_g):
    b, L = xbc.shape[:2]
    f32 = jnp.float32
    xbc, conv1 = causal_conv(xbc, conv0, conv_w)
    xbc = jax.nn.silu(xbc + conv_b)
    xs, Bm, Cm = jnp.split(xbc, [D_B, D_B + G_B * N_B], axis=-1)
    xs = xs.reshape(b, L, H_B, P_B).astype(f32)
    Bm = jnp.repeat(Bm.reshape(b, L, G_B, N_B), H_B // G_B, axis=2).astype(f32)
    Cm = jnp.repeat(Cm.reshape(b, L, G_B, N_B), H_B // G_B, axis=2).astype(f32)
    dt = jax.nn.softplus(dt_raw.astype(f32) + dt_bias.astype(f32))
    A = -jnp.exp(A_log.astype(f32))
    y, h1 = ssd_chunked(xs, dt, A, Bm, Cm, h0.astype(f32))
    y = y + xs * D_skip.astype(f32)[:, None]
    y = y.reshape(b, L, D_B).astype(z.dtype) * jax.nn.silu(z)
    y = rms_norm(y.reshape(b, L, G_B, D_B // G_B)).reshape(b, L, D_B) * norm_g
    return y, conv1, h1


def gdn_branch(qkv, beta_raw, a_raw, z, conv0, S0, conv_w, dt_bias, A_log, norm_g):
    b, L = qkv.shape[:2]
    f32 = jnp.float32
    qkv, conv1 = causal_conv(qkv, conv0, conv_w)
    qkv = jax.nn.silu(qkv).astype(f32)
    q, k, v = jnp.split(qkv, [H_C * DK_C, 2 * H_C * DK_C], axis=-1)
    q = l2_norm(q.reshape(b, L, H_C, DK_C)) * (DK_C ** -0.5)
    k = l2_norm(k.reshape(b, L, H_C, DK_C))
    v = v.reshape(b, L, H_C, DV_C)
    beta = jax.nn.sigmoid(beta_raw.astype(f32))
    g = -jnp.exp(A_log.astype(f32)) * jax.nn.softplus(a_raw.astype(f32) + dt_bias.astype(f32))
    o, S1 = gated_delta_chunked(q, k, v, g, beta, S0.astype(f32))
    o = (rms_norm(o) * norm_g).reshape(b, L, D_C).astype(z.dtype) * jax.nn.silu(z)
    return o, conv1, S1


def mixer_layer(x, c, k_past, v_past, ssm_conv0, ssm_h0, gdn_conv0, gdn_S0, w, table, lam_init):
    b, L, _ = x.shape
    P = k_past.shape[1]
    mod = jnp.dot(jax.nn.silu(c), w['ada']) + w['ada_b']
    shift, scale, gate = jnp.split(mod, 3, axis=-1)
    h = rms_norm(x) * (1 + scale[:, None]) + shift[:, None]
    proj = jnp.dot(h, w['in'])
    idx = [int(s) for s in np.cumsum(IN_SPLITS)[:-1]]
    q_a, k_a, v_a, xbc, dt_raw, qkv_c, beta_raw, a_raw, z = jnp.split(proj, idx, axis=-1)
    z_a, z_b, z_c = jnp.split(z, [D_A, D_A + D_B], axis=-1)

    k_new = k_a.reshape(b, L, H_A, 2 * DH_A)
    v_new = v_a.reshape(b, L, H_A, DV_A)
    k_all = jnp.concatenate([k_past.astype(k_new.dtype), k_new], axis=1).reshape(b, P + L, H_A, 2, DH_A)
    v_all = jnp.concatenate([v_past.astype(v_new.dtype), v_new], axis=1)
    q_pos = P + jnp.arange(L)
    k_pos = jnp.arange(P + L)
    lv = w['lam'].astype(jnp.float32)
    lam = jnp.exp(jnp.sum(lv[0] * lv[1])) - jnp.exp(jnp.sum(lv[2] * lv[3])) + lam_init
    o_a = diff_attention(q_a.reshape(b, L, H_A, 2, DH_A), k_all, v_all, q_pos, k_pos, table, lam)
    o_a = (rms_norm(o_a) * w['subln'] * (1 - lam_init)).reshape(b, L, D_A) * jax.nn.silu(z_a)

    o_b, ssm_conv1, ssm_h1 = mamba2_branch(xbc, dt_raw, z_b, ssm_conv0, ssm_h0, w['ssm_conv_w'], w['ssm_conv_b'],
                                           w['ssm_dt_bias'], w['ssm_A_log'], w['ssm_D'], w['ssm_norm_g'])
    o_c, gdn_conv1, gdn_S1 = gdn_branch(qkv_c, beta_raw, a_raw, z_c, gdn_conv0, gdn_S0, w['gdn_conv_w'],
                                        w['gdn_dt_bias'], w['gdn_A_log'], w['gdn_norm_g'])

    out = jnp.dot(jnp.concatenate([o_a, o_b, o_c], axis=-1), w['out'])
    x = x + gate[:, None] * out
    return x, (k_new, v_new, ssm_h1, ssm_conv1, gdn_S1, gdn_conv1)


def setup_inputs(seed: int = 0) -> dict:
    key = jax.random.key(seed)
    ks = jax.random.split(key, 32)
    f32 = jnp.float32
    n_pages = PAST_LEN // PAGE_SIZE
    n_pool = (DEC_BATCH * n_pages * POOL_NUM) // POOL_DEN

    def nrm(k, shape, s):
        return jax.random.normal(k, shape, f32) * s

    def inv_softplus_dt(k, shape):
        dt = jnp.exp(jax.random.uniform(k, shape, f32, math.log(1e-3), math.log(1e-1)))
        return jnp.log(jnp.expm1(dt))

    page_table = jax.random.permutation(ks[10], n_pool)[:DEC_BATCH * n_pages]
    page_table = page_table.reshape(DEC_BATCH, n_pages).astype(jnp.int32)
    return {
        'x_prompt': nrm(ks[0], (BATCH, SEQ, D_MODEL), 1.0),
        'x_sample': nrm(ks[1], (DEC_BATCH, DEC_SEQ, D_MODEL), 1.0),
        'c_prompt': nrm(ks[2], (BATCH, D_MODEL), 1.0),
        'c_sample': nrm(ks[3], (DEC_BATCH, D_MODEL), 1.0),
        'cache_k': nrm(ks[4], (DEPTH, n_pool, PAGE_SIZE, H_A, 2 * DH_A), 1.0),
        'cache_v': nrm(ks[5], (DEPTH, n_pool, PAGE_SIZE, H_A, DV_A), 1.0),
        'state_ssm': nrm(ks[6], (DEPTH, DEC_BATCH, H_B, P_B, N_B), 0.1),
        'state_ssm_conv': nrm(ks[7], (DEPTH, DEC_BATCH, CONV_W - 1, CONV_B), 1.0),
        'state_gdn': nrm(ks[8], (DEPTH, DEC_BATCH, H_C, DK_C, DV_C), 0.1),
        'state_gdn_conv': nrm(ks[9], (DEPTH, DEC_BATCH, CONV_W - 1, CONV_C), 1.0),
        'page_table': page_table,
        'w_ada': nrm(ks[11], (DEPTH, D_MODEL, 3 * D_MODEL), 0.5 * D_MODEL ** -0.5),
        'b_ada': nrm(ks[12], (DEPTH, 3 * D_MODEL), 0.02),
        'w_in': nrm(ks[13], (DEPTH, D_MODEL, N_IN), D_MODEL ** -0.5),
        'w_out': nrm(ks[14], (DEPTH, D_MIX, D_MODEL), D_MIX ** -0.5),
        'rel_bias': nrm(ks[15], (REL_BUCKETS, H_A), 0.5),
        'attn_lambda': nrm(ks[16], (DEPTH, 4, DH_A), 0.1),
        'attn_subln_g': 1.0 + nrm(ks[17], (DEPTH, DV_A), 0.02),
        'ssm_conv_w': nrm(ks[18], (DEPTH, CONV_W, CONV_B), CONV_W ** -0.5),
        'ssm_conv_b': nrm(ks[19], (DEPTH, CONV_B), 0.02),
        'ssm_dt_bias': inv_softplus_dt(ks[20], (DEPTH, H_B)),
        'ssm_A_log': jnp.log(jax.random.uniform(ks[21], (DEPTH, H_B), f32, 1.0, 16.0)),
        'ssm_D': 1.0 + nrm(ks[22], (DEPTH, H_B), 0.1),
        'ssm_norm_g': 1.0 + nrm(ks[23], (DEPTH, D_B), 0.02),
        'gdn_conv_w': nrm(ks[24], (DEPTH, CONV_W, CONV_C), CONV_W ** -0.5),
        'gdn_dt_bias': inv_softplus_dt(ks[25], (DEPTH, H_C)),
        'gdn_A_log': jnp.log(jax.random.uniform(ks[26], (DEPTH, H_C), f32, 1.0, 16.0)),
        'gdn_norm_g': 1.0 + nrm(ks[27], (DEPTH, DV_C), 0.02),
        'final_norm_g': 1.0 + nrm(ks[28], (D_MODEL,), 0.02),
    }


def reference(x_prompt, x_sample, c_prompt, c_sample, cache_k, cache_v, state_ssm, state_ssm_conv,
              state_gdn, state_gdn_conv, page_table, w_ada, b_ada, w_in, w_out, rel_bias, attn_lambda,
              attn_subln_g, ssm_conv_w, ssm_conv_b, ssm_dt_bias, ssm_A_log, ssm_D, ssm_norm_g,
              gdn_conv_w, gdn_dt_bias, gdn_A_log, gdn_norm_g, final_norm_g):
    bp = x_prompt.shape[0]
    bs = x_sample.shape[0]
    past = page_table.shape[1] * PAGE_SIZE
    dtp = x_prompt.dtype
    xp, xs = x_prompt, x_sample
    new_p, new_s = [], []
    for l in range(DEPTH):
        w = {'ada': w_ada[l], 'ada_b': b_ada[l], 'in': w_in[l], 'out': w_out[l], 'lam': attn_lambda[l],
             'subln': attn_subln_g[l], 'ssm_conv_w': ssm_conv_w[l], 'ssm_conv_b': ssm_conv_b[l],
             'ssm_dt_bias': ssm_dt_bias[l], 'ssm_A_log': ssm_A_log[l], 'ssm_D': ssm_D[l],
             'ssm_norm_g': ssm_norm_g[l], 'gdn_conv_w': gdn_conv_w[l], 'gdn_dt_bias': gdn_dt_bias[l],
             'gdn_A_log': gdn_A_log[l], 'gdn_norm_g': gdn_norm_g[l]}
        lam_init = 0.8 - 0.6 * math.exp(-0.3 * l)
        xp, st_p = mixer_layer(xp, c_prompt,
                               jnp.zeros((bp, 0, H_A, 2 * DH_A), dtp), jnp.zeros((bp, 0, H_A, DV_A), dtp),
                               jnp.zeros((bp, CONV_W - 1, CONV_B), dtp), jnp.zeros((bp, H_B, P_B, N_B), jnp.float32),
                               jnp.zeros((bp, CONV_W - 1, CONV_C), dtp), jnp.zeros((bp, H_C, DK_C, DV_C), jnp.float32),
                               w, rel_bias, lam_init)
        new_p.append(st_p)
        k_past = cache_k[l, page_table].reshape(bs, past, H_A, 2 * DH_A)
        v_past = cache_v[l, page_table].reshape(bs, past, H_A, DV_A)
        xs, st_s = mixer_layer(xs, c_sample, k_past, v_past, state_ssm_conv[l], state_ssm[l],
                               state_gdn_conv[l], state_gdn[l], w, rel_bias, lam_init)
        new_s.append(st_s)
    y_prompt = rms_norm(xp) * final_norm_g
    y_sample = rms_norm(xs) * final_norm_g
    k_p, v_p, ssm_p, ssm_conv_p, gdn_p, gdn_conv_p = [jnp.stack(a) for a in zip(*new_p)]
    k_s, v_s, ssm_s, ssm_conv_s, gdn_s, gdn_conv_s = [jnp.stack(a) for a in zip(*new_s)]
    return (y_prompt, y_sample, k_p, v_p, ssm_p, ssm_conv_p, gdn_p, gdn_conv_p,
            k_s, v_s, ssm_s, ssm_conv_s, gdn_s, gdn_conv_s)
```

```python
import math
from contextlib import ExitStack
import numpy as np
import concourse.bass as bass
import concourse.mybir as mybir
from concourse.bass_utils import run_bass_kernel_spmd

F32 = mybir.dt.float32
BF16 = mybir.dt.bfloat16
I32 = mybir.dt.int32
ALU = mybir.AluOpType
AF = mybir.ActivationFunctionType
AX = mybir.AxisListType

NCORE = 8
D = 1024
L = 2048
NT = 16
DEPTH = 2
NS = 32
NO = 4
NPAGE = 64
NPOOL = 2560
TPC = 16
OQ, OK_, OV, OXBC, ODT, OQKV, OBETA, OA, OZ = 0, 512, 1024, 1536, 2560, 2568, 4104, 4108, 4112
NIN = 5648
EPS = 1e-6
RSW = 4104
DBG_TILES = None
DBG_STOP = 0
DBG_A2 = None
DBG_VAR = 0


class Sched:
    ENG = ("pe", "dve", "act", "pool", "sp")

    def __init__(self, nc, stack):
        self.nc = nc
        self.stack = stack
        self.engs = {"pe": nc.tensor, "dve": nc.vector, "act": nc.scalar, "pool": nc.gpsimd, "sp": nc.sync}
        self.sem = {}
        self.cnt = {}
        for e in self.ENG:
            self.sem[e] = stack.enter_context(nc.semaphore("prog_" + e))
            self.cnt[e] = 0
        self.seen = {e: {} for e in self.ENG}
        self.last_write = {}
        self.readers = {}
        self.nops = 0
        self.nwaits = 0
        self.misc_map = {}
        self.excl = set()

    DEDICATED = ("wbuf", "wout", "wa", "kbuf", "vbuf", "kvout", "yp", "qbc", "rsin_", "hst", "ssms", "Sall", "gdns")
    NMISC = 12

    def _slot(self, name, q="sp"):
        if not name.startswith(self.DEDICATED):
            key = (name, q == "pool")
            if key not in self.misc_map:
                n = sum(1 for k in self.misc_map if k[1] == key[1])
                self.misc_map[key] = ("miscp%d" % (n % 3)) if key[1] else ("misc%d" % (n % self.NMISC))
            name = self.misc_map[key]
        k = "d_" + name
        if k not in self.sem:
            self.sem[k] = self.stack.enter_context(self.nc.semaphore(k))
            self.cnt[k] = 0
        return k

    def _wait(self, eng, tok):
        if tok is None:
            return
        k, v = tok
        if self.seen[eng].get(k, 0) >= v:
            return
        self.engs[eng].wait_ge(self.sem[k], v)
        self.seen[eng][k] = v
        self.nwaits += 1

    def _deps(self, eng, reads, writes):
        for b in reads:
            self._wait(eng, self.last_write.get(b))
            if b in self.excl:
                for k, v in self.readers.get(b, {}).items():
                    if k != eng:
                        self._wait(eng, (k, v))
        for b in writes:
            self._wait(eng, self.last_write.get(b))
            for k, v in self.readers.get(b, {}).items():
                self._wait(eng, (k, v))

    def _mark(self, tok, reads, writes):
        k, v = tok
        for b in reads:
            r = self.readers.setdefault(b, {})
            r[k] = max(r.get(k, 0), v)
        for b in writes:
            self.last_write[b] = tok
            self.readers[b] = {}

    def op(self, eng, fn, reads=(), writes=()):
        self._deps(eng, reads, writes)
        inst = fn(self.engs[eng])
        self.cnt[eng] += 1
        inst.then_inc(self.sem[eng], 1)
        self._mark((eng, self.cnt[eng]), reads, writes)
        self.nops += 1
        return inst

    def dma(self, q, out, in_, reads=(), writes=(), slot=None, fn=None):
        sk = self._slot(slot, q)
        self._deps(q, reads, writes)
        if self.cnt[sk] > 0:
            self._wait(q, (sk, self.cnt[sk]))
        if fn is None:
            inst = self.engs[q].dma_start(out=out, in_=in_)
        else:
            inst = fn(self.engs[q])
        self.cnt[sk] += 16
        inst.then_inc(self.sem[sk], 16)
        self._mark((sk, self.cnt[sk]), reads, writes)
        self.nops += 1
        return inst

    def dma_group(self, q, items, reads=(), writes=(), slot=None):
        sk = self._slot(slot, q)
        self._deps(q, reads, writes)
        if self.cnt[sk] > 0:
            self._wait(q, (sk, self.cnt[sk]))
        for (o, i) in items:
            inst = self.engs[q].dma_start(out=o, in_=i)
            self.cnt[sk] += 16
            inst.then_inc(self.sem[sk], 16)
        self._mark((sk, self.cnt[sk]), reads, writes)
        self.nops += len(items)

    def barrier(self):
        for e in self.ENG:
            for k in list(self.sem.keys()):
                if k != e and self.cnt[k] > 0:
                    self._wait(e, (k, self.cnt[k]))

    def finish(self, eng="sp"):
        for k in list(self.sem.keys()):
            if self.cnt[k] > 0 and k != eng:
                self._wait(eng, (k, self.cnt[k]))


def build_program(npool=NPOOL, do_sample=True, do_branches="ABC", depth=DEPTH):
    nc = bass.Bass("TRN2", target_bir_lowering=False)

    def din(name, shape, dt=F32):
        return nc.dram_tensor(name, list(shape), dt, kind="ExternalInput").ap()

    def dout(name, shape, dt=F32):
        return nc.dram_tensor(name, list(shape), dt, kind="ExternalOutput").ap()

    xp = din("xp", [L, D]); cp = din("cp", [1, D])
    xs_all = din("xs_all", [NS, D]); cs_all = din("cs_all", [NS, D])
    xs_own = din("xs_own", [NO, D]); cs_own = din("cs_own", [NO, D])
    poolk = din("poolk", [2 * npool * 128, 512]); poolv = din("poolv", [2 * npool * 128, 512])
    pt = din("pt", [1, NO * NPAGE], I32)
    st_ssm = din("st_ssm", [2, NO, 8, 64, 128]); st_sconv = din("st_sconv", [2, NO, 3, 1024])
    st_gdn = din("st_gdn", [2, NO, 4, 128, 128]); st_gconv = din("st_gconv", [2, NO, 3, 1536])
    w_ada = din("w_ada", [2, D, 3 * D]); b_ada = din("b_ada", [2, 3 * D])
    w_in = din("w_in", [2, D, NIN]); w_out = din("w_out", [2, 1536, D])
    rel_bias = din("rel_bias", [32, 4]); attn_lambda = din("attn_lambda", [2, 256])
    subln = din("subln", [2, 128])
    ssm_conv_w = din("ssm_conv_w", [2, 4 * 1024]); ssm_conv_b = din("ssm_conv_b", [2, 1024])
    ssm_dt_bias = din("ssm_dt_bias", [2, 8]); ssm_A_log = din("ssm_A_log", [2, 8]); ssm_D = din("ssm_D", [2, 8])
    ssm_norm_g = din("ssm_norm_g", [2, 512])
    gdn_conv_w = din("gdn_conv_w", [2, 4 * 1536]); gdn_dt_bias = din("gdn_dt_bias", [2, 4])
    gdn_A_log = din("gdn_A_log", [2, 4]); gdn_norm_g = din("gdn_norm_g", [2, 128])
    final_g = din("final_g", [1, D])
    cmat = din("cmat", [128, 5, 128])
    oh1d = din("oh1d", [33, 512])
    ohs = din("ohs", [32, 128])
    cvec = din("cvec", [128, 16])
    lamc = din("lamc", [1, 4])
    gmask_d = din("gmask", [128, 4, 128])
    selown_d = din("selown", [NO, NS])

    y_p = dout("y_p", [L, D]); y_s = dout("y_s", [NO, D])
    k_p = dout("k_p", [2, L, 512]); v_p = dout("v_p", [2, L, 512])
    ssm_p = dout("ssm_p", [2, 8, 64, 128]); sconv_p = dout("sconv_p", [2, 3, 1024])
    gdn_p = dout("gdn_p", [2, 4, 128, 128]); gconv_p = dout("gconv_p", [2, 3, 1536])
    k_s = dout("k_s", [2, NO, 512]); v_s = dout("v_s", [2, NO, 512])
    ssm_s = dout("ssm_s", [2, NO, 8, 64, 128]); sconv_s = dout("sconv_s", [2, NO, 3, 1024])
    gdn_s = dout("gdn_s", [2, NO, 4, 128, 128]); gconv_s = dout("gconv_s", [2, NO, 3, 1536])

    bvec_d = nc.dram_tensor("bvec_d", [4, 512], F32)
    rs_out = [nc.dram_tensor("rs_loc%d" % l, [NO, RSW], F32) for l in range(2)]
    ag_in = nc.dram_tensor("ag_in", [NS, D], F32)
    ag_out = nc.dram_tensor("ag_out", [NS, D], F32)
    q_d = [nc.dram_tensor("q_d%d" % l, [NO, 512], F32) for l in range(2)]
    gate_d = nc.dram_tensor("gate_d", [NO, D], F32)

    with ExitStack() as st:
        S = Sched(nc, st)
        cc_sem = st.enter_context(nc.semaphore("cc_sem"))
        cc_cnt = [0]

        _uid = [0]

        def T(stack, name, shape, dt=F32):
            _uid[0] += 1
            return stack.enter_context(nc.sbuf_tensor("%s_%d" % (name, _uid[0]), list(shape), dt))

        def V(fn, r=(), w=()): return S.op("dve", fn, r, w)
        def A(fn, r=(), w=()): return S.op("act", fn, r, w)
        def G(fn, r=(), w=()): return S.op("pool", fn, r, w)
        def PE(fn, r=(), w=()): return S.op("pe", fn, r, w)

        bank = [st.enter_context(nc.psum_tensor("bk%d" % i, [128, 512], F32)) for i in range(7)]
        trbank = st.enter_context(nc.psum_tensor("trbank", [128, 512], F32))
        _trv = trbank[:, :].bitcast(BF16)
        trb = [_trv[:, 0:512], _trv[:, 512:1024]]
        BK = ["bk%d" % i for i in range(7)]
        TRB = ["trb", "trb"]
        S.excl = set(BK) | {"trb"}

        x_sb = T(st, "x_sb", [128, NT, D])
        hT = T(st, "hT", [128, 8, L], BF16)
        wbuf = T(st, "wbuf", [128, 8, 2056], BF16)
        wout = T(st, "wout", [128, 4, D], BF16)
        cm = T(st, "cm", [128, 5, 128])
        identb = T(st, "identb", [128, 128], BF16)
        onesb = T(st, "onesb", [128, 128], BF16)
        onesf = T(st, "onesf", [128, 128])
        BT = T(st, "BT", [128, 4, 256])
        c31 = T(st, "c31", [128, 4])
        cv = T(st, "cv", [128, 16])
        lamc_sb = T(st, "lamc_sb", [128, 4])
        xo = T(st, "xo", [NO, D])
        cT_p = T(st, "cT_p", [128, 8, 1], BF16)
        cT_o = T(st, "cT_o", [128, 8, NO], BF16)
        cT_a = T(st, "cT_a", [128, 8, NS], BF16)
        sel4 = T(st, "sel4", [NO, NO, 128])
        eye4 = T(st, "eye4", [128, NO, NO])
        idx = T(st, "idx", [128, 2, NO * NPAGE], I32)
        bias_sm = T(st, "bias_sm", [128, NPAGE, 8])
        b0row = T(st, "b0row", [NO, 4])
        hTo = T(st, "hTo", [128, 8, NO], BF16)
        hTa = T(st, "hTa", [128, 8, NS], BF16)
        scT = T(st, "scT", [128, 16])
        gate_bc = T(st, "gate_bc", [128, D])
        lamv = T(st, "lamv", [128, 4])
        sgl = T(st, "sgl", [128, 128])
        ident = cm[:, 0, :]
        Umat = cm[:, 1, :]
        negmask = cm[:, 2, :]
        strictL = cm[:, 3, :]
        Jmat = cm[:, 4, :]

        def bcast_row(dst, src_row_ap, n, key, q="sp"):
            P = dst.shape[0]
            S.dma(q, dst, src_row_ap.to_broadcast([P, n]), writes=[key], slot=key)

        def transpose_f32(dst_psum, src, rk, wk, rows=None):
            PE(lambda e: e.transpose(out=dst_psum, in_=src, identity=ident if rows is None else cm[0:rows, 0, 0:rows]),
               list(rk) + ["cm"], wk)

        def rstd_from_ss(dst, ss, n, keys_r, key_w, shape_eng="act"):
            A(lambda e: e.activation(out=dst, in_=ss, func=AF.Sqrt, scale=1.0 / n, bias=epsc[0:dst.shape[0], 0:1]), list(keys_r) + ["epsc"], [key_w])
            V(lambda e: e.reciprocal(out=dst, in_=dst), [key_w], [key_w])

        def softplus(dst, src, kr, kw):
            A(lambda e: e.activation(out=dst, in_=src, func=AF.Exp), kr, [kw])
            A(lambda e: e.activation(out=dst, in_=dst, func=AF.Ln, bias=onec[0:dst.shape[0], 0:1]), [kw, "epsc"], [kw])

        def collective(kind, op, in_t, out_t, rkeys, wkeys):
            for k in rkeys:
                S._wait("pool", S.last_write.get(k))
            for k in wkeys:
                S._wait("pool", S.last_write.get(k))
                for kk, vv in S.readers.get(k, {}).items():
                    S._wait("pool", (kk, vv))
            cc_cnt[0] += 1
            nc.gpsimd.collective_compute(kind, op, replica_groups=[list(range(NCORE))],
                                         ins=[in_t.ap().opt()], outs=[out_t.ap().opt()]).then_inc(cc_sem, 1)
            nc.gpsimd.wait_ge(cc_sem, cc_cnt[0])
            G(lambda e: e.memset(ccj[:], 0.0), [], ["ccj"] + list(wkeys))

        print("sbuf bytes remaining after persistent:", nc.sbuf_bytes_remaining, flush=True)
        epsc = T(st, "epsc", [128, 3])
        onec = epsc[:, 1:2]
        ccj = T(st, "ccj", [1, 2])
        V(lambda e: e.memset(epsc[:, 0:1], EPS), [], ["epsc"])
        V(lambda e: e.memset(epsc[:, 1:2], 1.0), ["epsc"], ["epsc"])
        V(lambda e: e.memset(epsc[:, 2:3], 128.0 * EPS), ["epsc"], ["epsc"])
        S.dma("sp", cm[:], cmat, writes=["cm"], slot="cm")
        S.dma("sp", cv[:], cvec, writes=["cv"], slot="cv")
        bcast_row(lamc_sb[:], lamc[0:1, :], 4, "lamc_sb")
        bcast_row(c31[:], rel_bias[31:32, :], 4, "c31")
        bcast_row(b0row[:], rel_bias[0:1, :], 4, "b0row")
        V(lambda e: e.tensor_copy(out=identb[:], in_=ident), ["cm"], ["identb"])
        V(lambda e: e.memset(onesb[:], 1.0), [], ["onesb"])
        V(lambda e: e.memset(onesf[:], 1.0), [], ["onesf"])
        V(lambda e: e.tensor_copy(out=sel4[:], in_=cm[0:NO, 0, 0:NO].unsqueeze(2).to_broadcast([NO, NO, 128])), ["cm"], ["sel4"])
        with ExitStack() as cs:
            tb = T(cs, "tb_tab", [33, 4])
            oh = T(cs, "oh_sb", [33, 512])
            bv = T(cs, "bv_sb", [4, 512])
            hk = T(cs, "hk_sb", [128, 4, 384])
            ohs_sb = T(cs, "ohs_sb", [32, 128])
            i4 = T(cs, "i4", [NO, NO * NO])
            V(lambda e: e.memset(i4[:], 0.0), [], ["i4"])
            for j in range(NO):
                V(lambda e: e.tensor_copy(out=i4[:, j * NO + j:j * NO + j + 1], in_=onesf[0:NO, 0:1]), ["i4", "onesf"], ["i4"])
            PE(lambda e: e.matmul(bank[0][:, 0:16], lhsT=sel4[:, 0, :], rhs=i4[:], start=True, stop=True), ["sel4", "i4"], [BK[0]])
            V(lambda e: e.tensor_copy(out=eye4[:].rearrange("p a b -> p (a b)"), in_=bank[0][:, 0:16]), [BK[0]], ["eye4"])
            V(lambda e: e.memset(tb[:], -30000.0), [], ["tb"])
            S.dma("sp", tb[0:32, :], rel_bias, reads=["tb"], writes=["tb"], slot="tb")
            S.dma("sp", oh[:], oh1d, writes=["oh"], slot="oh")
            S.dma("sp", ohs_sb[:], ohs, writes=["ohs"], slot="ohs")
            PE(lambda e: e.matmul(bank[1][0:4, :], lhsT=tb[:], rhs=oh[:], start=True, stop=True), ["tb", "oh"], [BK[1]])
            V(lambda e: e.tensor_copy(out=bv[:], in_=bank[1][0:4, :]), [BK[1]], ["bv"])
            S.dma("sp", bvec_d.ap(), bv[:], reads=["bv"], writes=["bvec_d"], slot="bvd")
            hsrc = bass.AP(tensor=bvec_d.ap().tensor, offset=0, ap=[[1, 128], [512, 4], [1, 384]])
            S.dma("sp", hk[:], hsrc, reads=["bvec_d"], writes=["hk"], slot="hk")
            for h in range(4):
                PE(lambda e: e.matmul(bank[2][:, 0:384], lhsT=Jmat, rhs=hk[:, h, :], start=True, stop=True), ["cm", "hk"], [BK[2]])
                V(lambda e: e.tensor_copy(out=BT[:, h, :], in_=bank[2][:, 0:256]), [BK[2]], ["BT"])
            for m_ in range(2):
                V(lambda e: e.tensor_copy(out=bias_sm[:].rearrange("p t (h m) -> p t h m", m=2)[:, :, :, m_], in_=c31[:].unsqueeze(1).to_broadcast([128, NPAGE, 4])), ["c31", "bias_sm"], ["bias_sm"])
            PE(lambda e: e.matmul(bank[3][:, 0:4], lhsT=ohs_sb[:, :], rhs=tb[0:32, :], start=True, stop=True), ["ohs", "tb"], [BK[3]])
            V(lambda e: e.tensor_copy(out=bias_sm[:, NPAGE - 1, :].rearrange("p (h m) -> p h m", m=2), in_=bank[3][:, 0:4].unsqueeze(2).to_broadcast([128, 4, 2])),
              [BK[3], "bias_sm"], ["bias_sm"])
            S.barrier()
        with ExitStack() as cs:
            if do_sample:
                ptb = T(cs, "ptb", [128, NO * NPAGE], I32)
                ptf = T(cs, "ptf", [128, NO * NPAGE])
                ptg = T(cs, "ptg", [128, NO * NPAGE])
                S.dma("sp", ptb[:], pt[0:1, :].to_broadcast([128, NO * NPAGE]), writes=["ptb"], slot="ptb")
                V(lambda e: e.tensor_copy(out=ptf[:], in_=ptb[:]), ["ptb"], ["ptf"])
                for l in range(2):
                    V(lambda e: e.tensor_scalar(out=ptg[:], in0=ptf[:], scalar1=128.0, scalar2=cv[:, l:l + 1], op0=ALU.mult, op1=ALU.add), ["ptf", "cv", "ptg"], ["ptg"])
                    V(lambda e: e.tensor_copy(out=idx[:, l, :], in_=ptg[:]), ["ptg"], ["idx"])
            S.barrier()
        with ExitStack() as cs:
            craw = T(cs, "craw", [NS, D])
            csil = T(cs, "csil", [NS, D], BF16)
            for (src, n, dst, nm) in ((cp, 1, cT_p, "cT_p"), (cs_own, NO, cT_o, "cT_o"), (cs_all, NS, cT_a, "cT_a")):
                S.dma("sp", craw[0:n, :], src, writes=["craw"], slot="craw")
                A(lambda e: e.activation(out=csil[0:n, :], in_=craw[0:n, :], func=AF.Silu), ["craw"], ["csil"])
                for k in range(8):
                    PE(lambda e: e.transpose(out=trb[0][:, k * NS:k * NS + n], in_=csil[0:n, k * 128:(k + 1) * 128], identity=identb[0:n, 0:n]),
                       ["csil", "identb"], [TRB[0]])
                V(lambda e: e.tensor_copy(out=dst[:], in_=trb[0][:, 0:8 * NS].rearrange("p (k s) -> p k s", s=NS)[:, :, 0:n]), [TRB[0]], [nm])
            S.barrier()

        for i in range(NT):
            S.dma("sp", x_sb[:, i, :], xp[i * 128:(i + 1) * 128, :], writes=["x%d" % i], slot="x%d" % i)
        S.dma("sp", xo[:], xs_own, writes=["xo"], slot="xo")

        def load_w(l, segs, key="wbuf"):
            items = []
            o = 0
            for (c0, n) in segs:
                for k in range(8):
                    items.append((wbuf[:, k, o:o + n], w_in[l, k * 128:(k + 1) * 128, c0:c0 + n]))
                o += n
            S.dma_group("pool", items, writes=[key], slot="wbuf")

        def load_wout(l, r0):
            items = [(wout[:, k, :], w_out[l, r0 + k * 128:r0 + (k + 1) * 128, :]) for k in range(4)]
            S.dma_group("pool", items, writes=["wout"], slot="wout")

        def proj_tok(dst_psum, dkey, lhs_fn, lkeys, wcol0, ncols, M=128):
            def f(e):
                for k in range(8):
                    i = e.matmul(dst_psum, lhsT=lhs_fn(k), rhs=wbuf[:, k, wcol0:wcol0 + ncols], start=(k == 0), stop=(k == 7))
                return i
            PE(f, list(lkeys) + ["wbuf"], [dkey])

        def proj_feat(dst_psum, dkey, wcol0, rhs_fn, rkeys, mcols=128):
            def f(e):
                for k in range(8):
                    i = e.matmul(dst_psum, lhsT=wbuf[:, k, wcol0:wcol0 + mcols], rhs=rhs_fn(k), start=(k == 0), stop=(k == 7))
                return i
            PE(f, list(rkeys) + ["wbuf"], [dkey])

        def outproj_update_prompt(i, oT, okey, banks=(5, 6)):
            for half in range(2):
                bk = banks[half]
                def f(e):
                    for k in range(4):
                        ins = e.matmul(bank[bk][:, :], lhsT=oT[:, k, :], rhs=wout[:, k, half * 512:(half + 1) * 512], start=(k == 0), stop=(k == 3))
                    return ins
                PE(f, [okey, "wout"], [BK[bk]])
                V(lambda e: e.tensor_tensor(out=optmp[:, :], in0=bank[bk][:, :], in1=gate_bc[:, half * 512:(half + 1) * 512], op=ALU.mult),
                  [BK[bk], "gate_bc"], ["optmp"])
                G(lambda e: e.tensor_tensor(out=x_sb[:, i, half * 512:(half + 1) * 512], in0=x_sb[:, i, half * 512:(half + 1) * 512], in1=optmp[:, :], op=ALU.add),
                  ["optmp", "x%d" % i], ["x%d" % i])

        def o_to_T(o_bf, okey, oT, oTkey, M=128):
            def f(e):
                for k in range(4):
                    ins = e.transpose(out=trb[1][:, k * 128:k * 128 + M], in_=o_bf[:, k * 128:(k + 1) * 128], identity=identb[0:M, 0:M])
                return ins
            PE(f, [okey, "identb"], [TRB[1]])
            A(lambda e: e.activation(out=oT[:, :, 0:M], in_=trb[1][:, :].rearrange("p (k t) -> p k t", t=128)[:, :, 0:M], func=AF.Copy), [TRB[1]], [oTkey])

        optmp = T(st, "optmp", [128, 512])
        oTt = T(st, "oTt", [128, 4, 128], BF16)
        oTs = T(st, "oTs", [128, 12, NO], BF16)

        def outproj_sample(l, gstack):
            gate_o = T(gstack, "gate_o", [NO, D])
            S.dma("sp", gate_o[:], gate_d.ap(), reads=["gate_d"], writes=["gate_o"], slot="gate_o")
            for br in range(3):
                load_wout(l, br * 512)
                for half in range(2):
                    def f(e):
                        for k in range(4):
                            ins = e.matmul(bank[5 + half][0:NO, :], lhsT=oTs[:, br * 4 + k, :], rhs=wout[:, k, half * 512:(half + 1) * 512], start=(k == 0), stop=(k == 3))
                        return ins
                    PE(f, ["oTs", "wout"], [BK[5 + half]])
                    V(lambda e: e.tensor_tensor(out=optmp[0:NO, :], in0=bank[5 + half][0:NO, :], in1=gate_o[:, half * 512:(half + 1) * 512], op=ALU.mult),
                      [BK[5 + half], "gate_o"], ["optmp"])
                    V(lambda e: e.tensor_tensor(out=xo[:, half * 512:(half + 1) * 512], in0=xo[:, half * 512:(half + 1) * 512], in1=optmp[0:NO, :], op=ALU.add),
                      ["optmp", "xo"], ["xo"])

        for l in range(depth):
            lam_init = 0.8 - 0.6 * math.exp(-0.3 * l)
            with ExitStack() as ps:
                wa = T(ps, "wa", [128, 8, 512], BF16)
                brow = T(ps, "brow", [NS, 512])
                mp = T(ps, "mp", [1, 512])
                mo_ss = T(ps, "mo_ss", [NO, 2 * D])
                gst = T(ps, "gst", [NO, 512])
                moda = T(ps, "moda", [NS, 2 * D])
                lv = T(ps, "lv", [128, 256])
                lvp = T(ps, "lvp", [128, 128])
                l2 = T(ps, "l2", [128, 2])
                sg = T(ps, "sg", [128, 128])
                ssq = T(ps, "ssq", [128, 2])
                bcast_row(lv[:], attn_lambda[l:l + 1, :], 256, "lv")
                V(lambda e: e.tensor_tensor(out=lvp[:].rearrange("p (a d) -> p a d", a=2), in0=lv[:].rearrange("p (a b d) -> p a b d", a=2, b=2)[:, :, 0, :],
                                            in1=lv[:].rearrange("p (a b d) -> p a b d", a=2, b=2)[:, :, 1, :], op=ALU.mult), ["lv"], ["lvp"])
                V(lambda e: e.tensor_reduce(out=l2[:], in_=lvp[:].rearrange("p (a d) -> p a d", a=2), axis=AX.X, op=ALU.add), ["lvp"], ["l2"])
                A(lambda e: e.activation(out=l2[:], in_=l2[:], func=AF.Exp), ["l2"], ["l2"])
                V(lambda e: e.tensor_tensor(out=lamv[:, 0:1], in0=l2[:, 0:1], in1=l2[:, 1:2], op=ALU.subtract), ["l2"], ["lamv"])
                V(lambda e: e.tensor_scalar(out=lamv[:, 0:1], in0=lamv[:, 0:1], scalar1=float(lam_init), scalar2=None, op0=ALU.add), ["lamv"], ["lamv"])
                V(lambda e: e.tensor_scalar(out=lamv[:, 1:2], in0=lamv[:, 0:1], scalar1=-1.0, scalar2=None, op0=ALU.mult), ["lamv"], ["lamv"])
                bcast_row(sg[:], subln[l:l + 1, :], 128, "sg")
                V(lambda e: e.tensor_scalar(out=sgl[:], in0=sg[:], scalar1=float(1.0 - lam_init), scalar2=None, op0=ALU.mult), ["sg"], ["sgl"])
                for cch in range(6):
                    S.dma_group("pool", [(wa[:, k, :], w_ada[l, k * 128:(k + 1) * 128, cch * 512:(cch + 1) * 512]) for k in range(8)], writes=["wa"], slot="wa")
                    bcast_row(brow[:], b_ada[l:l + 1, cch * 512:(cch + 1) * 512], 512, "brow")
                    def f(e):
                        for k in range(8):
                            ins = e.matmul(bank[0][0:1, :], lhsT=cT_p[:, k, :], rhs=wa[:, k, :], start=(k == 0), stop=(k == 7))
                        return ins
                    PE(f, ["wa", "cT_p"], [BK[0]])
                    V(lambda e: e.tensor_tensor(out=mp[:], in0=bank[0][0:1, :], in1=brow[0:1, :], op=ALU.add), [BK[0], "brow"], ["mp"])
                    if cch < 4:
                        def f(e):
                            for j in range(4):
                                ins = e.transpose(out=bank[3][:, cch * 4 + j:cch * 4 + j + 1], in_=mp[0:1, j * 128:(j + 1) * 128], identity=cm[0:1, 0, 0:1])
                            return ins
                        PE(f, ["mp", "cm"], [BK[3]])
                    else:
                        PE(lambda e: e.matmul(bank[4][:, :], lhsT=onesf[0:1, :], rhs=mp[0:1, :], start=True, stop=True), ["onesf", "mp"], [BK[4]])
                        V(lambda e: e.tensor_copy(out=gate_bc[:, (cch - 4) * 512:(cch - 3) * 512], in_=bank[4][:, :]), [BK[4]], ["gate_bc"])
                    def f(e):
                        for k in range(8):
                            ins = e.matmul(bank[1][0:NO, :], lhsT=cT_o[:, k, :], rhs=wa[:, k, :], start=(k == 0), stop=(k == 7))
                        return ins
                    PE(f, ["wa", "cT_o"], [BK[1]])
                    if cch < 4:
                        V(lambda e: e.tensor_tensor(out=mo_ss[:, cch * 512:(cch + 1) * 512], in0=bank[1][0:NO, :], in1=brow[0:NO, :], op=ALU.add), [BK[1], "brow"], ["mo_ss"])
                    else:
                        V(lambda e: e.tensor_tensor(out=gst[:], in0=bank[1][0:NO, :], in1=brow[0:NO, :], op=ALU.add), [BK[1], "brow"], ["gst"])
                        S.dma("sp", gate_d[:, (cch - 4) * 512:(cch - 3) * 512], gst[:], reads=["gst"], writes=["gate_d"], slot="gate_d")
                V(lambda e: e.tensor_copy(out=scT[:], in_=bank[3][:, 0:16]), [BK[3]], ["scT"])
                V(lambda e: e.tensor_scalar(out=scT[:, 8:16], in0=scT[:, 8:16], scalar1=1.0, scalar2=None, op0=ALU.add), ["scT"], ["scT"])
                V(lambda e: e.tensor_scalar(out=mo_ss[:, D:2 * D], in0=mo_ss[:, D:2 * D], scalar1=1.0, scalar2=None, op0=ALU.add), ["mo_ss"], ["mo_ss"])
                S.barrier()
                with ExitStack() as ps2:
                    xall = T(ps2, "xall", [NS, D])
                    hs = T(ps2, "hs_s", [NS, D])
                    hsb = T(ps2, "hsb", [NS, D], BF16)
                    if do_sample:
                        for (xs, xk, n, md, mk, dst, dk) in ((xo, "xo", NO, mo_ss, "mo_ss", hTo, "hTo"),):
                            A(lambda e: e.activation(out=hs[0:n, :], in_=xs[:], func=AF.Square, accum_out=ssq[0:n, 0:1]), [xk], ["hs", "ssq"])
                            rstd_from_ss(ssq[0:n, 1:2], ssq[0:n, 0:1], D, ["ssq"], "ssq")
                            V(lambda e: e.tensor_scalar(out=hs[0:n, :], in0=xs[:], scalar1=ssq[0:n, 1:2], scalar2=None, op0=ALU.mult), [xk, "ssq"], ["hs"])
                            V(lambda e: e.tensor_tensor(out=hs[0:n, :], in0=hs[0:n, :], in1=md[:, D:2 * D], op=ALU.mult), ["hs", mk], ["hs"])
                            V(lambda e: e.tensor_tensor(out=hsb[0:n, :], in0=hs[0:n, :], in1=md[:, 0:D], op=ALU.add), ["hs", mk], ["hsb"])
                            def f(e):
                                for k in range(8):
                                    ins = e.transpose(out=trb[0][:, k * NS:k * NS + n], in_=hsb[0:n, k * 128:(k + 1) * 128], identity=identb[0:n, 0:n])
                                return ins
                            PE(f, ["hsb", "identb"], [TRB[0]])
                            V(lambda e: e.tensor_copy(out=dst[:], in_=trb[0][:, 0:8 * NS].rearrange("p (k s) -> p k s", s=NS)[:, :, 0:n]), [TRB[0]], [dk])
                    S.barrier()
            with ExitStack() as ps:
                hn = T(ps, "hn", [128, D], BF16)
                hs = T(ps, "hs", [128, D])
                ssq = T(ps, "ssq2", [128, 2])
                for i in range(NT):
                    A(lambda e: e.activation(out=hs[:], in_=x_sb[:, i, :], func=AF.Square, accum_out=ssq[:, 0:1]), ["x%d" % i], ["hs", "ssq"])
                    rstd_from_ss(ssq[:, 1:2], ssq[:, 0:1], D, ["ssq"], "ssq")
                    V(lambda e: e.tensor_scalar(out=hn[:], in0=x_sb[:, i, :], scalar1=ssq[:, 1:2], scalar2=None, op0=ALU.mult), ["x%d" % i, "ssq"], ["hn"])
                    for half in range(2):
                        def f(e):
                            for k in range(4):
                                kk = half * 4 + k
                                ins = e.transpose(out=trb[half][:, k * 128:(k + 1) * 128], in_=hn[:, kk * 128:(kk + 1) * 128], identity=identb[:])
                            return ins
                        PE(f, ["hn", "identb"], [TRB[half]])
                        for k in range(4):
                            kk = half * 4 + k
                            A(lambda e: e.activation(out=hT[:, kk, i * 128:(i + 1) * 128], in_=trb[half][:, k * 128:(k + 1) * 128], func=AF.Identity,
                                                     scale=scT[:, 8 + kk:9 + kk], bias=scT[:, kk:kk + 1]), [TRB[half], "scT"], ["hT%d" % i])
                S.barrier()

            def proj_rows(dst, dkey, segs):
                for (w0, n, d0) in segs:
                    proj_tok(bank[0][0:NO, 0:n], BK[0], lambda k: hTo[:, k, :], ["hTo"], w0, n)
                    V(lambda e: e.tensor_copy(out=dst[:, d0:d0 + n], in_=bank[0][0:NO, 0:n]), [BK[0]], [dkey])

            def gated_norm_rows(ss, o3, okey, ng, width, gvec, gkey, zs_ap, zkey, br):
                tq = T(ss, "gn_tq%d" % br, [NO, 512])
                t4 = T(ss, "gn_t4%d" % br, [NO, 8])
                ob4 = T(ss, "gn_ob%d" % br, [NO, 512], BF16)
                tq3 = tq[:, :].rearrange("s (g d) -> s g d", d=width)
                V(lambda e: e.tensor_tensor(out=tq3, in0=o3, in1=o3, op=ALU.mult), [okey], ["gn_tq"])
                V(lambda e: e.tensor_reduce(out=t4[:, 0:ng], in_=tq3, axis=AX.X, op=ALU.add), ["gn_tq"], ["gn_t4"])
                rstd_from_ss(t4[:, 0:ng], t4[:, 0:ng], width, ["gn_t4"], "gn_t4")
                V(lambda e: e.tensor_tensor(out=tq3, in0=o3, in1=t4[:, 0:ng].unsqueeze(2).to_broadcast([NO, ng, width]), op=ALU.mult), [okey, "gn_t4"], ["gn_tq"])
                V(lambda e: e.tensor_tensor(out=tq3, in0=tq3, in1=gvec, op=ALU.mult), ["gn_tq", gkey], ["gn_tq"])
                if zs_ap is not None:
                    V(lambda e: e.tensor_tensor(out=ob4[:], in0=tq[:], in1=zs_ap, op=ALU.mult), ["gn_tq", zkey], ["gn_ob"])
                else:
                    V(lambda e: e.tensor_copy(out=ob4[:], in_=tq[:]), ["gn_tq"], ["gn_ob"])
                def f(e):
                    for k in range(4):
                        ins = e.transpose(out=trb[0][:, k * NO:(k + 1) * NO], in_=ob4[:, k * 128:(k + 1) * 128], identity=identb[0:NO, 0:NO])
                    return ins
                PE(f, ["gn_ob", "identb"], [TRB[0]])
                V(lambda e: e.tensor_copy(out=oTs[:, br * 4:(br + 1) * 4, :].rearrange("p k s -> p (k s)"), in_=trb[0][:, 0:4 * NO]), [TRB[0]], ["oTs"])

            def sample_attn():
              with ExitStack() as stack2:
                pj = T(stack2, "pjA", [NO, 2048])
                with ExitStack() as ss:
                    qall = T(ss, "qall", [NO, 512])
                    proj_rows(pj, "pjA", [(0, 512, 0), (512, 512, 512), (1024, 512, 1024), (1536, 512, 1536)])
                    A(lambda e: e.activation(out=qall[:], in_=pj[:, 0:512], func=AF.Copy, scale=0.125), ["pjA"], ["qall"])
                    S.dma("sp", q_d[l].ap(), qall[:], reads=["qall"], writes=["q_d"], slot="q_d")
                    S.dma("sp", k_s[l], pj[:, 512:1024], reads=["pjA"], slot="ks")
                    S.dma("sp", v_s[l], pj[:, 1024:1536], reads=["pjA"], slot="vs")
                    NSL = 4
                    kbuf = [T(ss, "kbuf%d" % i, [128, 512], BF16) for i in range(NSL)]
                    vbuf = [T(ss, "vbuf%d" % i, [128, 512], BF16) for i in range(2 * NSL)]
                    qbc = [T(ss, "qbc%d" % i, [128, 512]) for i in range(2)]
                    prod = [T(ss, "prod%d" % i, [128, 512]) for i in range(2)]
                    lg = [T(ss, "lg%d" % i, [128, 8, 8]) for i in range(2)]
                    pexp = [T(ss, "pexp%d" % i, [128, 8, 8], BF16) for i in range(2)]
                    osb = [T(ss, "osb%d" % i, [8, 512]) for i in range(2)]
                    ssb = [T(ss, "ssb%d" % i, [1, 8]) for i in range(2)]
                    gi = 0
                    for sv in range(NO * 8):
                        s, g8 = sv // 8, sv % 8
                        b2 = s % 2
                        if g8 == 0:
                            S.dma("sp", qbc[b2][:], q_d[l][s:s + 1, :].to_broadcast([128, 512]), reads=["q_d"], writes=["qbc%d" % b2], slot="qbc%d" % b2)
                        vslots = []
                        for t in range(8):
                            sl = gi % NSL
                            vsl = gi % (2 * NSL)
                            gi += 1
                            vslots.append(vsl)
                            col = sv * 8 + t
                            S.dma("pool", None, None, writes=["kbuf%d" % sl], slot="kbuf%d" % sl, reads=["idx"],
                                  fn=lambda e: e.indirect_dma_start(out=kbuf[sl][:], out_offset=None, in_=poolk[:, :],
                                                                    in_offset=bass.IndirectOffsetOnAxis(ap=idx[:, l, col:col + 1], axis=0)))
                            S.dma("pool", None, None, writes=["vbuf%d" % vsl], slot="vbuf%d" % vsl, reads=["idx"],
                                  fn=lambda e: e.indirect_dma_start(out=vbuf[vsl][:], out_offset=None, in_=poolv[:, :],
                                                                    in_offset=bass.IndirectOffsetOnAxis(ap=idx[:, l, col:col + 1], axis=0)))
                            pb = t % 2
                            V(lambda e: e.tensor_tensor(out=prod[pb][:], in0=kbuf[sl][:], in1=qbc[b2][:], op=ALU.mult), ["kbuf%d" % sl, "qbc%d" % b2], ["prod%d" % pb])
                            V(lambda e: e.tensor_reduce(out=lg[b2][:, t, :], in_=prod[pb][:].rearrange("p (a d) -> p a d", d=64), axis=AX.X, op=ALU.add),
                              ["prod%d" % pb], ["lg%d" % b2])
                        V(lambda e: e.tensor_tensor(out=lg[b2][:], in0=lg[b2][:], in1=bias_sm[:, g8 * 8:(g8 + 1) * 8, :], op=ALU.add), ["lg%d" % b2, "bias_sm"], ["lg%d" % b2])
                        A(lambda e: e.activation(out=pexp[b2][:], in_=lg[b2][:], func=AF.Exp), ["lg%d" % b2], ["pexp%d" % b2])
                        def f(e):
                            for t in range(8):
                                ins = e.matmul(bank[4 + b2][0:8, :], lhsT=pexp[b2][:, t, :], rhs=vbuf[vslots[t]][:], start=(t == 0 and g8 == 0), stop=(t == 7 and g8 == 7))
                            return ins
                        PE(f, ["pexp%d" % b2] + ["vbuf%d" % v_ for v_ in vslots], [BK[4 + b2]])
                        PE(lambda e: e.matmul(bank[6][0:1, g8 * 64:(g8 + 1) * 64], lhsT=onesb[:, 0:1], rhs=pexp[b2][:].rearrange("p t a -> p (t a)"), start=True, stop=True),
                           ["onesb", "pexp%d" % b2], [BK[6]])
                        if g8 == 7:
                            A(lambda e: e.activation(out=osb[b2][:], in_=bank[4 + b2][0:8, :], func=AF.Copy), [BK[4 + b2]], ["osb%d" % b2])
                            V(lambda e: e.tensor_reduce(out=ssb[b2][:], in_=bank[6][0:1, :].rearrange("p (t a) -> p a t", a=8), axis=AX.X, op=ALU.add),
                              [BK[6]], ["ssb%d" % b2])
                            S.dma("sp", rs_out[l][s:s + 1, 0:4096].rearrange("o (a c) -> (o a) c", a=8), osb[b2][:], reads=["osb%d" % b2], writes=["rs_out"], slot="rsin_o%d" % b2)
                            S.dma("sp", rs_out[l][s:s + 1, 4096:4104], ssb[b2][:], reads=["ssb%d" % b2], writes=["rs_out"], slot="rsin_s%d" % b2)
                    S.barrier()
                    ss.close()
                    ss = stack2
                    Op = T(ss, "Op", [NO, 8, 128])
                    sm = T(ss, "sm", [NO, 8])
                    pn = T(ss, "pn", [NO, 8])
                    tq = T(ss, "tqA", [NO, 512])
                    oa = T(ss, "oa", [NO, 4, 128])
                    zs = T(ss, "zsA", [NO, 512])
                    Op4 = Op[:, :, :].rearrange("s (h m) d -> s h m d", m=2)
                    S.dma_group("sp", [(Op4[:, :, m, :], bass.AP(tensor=rs_out[l].ap().tensor, offset=m * 512, ap=[[RSW, NO], [1152, 4], [1, 128]])) for m in range(2)],
                                reads=["rs_out"], writes=["Op"], slot="Op")
                    S.dma("sp", sm[:], rs_out[l][:, 4096:4104], reads=["rs_out"], writes=["sm"], slot="sm")
                    V(lambda e: e.tensor_tensor(out=tq[:], in0=pj[:, 0:512], in1=pj[:, 512:1024], op=ALU.mult), ["pjA"], ["tqA"])
                    V(lambda e: e.tensor_reduce(out=pn[:], in_=tq[:].rearrange("s (a d) -> s a d", d=64), axis=AX.X, op=ALU.add), ["tqA"], ["pn"])
                    pn3 = pn[:, :].rearrange("s (h m) -> s h m", m=2)
                    sm3 = sm[:, :].rearrange("s (h m) -> s h m", m=2)
                    V(lambda e: e.scalar_tensor_tensor(out=pn3, in0=pn3, scalar=0.125, in1=b0row[:].unsqueeze(2).to_broadcast([NO, 4, 2]), op0=ALU.mult, op1=ALU.add), ["pn", "b0row"], ["pn"])
                    A(lambda e: e.activation(out=pn[:], in_=pn[:], func=AF.Exp), ["pn"], ["pn"])
                    V(lambda e: e.tensor_tensor(out=sm[:], in0=sm[:], in1=pn[:], op=ALU.add), ["sm", "pn"], ["sm"])
                    V(lambda e: e.reciprocal(out=sm[:], in_=sm[:]), ["sm"], ["sm"])
                    V(lambda e: e.tensor_scalar(out=sm3[:, :, 1], in0=sm3[:, :, 1], scalar1=lamv[0:NO, 1:2], scalar2=None, op0=ALU.mult), ["sm", "lamv"], ["sm"])
                    tq3 = tq[:, :].rearrange("s (h d) -> s h d", d=128)
                    for m in range(2):
                        V(lambda e: e.tensor_tensor(out=tq3, in0=pj[:, 1024:1536].rearrange("s (h d) -> s h d", d=128),
                                                    in1=pn3[:, :, m].unsqueeze(2).to_broadcast([NO, 4, 128]), op=ALU.mult), ["pjA", "pn"], ["tqA"])
                        V(lambda e: e.tensor_tensor(out=Op4[:, :, m, :], in0=Op4[:, :, m, :], in1=tq3, op=ALU.add), ["Op", "tqA"], ["Op"])
                    V(lambda e: e.tensor_tensor(out=Op[:], in0=Op[:], in1=sm[:].unsqueeze(2).to_broadcast([NO, 8, 128]), op=ALU.mult), ["Op", "sm"], ["Op"])
                    V(lambda e: e.tensor_tensor(out=oa[:], in0=Op4[:, :, 0, :], in1=Op4[:, :, 1, :], op=ALU.add), ["Op"], ["oa"])
                    A(lambda e: e.activation(out=zs[:], in_=pj[:, 1536:2048], func=AF.Silu), ["pjA"], ["zsA"])
                    gated_norm_rows(ss, oa[:], "oa", 4, 128, sgl[0:NO, :].unsqueeze(1).to_broadcast([NO, 4, 128]), "sgl", zs[:], "zsA", 0)
                S.barrier()

            def sample_ssd():
                with ExitStack() as ss:
                    pj = T(ss, "pjB", [NO, 1544])
                    cw = T(ss, "cwB", [NO, 1024])
                    cs_ = T(ss, "csB", [NO, 1024])
                    xa = T(ss, "xaB", [NO, 1024])
                    big = T(ss, "bigB", [NO, 1024])
                    sv = T(ss, "svB", [NO, 32])
                    rowp = T(ss, "rowpB", [NO, 24])
                    yb = T(ss, "ybB", [NO, 512])
                    zs = T(ss, "zsB", [NO, 512])
                    gB = T(ss, "gB", [NO, 512])
                    colsT = T(ss, "colsT", [128, 8, NO])
                    bcb = T(ss, "bcb", [128, 512])
                    hst = T(ss, "hst", [128, 4, 128])
                    ycol = T(ss, "ycol", [128, 4, NO])
                    hj = T(ss, "hj", [128, 128])
                    proj_rows(pj, "pjB", [(0, 512, 0), (512, 512, 512), (1024, 8, 1024), (1032, 512, 1032)])
                    bcast_row(cw[:], ssm_conv_w[l:l + 1, 3 * 1024:4 * 1024], 1024, "cwB")
                    V(lambda e: e.tensor_tensor(out=xa[:], in0=pj[:, 0:1024], in1=cw[:], op=ALU.mult), ["pjB", "cwB"], ["xaB"])
                    for i in range(3):
                        bcast_row(cw[:], ssm_conv_w[l:l + 1, i * 1024:(i + 1) * 1024], 1024, "cwB")
                        S.dma("sp", cs_[:], st_sconv[l, :, i, :], writes=["csB"], slot="csB")
                        V(lambda e: e.tensor_tensor(out=big[:], in0=cs_[:], in1=cw[:], op=ALU.mult), ["csB", "cwB"], ["bigB"])
                        V(lambda e: e.tensor_tensor(out=xa[:], in0=xa[:], in1=big[:], op=ALU.add), ["xaB", "bigB"], ["xaB"])
                        if i >= 1:
                            S.dma("sp", sconv_s[l][:, i - 1, :], cs_[:], reads=["csB"], slot="scs0")
                    S.dma("sp", sconv_s[l][:, 2, :], pj[:, 0:1024], reads=["pjB"], slot="scs1")
                    bcast_row(cw[:], ssm_conv_b[l:l + 1, :], 1024, "cwB")
                    V(lambda e: e.tensor_tensor(out=xa[:], in0=xa[:], in1=cw[:], op=ALU.add), ["xaB", "cwB"], ["xaB"])
                    A(lambda e: e.activation(out=xa[:], in_=xa[:], func=AF.Silu), ["xaB"], ["xaB"])
                    bcast_row(rowp[:, 0:8], ssm_dt_bias[l:l + 1, :], 8, "rowpB_a")
                    bcast_row(rowp[:, 8:16], ssm_A_log[l:l + 1, :], 8, "rowpB_b")
                    bcast_row(rowp[:, 16:24], ssm_D[l:l + 1, :], 8, "rowpB_c")
                    RP = ["rowpB_a", "rowpB_b", "rowpB_c"]
                    V(lambda e: e.tensor_tensor(out=sv[:, 0:8], in0=pj[:, 1024:1032], in1=rowp[:, 0:8], op=ALU.add), ["pjB"] + RP, ["svB"])
                    softplus(sv[:, 0:8], sv[:, 0:8], ["svB"], "svB")
                    A(lambda e: e.activation(out=sv[:, 8:16], in_=rowp[:, 8:16], func=AF.Exp), RP + ["svB"], ["svB"])
                    V(lambda e: e.scalar_tensor_tensor(out=sv[:, 16:24], in0=sv[:, 8:16], scalar=-1.0, in1=sv[:, 0:8], op0=ALU.mult, op1=ALU.mult), ["svB"], ["svB"])
                    A(lambda e: e.activation(out=sv[:, 16:24], in_=sv[:, 16:24], func=AF.Exp), ["svB"], ["svB"])
                    V(lambda e: e.tensor_tensor(out=big[:, 0:512].rearrange("s (h p) -> s h p", p=64), in0=xa[:, 0:512].rearrange("s (h p) -> s h p", p=64),
                                                in1=sv[:, 0:8].unsqueeze(2).to_broadcast([NO, 8, 64]), op=ALU.mult), ["xaB", "svB"], ["bigB"])
                    V(lambda e: e.tensor_copy(out=big[:, 512:1024].rearrange("s (h p) -> s h p", p=64), in_=sv[:, 16:24].unsqueeze(2).to_broadcast([NO, 8, 64])), ["svB", "bigB"], ["bigB"])
                    def f(e):
                        for j in range(8):
                            ins = e.transpose(out=bank[0][:, j * NO:(j + 1) * NO], in_=big[:, j * 128:(j + 1) * 128], identity=cm[0:NO, 0, 0:NO])
                        return ins
                    PE(f, ["bigB", "cm"], [BK[0]])
                    V(lambda e: e.tensor_copy(out=colsT[:].rearrange("p j s -> p (j s)"), in_=bank[0][:, 0:8 * NO]), [BK[0]], ["colsT"])
                    for s in range(NO):
                        S.dma("sp", hst[:], st_ssm[l, s].rearrange("h p n -> (h p) n").rearrange("(j q) n -> q j n", q=128), writes=["hst"], slot="hst")
                        PE(lambda e: e.matmul(bank[1][:, :], lhsT=sel4[:, s, :], rhs=xa[:, 512:1024], start=True, stop=True), ["sel4", "xaB"], [BK[1]])
                        A(lambda e: e.activation(out=bcb[:], in_=bank[1][:, :], func=AF.Copy), [BK[1]], ["bcb"])
                        for j in range(4):
                            g = j // 2
                            V(lambda e: e.tensor_scalar(out=hst[:, j, :], in0=hst[:, j, :], scalar1=colsT[:, 4 + j, s:s + 1], scalar2=None, op0=ALU.mult), ["hst", "colsT"], ["hst"])
                            V(lambda e: e.scalar_tensor_tensor(out=hst[:, j, :], in0=bcb[:, g * 128:(g + 1) * 128], scalar=colsT[:, j, s:s + 1], in1=hst[:, j, :],
                                                               op0=ALU.mult, op1=ALU.add), ["hst", "bcb", "colsT"], ["hst"])
                            V(lambda e: e.tensor_tensor(out=hj[:], in0=hst[:, j, :], in1=bcb[:, 256 + g * 128:256 + (g + 1) * 128], op=ALU.mult), ["hst", "bcb"], ["hj"])
                            V(lambda e: e.tensor_reduce(out=ycol[:, j, s:s + 1], in_=hj[:], axis=AX.X, op=ALU.add), ["hj"], ["ycol"])
                        S.dma("sp", ssm_s[l, s].rearrange("h p n -> (h p) n").rearrange("(j q) n -> q j n", q=128), hst[:], reads=["hst"], slot="ssms")
                    def f(e):
                        for j in range(4):
                            ins = e.transpose(out=bank[2][0:NO, j * 128:(j + 1) * 128], in_=ycol[:, j, :], identity=ident)
                        return ins
                    PE(f, ["ycol", "cm"], [BK[2]])
                    V(lambda e: e.tensor_tensor(out=big[:, 0:512].rearrange("s (h p) -> s h p", p=64), in0=xa[:, 0:512].rearrange("s (h p) -> s h p", p=64),
                                                in1=rowp[:, 16:24].unsqueeze(2).to_broadcast([NO, 8, 64]), op=ALU.mult), ["xaB"] + RP, ["bigB"])
                    V(lambda e: e.tensor_tensor(out=yb[:], in0=bank[2][0:NO, :], in1=big[:, 0:512], op=ALU.add), [BK[2], "bigB"], ["ybB"])
                    A(lambda e: e.activation(out=zs[:], in_=pj[:, 1032:1544], func=AF.Silu), ["pjB"], ["zsB"])
                    V(lambda e: e.tensor_tensor(out=yb[:], in0=yb[:], in1=zs[:], op=ALU.mult), ["ybB", "zsB"], ["ybB"])
                    bcast_row(gB[:], ssm_norm_g[l:l + 1, :], 512, "gB")
                    gated_norm_rows(ss, yb[:, :].rearrange("s (g d) -> s g d", d=256), "ybB", 2, 256, gB[:, :].rearrange("s (g d) -> s g d", d=256), "gB", None, None, 1)
                S.barrier()

            def sample_gdn():
                with ExitStack() as ss:
                    pj = T(ss, "pjC", [NO, 2056])
                    cw = T(ss, "cwC", [NO, 1536])
                    cs_ = T(ss, "csC", [NO, 1536])
                    sv = T(ss, "svC", [NO, 32])
                    rowp = T(ss, "rowpC", [NO, 8])
                    zs = T(ss, "zsC", [NO, 512])
                    gC = T(ss, "gC", [NO, 128])
                    qkT = T(ss, "qkT", [128, 8, NO])
                    qkTm = T(ss, "qkTm", [128, 8, NO, NO])
                    kmr = T(ss, "kmr", [NO, 512])
                    egb = T(ss, "egb", [128, NO, 4])
                    vn = T(ss, "vn", [NO, 512])
                    Sall = T(ss, "Sall", [128, NO, 4, 128])
                    proj_rows(pj, "pjC", [(0, 512, 0), (512, 512, 512), (1024, 512, 1024), (1536, 8, 1536), (1544, 512, 1544)])
                    xa = pj[:, 0:1536]
                    S.dma("sp", gconv_s[l][:, 2, :], pj[:, 0:1536], reads=["pjC"], slot="gcs1")
                    for s in range(NO):
                        S.dma("sp", Sall[:, s, :, :], st_gdn[l, s].rearrange("h k v -> k h v"), writes=["Sall%d" % s], slot="Sall%d" % s)
                    bcast_row(cw[:], gdn_conv_w[l:l + 1, 3 * 1536:4 * 1536], 1536, "cwC")
                    V(lambda e: e.tensor_tensor(out=xa, in0=xa, in1=cw[:], op=ALU.mult), ["pjC", "cwC"], ["pjC"])
                    for i in range(3):
                        bcast_row(cw[:], gdn_conv_w[l:l + 1, i * 1536:(i + 1) * 1536], 1536, "cwC")
                        S.dma("sp", cs_[:], st_gconv[l, :, i, :], writes=["csC"], slot="csC")
                        if i >= 1:
                            S.dma("sp", gconv_s[l][:, i - 1, :], cs_[:], reads=["csC"], slot="gcs0")
                        V(lambda e: e.tensor_tensor(out=cs_[:], in0=cs_[:], in1=cw[:], op=ALU.mult), ["csC", "cwC"], ["csC"])
                        V(lambda e: e.tensor_tensor(out=xa, in0=xa, in1=cs_[:], op=ALU.add), ["pjC", "csC"], ["pjC"])
                    A(lambda e: e.activation(out=xa, in_=xa, func=AF.Silu), ["pjC"], ["pjC"])
                    V(lambda e: e.tensor_tensor(out=cs_[:, 0:1024], in0=xa[:, 0:1024], in1=xa[:, 0:1024], op=ALU.mult), ["pjC", "csC"], ["csC"])
                    V(lambda e: e.tensor_reduce(out=sv[:, 0:8], in_=cs_[:, 0:1024].rearrange("s (g d) -> s g d", d=128), axis=AX.X, op=ALU.add), ["csC"], ["svC"])
                    A(lambda e: e.activation(out=sv[:, 0:8], in_=sv[:, 0:8], func=AF.Sqrt, bias=epsc[0:NO, 0:1]), ["svC", "epsc"], ["svC"])
                    V(lambda e: e.reciprocal(out=sv[:, 0:8], in_=sv[:, 0:8]), ["svC"], ["svC"])
                    V(lambda e: e.tensor_scalar(out=sv[:, 0:4], in0=sv[:, 0:4], scalar1=float(128 ** -0.5), scalar2=None, op0=ALU.mult), ["svC"], ["svC"])
                    V(lambda e: e.tensor_tensor(out=xa[:, 0:1024].rearrange("s (g d) -> s g d", d=128), in0=xa[:, 0:1024].rearrange("s (g d) -> s g d", d=128),
                                                in1=sv[:, 0:8].unsqueeze(2).to_broadcast([NO, 8, 128]), op=ALU.mult), ["pjC", "svC"], ["pjC"])
                    A(lambda e: e.activation(out=sv[:, 8:12], in_=pj[:, 1536:1540], func=AF.Sigmoid), ["pjC", "svC"], ["svC"])
                    bcast_row(rowp[:, 0:4], gdn_dt_bias[l:l + 1, :], 4, "rowpC_a")
                    bcast_row(rowp[:, 4:8], gdn_A_log[l:l + 1, :], 4, "rowpC_b")
                    RQ = ["rowpC_a", "rowpC_b"]
                    V(lambda e: e.tensor_tensor(out=sv[:, 12:16], in0=pj[:, 1540:1544], in1=rowp[:, 0:4], op=ALU.add), ["pjC", "svC"] + RQ, ["svC"])
                    softplus(sv[:, 12:16], sv[:, 12:16], ["svC"], "svC")
                    A(lambda e: e.activation(out=sv[:, 16:20], in_=rowp[:, 4:8], func=AF.Exp), RQ + ["svC"], ["svC"])
                    V(lambda e: e.scalar_tensor_tensor(out=sv[:, 12:16], in0=sv[:, 16:20], scalar=-1.0, in1=sv[:, 12:16], op0=ALU.mult, op1=ALU.mult), ["svC"], ["svC"])
                    A(lambda e: e.activation(out=sv[:, 12:16], in_=sv[:, 12:16], func=AF.Exp), ["svC"], ["svC"])
                    def f(e):
                        for j in range(8):
                            ins = e.transpose(out=bank[0][:, j * NO:(j + 1) * NO], in_=xa[:, j * 128:(j + 1) * 128], identity=cm[0:NO, 0, 0:NO])
                        return ins
                    PE(f, ["pjC", "cm"], [BK[0]])
                    V(lambda e: e.tensor_copy(out=qkT[:].rearrange("p j s -> p (j s)"), in_=bank[0][:, 0:8 * NO]), [BK[0]], ["qkT"])
                    V(lambda e: e.tensor_tensor(out=qkTm[:], in0=qkT[:].unsqueeze(2).to_broadcast([128, 8, NO, NO]),
                                                in1=eye4[:].unsqueeze(1).to_broadcast([128, 8, NO, NO]), op=ALU.mult), ["qkT", "eye4"], ["qkTm"])
                    for s in range(NO):
                        PE(lambda e: e.matmul(bank[1][:, s * 4:(s + 1) * 4], lhsT=sel4[:, s, :], rhs=sv[:, 12:16], start=True, stop=True), ["sel4", "svC"], [BK[1]])
                    V(lambda e: e.tensor_copy(out=egb[:].rearrange("p s h -> p (s h)"), in_=bank[1][:, 0:16]), [BK[1]], ["egb"])
                    SA = ["Sall%d" % s for s in range(NO)]
                    for h in range(4):
                        def f(e):
                            for s in range(NO):
                                ins = e.matmul(bank[2][0:NO, h * 128:(h + 1) * 128], lhsT=qkTm[:, 4 + h, s, :], rhs=Sall[:, s, h, :], start=(s == 0), stop=(s == NO - 1))
                            return ins
                        PE(f, ["qkTm"] + SA, [BK[2]])
                    V(lambda e: e.tensor_tensor(out=vn[:].rearrange("s (h d) -> s h d", d=128), in0=bank[2][0:NO, :].rearrange("s (h d) -> s h d", d=128),
                                                in1=sv[:, 12:16].unsqueeze(2).to_broadcast([NO, 4, 128]), op=ALU.mult), [BK[2], "svC"], ["vn"])
                    V(lambda e: e.tensor_tensor(out=vn[:], in0=xa[:, 1024:1536], in1=vn[:], op=ALU.subtract), ["pjC", "vn"], ["vn"])
                    V(lambda e: e.tensor_tensor(out=vn[:].rearrange("s (h d) -> s h d", d=128), in0=vn[:].rearrange("s (h d) -> s h d", d=128),
                                                in1=sv[:, 8:12].unsqueeze(2).to_broadcast([NO, 4, 128]), op=ALU.mult), ["vn", "svC"], ["vn"])
                    for s in range(NO):
                        V(lambda e: e.tensor_scalar(out=kmr[:], in0=xa[:, 512:1024], scalar1=cm[0:NO, 0, s:s + 1], scalar2=None, op0=ALU.mult), ["pjC", "cm"], ["kmr"])
                        for h in range(4):
                            PE(lambda e: e.matmul(bank[3][:, h * 128:(h + 1) * 128], lhsT=kmr[:, h * 128:(h + 1) * 128], rhs=vn[:, h * 128:(h + 1) * 128], start=True, stop=True),
                               ["kmr", "vn"], [BK[3]])
                            V(lambda e: e.scalar_tensor_tensor(out=Sall[:, s, h, :], in0=Sall[:, s, h, :], scalar=egb[:, s, h:h + 1], in1=bank[3][:, h * 128:(h + 1) * 128],
                                                               op0=ALU.mult, op1=ALU.add), ["Sall%d" % s, "egb", BK[3]], ["Sall%d" % s])
                        S.dma("sp", gdn_s[l, s].rearrange("h k v -> k h v"), Sall[:, s, :, :], reads=["Sall%d" % s], slot="gdns%d" % s)
                    for h in range(4):
                        def f(e):
                            for s in range(NO):
                                ins = e.matmul(bank[4][0:NO, h * 128:(h + 1) * 128], lhsT=qkTm[:, h, s, :], rhs=Sall[:, s, h, :], start=(s == 0), stop=(s == NO - 1))
                            return ins
                        PE(f, ["qkTm"] + SA, [BK[4]])
                    og = vn
                    V(lambda e: e.tensor_copy(out=og[:], in_=bank[4][0:NO, :]), [BK[4], "vn"], ["vn"])
                    A(lambda e: e.activation(out=zs[:], in_=pj[:, 1544:2056], func=AF.Silu), ["pjC"], ["zsC"])
                    bcast_row(gC[:], gdn_norm_g[l:l + 1, :], 128, "gC")
                    gated_norm_rows(ss, og[:, :].rearrange("s (h d) -> s h d", d=128), "vn", 4, 128, gC[:, :].unsqueeze(1).to_broadcast([NO, 4, 128]), "gC", zs[:], "zsC", 2)
                S.barrier()

            def sample_finish():
                with ExitStack() as gstack:
                    outproj_sample(l, gstack)
                    S.barrier()
                S.barrier()

            if "A" in do_branches or do_sample:
                load_w(l, [(0, 1536), (OZ, 512)])
                if do_sample:
                    sample_attn()
            if "A" in do_branches:
                with ExitStack() as bs:
                    kT = T(bs, "kT", [128, 4, L], BF16)
                    vbf = T(bs, "vbf", [128, NT, 4, 130], BF16)
                    kvf = T(bs, "kvf", [128, 1, 512])
                    qT = T(bs, "qT", [128, 2, 4, 128], BF16)
                    pT = [T(bs, "pT%d" % i, [128, 4, 128], BF16) for i in range(2)]
                    sdg = T(bs, "sdg", [128, 2, 128])
                    za = T(bs, "za", [128, 512])
                    oh_ = T(bs, "oh_", [128, 512])
                    ob = T(bs, "ob", [128, 512], BF16)
                    r8 = T(bs, "r8", [128, 8])
                    s4 = T(bs, "s4", [128, 8])
                    jk = T(bs, "jk", [128, 128])
                    load_wout(l, 0)
                    V(lambda e: e.memset(vbf[:], 1.0), [], ["vbf"])
                    V(lambda e: e.memset(qT[:], 0.0), [], ["qT"])
                    for i in range(NT):
                        for h in range(4):
                            proj_feat(bank[0][:, h * 128:(h + 1) * 128], BK[0], 512 + h * 128, lambda k: hT[:, k, i * 128:(i + 1) * 128], ["hT%d" % i])
                        A(lambda e: e.activation(out=kT[:, :, i * 128:(i + 1) * 128], in_=bank[0][:, :].rearrange("p (h t) -> p h t", t=128), func=AF.Copy), [BK[0]], ["kT%d" % i])
                        for (j, c0, dst) in ((0, 512, k_p), (1, 1024, v_p)):
                            proj_tok(bank[1 + j][:, :], BK[1 + j], lambda k: hT[:, k, i * 128:(i + 1) * 128], ["hT%d" % i], c0, 512)
                            V(lambda e: e.tensor_copy(out=kvf[:, 0, :], in_=bank[1 + j][:, :]), [BK[1 + j]], ["kvf"])
                            S.dma("sp", dst[l, i * 128:(i + 1) * 128, :], kvf[:, 0, :], reads=["kvf"], slot="kvout")
                            if j == 1:
                                A(lambda e: e.activation(out=vbf[:, i, :, 0:128], in_=bank[2][:, :].rearrange("p (h d) -> p h d", d=128), func=AF.Copy), [BK[2]], ["vbf%d" % i, "vbf"])
                    for i in range((NT if DBG_A2 is None else DBG_A2) if DBG_STOP != 1 else 0):
                        for h in range(4):
                            proj_feat(bank[0][:, h * 128:(h + 1) * 128], BK[0], h * 128, lambda k: hT[:, k, i * 128:(i + 1) * 128], ["hT%d" % i])
                        A(lambda e: e.activation(out=qT[0:64, 0, :, :], in_=bank[0][0:64, :].rearrange("p (h t) -> p h t", t=128), func=AF.Copy, scale=0.125), [BK[0], "qT"], ["qT"])
                        A(lambda e: e.activation(out=qT[64:128, 1, :, :], in_=bank[0][64:128, :].rearrange("p (h t) -> p h t", t=128), func=AF.Copy, scale=0.125), [BK[0], "qT"], ["qT"])
                        proj_tok(bank[1][:, :], BK[1], lambda k: hT[:, k, i * 128:(i + 1) * 128], ["hT%d" % i], 1536, 512)
                        A(lambda e: e.activation(out=za[:], in_=bank[1][:, :], func=AF.Silu), [BK[1]], ["za"])
                        cnt = 0
                        if DBG_STOP == 2:
                            continue
                        for h in range(4):
                            for m in range(2 if DBG_STOP != 5 else 1):
                                hm = h * 2 + m
                                ob_bank = 4 + hm // 3
                                ocol = (hm % 3) * 130
                                ngroups = (i + 1 + 3) // 4
                                for g in range(ngroups):
                                    j0 = g * 4
                                    nb = min(4, i + 1 - j0)
                                    sb = 2 + (cnt % 2); pb = cnt % 2; cnt += 1
                                    def f(e):
                                        for jj in range(nb):
                                            j = j0 + jj
                                            ins = e.matmul(bank[sb][:, jj * 128:(jj + 1) * 128], lhsT=kT[:, h, j * 128:(j + 1) * 128],
                                                           rhs=qT[:, m, h, :], start=True, stop=True)
                                        return ins
                                    PE(f, ["qT"] + ["kT%d" % (j0 + jj) for jj in range(nb)], [BK[sb]])
                                    nfar = max(0, min(nb, (i - 1) - j0))
                                    if nfar > 0 and DBG_VAR != 1:
                                        A(lambda e: e.activation(out=pT[pb][:, 0:nfar, :], in_=bank[sb][:, 0:nfar * 128].rearrange("p (j t) -> p j t", t=128), func=AF.Exp,
                                                                 bias=c31[:, h:h + 1]), [BK[sb], "c31"], ["pT%d" % pb])
                                    for jj in range(nfar, nb if DBG_VAR != 2 else nfar):
                                        j = j0 + jj
                                        x_ = i - j
                                        V(lambda e: e.tensor_tensor(out=sdg[:, x_, :], in0=bank[sb][:, jj * 128:(jj + 1) * 128], in1=BT[:, h, x_ * 128:(x_ + 1) * 128], op=ALU.add),
                                          [BK[sb], "BT"], ["sdg%d" % x_])
                                        A(lambda e: e.activation(out=pT[pb][:, jj, :], in_=sdg[:, x_, :], func=AF.Exp), ["sdg%d" % x_], ["pT%d" % pb])
                                    if DBG_STOP == 4:
                                        continue
                                    def f2(e):
                                        for jj in range(nb):
                                            j = j0 + jj
                                            ins = e.matmul(bank[ob_bank][:, ocol:ocol + 129], lhsT=pT[pb][:, jj, :], rhs=vbf[:, j, h, 0:129], start=(j == 0), stop=(j == i))
                                        return ins
                                    PE(f2, ["pT%d" % pb] + ["vbf%d" % (j0 + jj) for jj in range(nb)], [BK[ob_bank]])
                        if DBG_STOP in (3, 4, 5):
                            continue
                        for hm in range(8):
                            ob_bank = 4 + hm // 3; ocol = (hm % 3) * 130
                            V(lambda e: e.reciprocal(out=r8[:, hm:hm + 1], in_=bank[ob_bank][:, ocol + 128:ocol + 129]), [BK[ob_bank]], ["r8_%d" % hm])
                            if hm % 2 == 1:
                                V(lambda e: e.tensor_scalar(out=r8[:, hm:hm + 1], in0=r8[:, hm:hm + 1], scalar1=lamv[:, 1:2], scalar2=None, op0=ALU.mult), ["r8_%d" % hm, "lamv"], ["r8_%d" % hm])
                        for h in range(4):
                            b0 = 4 + (2 * h) // 3; c0 = ((2 * h) % 3) * 130
                            b1 = 4 + (2 * h + 1) // 3; c1 = ((2 * h + 1) % 3) * 130
                            V(lambda e: e.tensor_scalar(out=oh_[:, h * 128:(h + 1) * 128], in0=bank[b0][:, c0:c0 + 128], scalar1=r8[:, 2 * h:2 * h + 1], scalar2=None, op0=ALU.mult),
                              [BK[b0], "r8_%d" % (2 * h)], ["oh_%d" % h])
                            V(lambda e: e.scalar_tensor_tensor(out=oh_[:, h * 128:(h + 1) * 128], in0=bank[b1][:, c1:c1 + 128], scalar=r8[:, 2 * h + 1:2 * h + 2],
                                                               in1=oh_[:, h * 128:(h + 1) * 128], op0=ALU.mult, op1=ALU.add), [BK[b1], "r8_%d" % (2 * h + 1), "oh_%d" % h], ["oh_%d" % h])
                            A(lambda e: e.activation(out=jk[:], in_=oh_[:, h * 128:(h + 1) * 128], func=AF.Square, accum_out=s4[:, h:h + 1]), ["oh_%d" % h], ["jk", "s4_%d" % h])
                        rstd_from_ss(s4[:, 4:8], s4[:, 0:4], 128, ["s4_%d" % h for h in range(4)], "s4r")
                        for h in range(4):
                            V(lambda e: e.scalar_tensor_tensor(out=oh_[:, h * 128:(h + 1) * 128], in0=oh_[:, h * 128:(h + 1) * 128], scalar=s4[:, 4 + h:5 + h], in1=sgl[:],
                                                               op0=ALU.mult, op1=ALU.mult), ["oh_%d" % h, "s4r", "sgl"], ["oh_%d" % h])
                        V(lambda e: e.tensor_tensor(out=ob[:], in0=oh_[:], in1=za[:], op=ALU.mult), ["oh_%d" % h for h in range(4)] + ["za"], ["ob"])
                        o_to_T(ob[:], "ob", oTt, "oTt")
                        outproj_update_prompt(i, oTt, "oTt", banks=(0, 1))
                    S.barrier()

            if "B" in do_branches or do_sample:
                load_w(l, [(OXBC, 1032), (OZ + 512, 512)])
                if do_sample:
                    sample_ssd()
            if "B" in do_branches:
                with ExitStack() as bs:
                    NCH = 8
                    xc = T(bs, "xc", [128, NCH, 131])
                    cwT = T(bs, "cwT", [128, 40])
                    cwr = T(bs, "cwr", [40, 128])
                    xact = T(bs, "xact", [128, NCH, 128])
                    acc = T(bs, "acc", [128, 128])
                    xact_b = T(bs, "xact_b", [128, 4, 128], BF16)
                    xs_tok = T(bs, "xs_tok", [128, 512])
                    Btok = T(bs, "Btok", [128, 2, 128], BF16)
                    rows = T(bs, "rows", [128, 32])
                    dtv = T(bs, "dtv", [128, 48])
                    aU = T(bs, "aU", [128, 128])
                    Es = T(bs, "Es", [128, 8, 128])
                    MT = T(bs, "MT", [128, 8, 128], BF16)
                    xdt = T(bs, "xdt", [128, 8, 64], BF16)
                    xdtw = T(bs, "xdtw", [128, 8, 64], BF16)
                    hS = T(bs, "hS", [128, 8, 64])
                    hSb = T(bs, "hSb", [128, 8, 64], BF16)
                    ysb = T(bs, "ysb", [128, 512])
                    zb = T(bs, "zb", [128, 512])
                    ng = T(bs, "ng", [128, 512])
                    ob = T(bs, "obB", [128, 512], BF16)
                    s2 = T(bs, "s2", [128, 4])
                    jk = T(bs, "jkB", [128, 256])
                    st3 = T(bs, "st3", [3, 1024])
                    hout = T(bs, "hout", [64, 8, 128])
                    load_wout(l, 512)
                    S.dma("sp", cwr[0:32, :], ssm_conv_w[l:l + 1, :].rearrange("o (a p) -> (o a) p", p=128), writes=["cwr"], slot="cwr")
                    S.dma("sp", cwr[32:40, :], ssm_conv_b[l:l + 1, :].rearrange("o (a p) -> (o a) p", p=128), writes=["cwr"], slot="cwr")
                    transpose_f32(bank[0][:, 0:40], cwr[:], ["cwr"], [BK[0]], rows=40)
                    V(lambda e: e.tensor_copy(out=cwT[:], in_=bank[0][:, 0:40]), [BK[0]], ["cwT"])
                    bcast_row(rows[:, 0:8], ssm_dt_bias[l:l + 1, :], 8, "rows_a")
                    bcast_row(rows[:, 8:16], ssm_A_log[l:l + 1, :], 8, "rows_b")
                    bcast_row(rows[:, 16:24], ssm_D[l:l + 1, :], 8, "rows_c")
                    RW = ["rows_a", "rows_b", "rows_c"]
                    A(lambda e: e.activation(out=rows[:, 24:32], in_=rows[:, 8:16], func=AF.Exp), RW, ["rows_d"])
                    RW = RW + ["rows_d"]
                    bcast_row(ng[:], ssm_norm_g[l:l + 1, :], 512, "ngB")
                    V(lambda e: e.memset(xc[:], 0.0), [], ["xc"])
                    V(lambda e: e.memset(hS[:], 0.0), [], ["hS"])
                    V(lambda e: e.memset(hSb[:], 0.0), [], ["hSb"])
                    for i in range(NT):
                        hk_ = ["hT%d" % i]
                        rhs_fn = lambda k: hT[:, k, i * 128:(i + 1) * 128]
                        for half in range(2):
                            for c in range(4):
                                proj_feat(bank[half][:, c * 128:(c + 1) * 128], BK[half], (half * 4 + c) * 128, rhs_fn, hk_)
                            A(lambda e: e.activation(out=xc[:, half * 4:(half + 1) * 4, 3:131], in_=bank[half][:, :].rearrange("p (c t) -> p c t", t=128), func=AF.Copy),
                              [BK[half]], ["xc"])
                        for c in range(NCH):
                            G(lambda e: e.tensor_scalar(out=acc[:], in0=xc[:, c, 3:131], scalar1=cwT[:, 24 + c:25 + c], scalar2=None, op0=ALU.mult), ["xc", "cwT"], ["acc"])
                            for tap in range(3):
                                V(lambda e: e.scalar_tensor_tensor(out=acc[:], in0=xc[:, c, tap:tap + 128], scalar=cwT[:, tap * 8 + c:tap * 8 + c + 1], in1=acc[:],
                                                                   op0=ALU.mult, op1=ALU.add), ["xc", "cwT", "acc"], ["acc"])
                            A(lambda e: e.activation(out=xact[:, c, :], in_=acc[:], func=AF.Silu, bias=cwT[:, 32 + c:33 + c]), ["acc", "cwT"], ["xact%d" % c])
                        if i == NT - 1:
                            for c in range(NCH):
                                transpose_f32(bank[6][0:3, c * 128:(c + 1) * 128] if c < 4 else bank[5][0:3, (c - 4) * 128:(c - 3) * 128], xc[:, c, 128:131], ["xc"], [BK[6] if c < 4 else BK[5]])
                            V(lambda e: e.tensor_copy(out=st3[:, 0:512], in_=bank[6][0:3, :]), [BK[6]], ["st3"])
                            V(lambda e: e.tensor_copy(out=st3[:, 512:1024], in_=bank[5][0:3, :]), [BK[5], "st3"], ["st3"])
                            S.dma("sp", sconv_p[l], st3[:], reads=["st3"], slot="sconvp")
                        V(lambda e: e.tensor_copy(out=xc[:, :, 0:3], in_=xc[:, :, 128:131]), ["xc"], ["xc"])
                        XA = ["xact%d" % c for c in range(NCH)]
                        proj_tok(bank[2][:, 0:8], BK[2], lambda k: hT[:, k, i * 128:(i + 1) * 128], hk_, 1024, 8)
                        V(lambda e: e.tensor_tensor(out=dtv[:, 0:8], in0=bank[2][:, 0:8], in1=rows[:, 0:8], op=ALU.add), [BK[2]] + RW, ["dtv"])
                        softplus(dtv[:, 0:8], dtv[:, 0:8], ["dtv"], "dtv")
                        V(lambda e: e.scalar_tensor_tensor(out=dtv[:, 8:16], in0=dtv[:, 0:8], scalar=-1.0, in1=rows[:, 24:32], op0=ALU.mult, op1=ALU.mult), ["dtv"] + RW, ["dtv"])
                        proj_tok(bank[3][:, :], BK[3], lambda k: hT[:, k, i * 128:(i + 1) * 128], hk_, 1032, 512)
                        A(lambda e: e.activation(out=zb[:], in_=bank[3][:, :], func=AF.Silu), [BK[3]], ["zb"])
                        for c in range(4):
                            transpose_f32(bank[4][:, c * 128:(c + 1) * 128], xact[:, c, :], ["xact%d" % c], [BK[4]])
                        V(lambda e: e.tensor_copy(out=xs_tok[:], in_=bank[4][:, :]), [BK[4]], ["xs_tok"])
                        for c in range(2):
                            transpose_f32(bank[5][:, c * 128:(c + 1) * 128], xact[:, 4 + c, :], ["xact%d" % (4 + c)], [BK[5]])
                        A(lambda e: e.activation(out=Btok[:], in_=bank[5][:, 0:256].rearrange("p (g n) -> p g n", n=128), func=AF.Copy), [BK[5]], ["Btok"])
                        A(lambda e: e.activation(out=xact_b[:], in_=xact[:, 4:8, :], func=AF.Copy), XA, ["xact_b"])
                        V(lambda e: e.tensor_tensor(out=xdt[:], in0=xs_tok[:].rearrange("p (h q) -> p h q", q=64), in1=dtv[:, 0:8].unsqueeze(2).to_broadcast([128, 8, 64]), op=ALU.mult),
                          ["xs_tok", "dtv"], ["xdt"])
                        PE(lambda e: e.matmul(bank[2][:, 16:24], lhsT=Umat, rhs=dtv[:, 8:16], start=True, stop=True), ["cm", "dtv"], [BK[2]])
                        PE(lambda e: e.matmul(bank[2][:, 32:40], lhsT=onesf[:], rhs=dtv[:, 8:16], start=True, stop=True), ["onesf", "dtv"], [BK[2]])
                        V(lambda e: e.tensor_copy(out=dtv[:, 16:24], in_=bank[2][:, 16:24]), [BK[2], "dtv"], ["dtv"])
                        V(lambda e: e.tensor_scalar(out=dtv[:, 24:32], in0=bank[2][:, 16:24], scalar1=-1.0, scalar2=None, op0=ALU.mult), [BK[2], "dtv"], ["dtv"])
                        A(lambda e: e.activation(out=dtv[:, 32:40], in_=bank[2][:, 16:24], func=AF.Exp), [BK[2], "dtv"], ["dtv"])
                        A(lambda e: e.activation(out=dtv[:, 40:48], in_=bank[2][:, 32:40], func=AF.Exp), [BK[2], "dtv"], ["dtv"])
                        for g in range(2):
                            PE(lambda e: e.matmul(bank[3][:, g * 128:(g + 1) * 128], lhsT=xact_b[:, g, :], rhs=xact_b[:, 2 + g, :], start=True, stop=True), ["xact_b"], [BK[3]])
                        for h in range(8):
                            g = h // 4
                            sb_ = h % 2
                            G(lambda e: e.tensor_scalar(out=aU[:], in0=Umat, scalar1=dtv[:, 8 + h:9 + h], scalar2=None, op0=ALU.mult), ["cm", "dtv"], ["aU"])
                            def f(e):
                                e.matmul(bank[sb_][:, 0:128], lhsT=onesf[:], rhs=aU[:], start=True, stop=False)
                                return e.matmul(bank[sb_][:, 0:128], lhsT=ident, rhs=negmask, start=False, stop=True)
                            PE(f, ["onesf", "aU", "cm"], [BK[sb_]])
                            A(lambda e: e.activation(out=Es[:, h, :], in_=bank[sb_][:, 0:128], func=AF.Exp, bias=dtv[:, 24 + h:25 + h]), [BK[sb_], "dtv"], ["Es%d" % h])
                            V(lambda e: e.tensor_tensor(out=MT[:, h, :], in0=Es[:, h, :], in1=bank[3][:, g * 128:(g + 1) * 128], op=ALU.mult), ["Es%d" % h, BK[3]], ["MT%d" % h])
                        ES = ["Es%d" % h for h in range(8)]
                        V(lambda e: e.tensor_tensor(out=xdtw[:], in0=xdt[:], in1=Es[:, :, 127:128].to_broadcast([128, 8, 64]), op=ALU.mult), ["xdt"] + ES, ["xdtw"])
                        def f(e):
                            for h in range(8):
                                ins = e.matmul(bank[4][:, h * 64:(h + 1) * 64], lhsT=MT[:, h, :], rhs=xdt[:, h, :], start=True, stop=True)
                            return ins
                        PE(f, ["MT%d" % h for h in range(8)] + ["xdt"], [BK[4]])
                        def f(e):
                            for h in range(8):
                                ins = e.matmul(bank[5][:, h * 64:(h + 1) * 64], lhsT=xact_b[:, 2 + h // 4, :], rhs=hSb[:, h, :], start=True, stop=True)
                            return ins
                        PE(f, ["xact_b", "hSb"], [BK[5]])
                        V(lambda e: e.tensor_tensor(out=ysb[:].rearrange("p (h q) -> p h q", q=64), in0=bank[5][:, :].rearrange("p (h q) -> p h q", q=64),
                                                    in1=dtv[:, 32:40].unsqueeze(2).to_broadcast([128, 8, 64]), op=ALU.mult), [BK[5], "dtv"], ["ysb"])
                        V(lambda e: e.tensor_tensor(out=ysb[:], in0=ysb[:], in1=bank[4][:, :], op=ALU.add), ["ysb", BK[4]], ["ysb"])
                        def f(e):
                            for h in range(8):
                                ins = e.matmul(bank[6][:, h * 64:(h + 1) * 64], lhsT=Btok[:, h // 4, :], rhs=xdtw[:, h, :], start=True, stop=True)
                            return ins
                        PE(f, ["Btok", "xdtw"], [BK[6]])
                        V(lambda e: e.tensor_tensor(out=hS[:], in0=hS[:], in1=dtv[:, 40:48].unsqueeze(2).to_broadcast([128, 8, 64]), op=ALU.mult), ["hS", "dtv"], ["hS"])
                        V(lambda e: e.tensor_tensor(out=hS[:], in0=hS[:], in1=bank[6][:, :].rearrange("p (h q) -> p h q", q=64), op=ALU.add), ["hS", BK[6]], ["hS"])
                        A(lambda e: e.activation(out=hSb[:], in_=hS[:], func=AF.Copy), ["hS"], ["hSb"])
                        V(lambda e: e.tensor_tensor(out=xs_tok[:].rearrange("p (h q) -> p h q", q=64), in0=xs_tok[:].rearrange("p (h q) -> p h q", q=64),
                                                    in1=rows[:, 16:24].unsqueeze(2).to_broadcast([128, 8, 64]), op=ALU.mult), ["xs_tok", "xdt"] + RW, ["xs_tok"])
                        V(lambda e: e.tensor_tensor(out=ysb[:], in0=ysb[:], in1=xs_tok[:], op=ALU.add), ["ysb", "xs_tok"], ["ysb"])
                        V(lambda e: e.tensor_tensor(out=ysb[:], in0=ysb[:], in1=zb[:], op=ALU.mult), ["ysb", "zb"], ["ysb"])
                        for g in range(2):
                            A(lambda e: e.activation(out=jk[:], in_=ysb[:, g * 256:(g + 1) * 256], func=AF.Square, accum_out=s2[:, g:g + 1]), ["ysb"], ["jkB", "s2_%d" % g])
                        rstd_from_ss(s2[:, 2:4], s2[:, 0:2], 256, ["s2_0", "s2_1"], "s2r")
                        for g in range(2):
                            V(lambda e: e.scalar_tensor_tensor(out=ob[:, g * 256:(g + 1) * 256], in0=ysb[:, g * 256:(g + 1) * 256], scalar=s2[:, 2 + g:3 + g], in1=ng[:, g * 256:(g + 1) * 256],
                                                               op0=ALU.mult, op1=ALU.mult), ["ysb", "s2r", "ngB"], ["obB"])
                        o_to_T(ob[:], "obB", oTt, "oTt")
                        outproj_update_prompt(i, oTt, "oTt")
                    for h in range(8):
                        transpose_f32(bank[h % 2][0:64, 0:128], hS[:, h, :], ["hS"], [BK[h % 2]])
                        V(lambda e: e.tensor_copy(out=hout[:, h, :], in_=bank[h % 2][0:64, 0:128]), [BK[h % 2]], ["hout"])
                    S.dma("sp", ssm_p[l].rearrange("h p n -> p h n"), hout[:], reads=["hout"], slot="ssmp")
                    S.barrier()

            if "C" in do_branches or do_sample:
                load_w(l, [(OQKV, 1544), (OZ + 1024, 512)])
                if do_sample:
                    sample_gdn()
            if "C" in do_branches:
                with ExitStack() as bs:
                    NCH = 12
                    xc = T(bs, "xcC", [128, 4, 131])
                    hist = T(bs, "histC", [128, NCH, 3])
                    cwT = T(bs, "cwTC", [128, 48])
                    cwr = T(bs, "cwrC", [48, 128])
                    xact = T(bs, "xactC", [128, 4, 128])
                    acc = T(bs, "accC", [128, 128])
                    sq = T(bs, "sqC", [128, 4, 128])
                    qkb = T(bs, "qkb", [128, 8, 128], BF16)
                    ktok = T(bs, "ktok", [128, 4, 128], BF16)
                    kd = T(bs, "kd", [128, 4, 128], BF16)
                    vtok = T(bs, "vtok", [128, 4, 128])
                    rows = T(bs, "rowsC", [128, 16])
                    gv = T(bs, "gv", [128, 40])
                    gU = T(bs, "gU", [128, 128])
                    DTm = T(bs, "DTm", [128, 4, 128])
                    Dm = T(bs, "Dm", [128, 128])
                    NX = T(bs, "NX", [128, 4, 128], BF16)
                    NY = T(bs, "NY", [128, 4, 128], BF16)
                    XY = [T(bs, "XY%d" % i, [128, 4, 128], BF16) for i in range(4)]
                    Pn = T(bs, "Pn", [128, 4, 128], BF16)
                    Pt = T(bs, "Pt", [128, 4, 128], BF16)
                    gmask = T(bs, "gmask", [128, 4, 128], BF16)
                    S.dma("pool", gmask[:], gmask_d, writes=["gmask"], slot="gmask")
                    attT = T(bs, "attT", [128, 4, 128], BF16)
                    Sg = T(bs, "Sg", [128, 4, 128])
                    Sgb = T(bs, "Sgb", [128, 4, 128], BF16)
                    rr = T(bs, "rr", [128, 4, 128])
                    rb = T(bs, "rb", [128, 4, 128], BF16)
                    vnb = T(bs, "vnb", [128, 4, 128], BF16)
                    o1 = rr
                    zc = T(bs, "zc", [128, 512])
                    ngc = T(bs, "ngc", [128, 128])
                    ob = T(bs, "obC", [128, 512], BF16)
                    s4 = T(bs, "s4C", [128, 8])
                    jk = T(bs, "jkC", [128, 128])
                    st3 = T(bs, "st3C", [3, 512])
                    load_wout(l, 1024)
                    S.dma("sp", cwr[:], gdn_conv_w[l:l + 1, :].rearrange("o (a p) -> (o a) p", p=128), writes=["cwrC"], slot="cwrC")
                    transpose_f32(bank[0][:, 0:48], cwr[:], ["cwrC"], [BK[0]], rows=48)
                    V(lambda e: e.tensor_copy(out=cwT[:], in_=bank[0][:, 0:48]), [BK[0]], ["cwTC"])
                    bcast_row(rows[:, 0:4], gdn_dt_bias[l:l + 1, :], 4, "rowsC_a")
                    bcast_row(rows[:, 4:8], gdn_A_log[l:l + 1, :], 4, "rowsC_b")
                    RW = ["rowsC_a", "rowsC_b"]
                    A(lambda e: e.activation(out=rows[:, 8:12], in_=rows[:, 4:8], func=AF.Exp), RW, ["rowsC_c"])
                    RW = RW + ["rowsC_c"]
                    bcast_row(ngc[:], gdn_norm_g[l:l + 1, :], 128, "ngc")
                    V(lambda e: e.memset(hist[:], 0.0), [], ["histC"])
                    V(lambda e: e.memset(Sg[:], 0.0), [], ["Sg"])
                    V(lambda e: e.memset(Sgb[:], 0.0), [], ["Sgb"])
                    for i in range(NT if DBG_TILES is None else DBG_TILES):
                        hk_ = ["hT%d" % i]
                        rhs_fn = lambda k: hT[:, k, i * 128:(i + 1) * 128]
                        for q4 in range(3):
                            bk_ = q4 % 2
                            for c in range(4):
                                proj_feat(bank[bk_][:, c * 128:(c + 1) * 128], BK[bk_], (q4 * 4 + c) * 128, rhs_fn, hk_)
                            A(lambda e: e.activation(out=xc[:, :, 3:131], in_=bank[bk_][:, :].rearrange("p (c t) -> p c t", t=128), func=AF.Copy), [BK[bk_]], ["xcC"])
                            V(lambda e: e.tensor_copy(out=xc[:, :, 0:3], in_=hist[:, q4 * 4:(q4 + 1) * 4, :]), ["histC", "xcC"], ["xcC"])
                            for c in range(4):
                                cc = q4 * 4 + c
                                G(lambda e: e.tensor_scalar(out=acc[:], in0=xc[:, c, 3:131], scalar1=cwT[:, 36 + cc:37 + cc], scalar2=None, op0=ALU.mult), ["xcC", "cwTC"], ["accC"])
                                for tap in range(3):
                                    V(lambda e: e.scalar_tensor_tensor(out=acc[:], in0=xc[:, c, tap:tap + 128], scalar=cwT[:, tap * 12 + cc:tap * 12 + cc + 1], in1=acc[:],
                                                                       op0=ALU.mult, op1=ALU.add), ["xcC", "cwTC", "accC"], ["accC"])
                                A(lambda e: e.activation(out=xact[:, c, :], in_=acc[:], func=AF.Silu), ["accC"], ["xactC"])
                            if i == NT - 1:
                                for c in range(4):
                                    transpose_f32(bank[4][0:3, c * 128:(c + 1) * 128], xc[:, c, 128:131], ["xcC"], [BK[4]])
                                V(lambda e: e.tensor_copy(out=st3[:], in_=bank[4][0:3, :]), [BK[4]], ["st3C"])
                                S.dma("sp", gconv_p[l][:, q4 * 512:(q4 + 1) * 512], st3[:], reads=["st3C"], slot="gconvp")
                            V(lambda e: e.tensor_copy(out=hist[:, q4 * 4:(q4 + 1) * 4, :], in_=xc[:, :, 128:131]), ["xcC"], ["histC"])
                            if q4 < 2:
                                A(lambda e: e.activation(out=sq[:], in_=xact[:], func=AF.Square), ["xactC"], ["sqC"])
                                PE(lambda e: e.matmul(bank[2 + q4][:, :], lhsT=onesf[:], rhs=sq[:].rearrange("p c t -> p (c t)"), start=True, stop=True), ["onesf", "sqC"], [BK[2 + q4]])
                                A(lambda e: e.activation(out=sq[:], in_=bank[2 + q4][:, :].rearrange("p (c t) -> p c t", t=128), func=AF.Sqrt,
                                                         bias=(epsc[:, 2:3] if q4 == 0 else epsc[:, 0:1]), scale=(128.0 if q4 == 0 else 1.0)), [BK[2 + q4], "epsc"], ["sqC"])
                                V(lambda e: e.reciprocal(out=sq[:], in_=sq[:]), ["sqC"], ["sqC"])
                                V(lambda e: e.tensor_tensor(out=qkb[:, q4 * 4:(q4 + 1) * 4, :], in0=xact[:], in1=sq[:], op=ALU.mult), ["xactC", "sqC"], ["qkb"])
                            else:
                                for h in range(4):
                                    transpose_f32(bank[4][:, h * 128:(h + 1) * 128], xact[:, h, :], ["xactC"], [BK[4]])
                                V(lambda e: e.tensor_copy(out=vtok[:], in_=bank[4][:, :].rearrange("p (h d) -> p h d", d=128)), [BK[4]], ["vtok"])
                        def f(e):
                            for h in range(4):
                                ins = e.transpose(out=trb[0][:, h * 128:(h + 1) * 128], in_=qkb[:, 4 + h, :], identity=identb[:])
                            return ins
                        PE(f, ["qkb", "identb"], [TRB[0]])
                        A(lambda e: e.activation(out=ktok[:], in_=trb[0][:, :].rearrange("p (h d) -> p h d", d=128), func=AF.Copy), [TRB[0]], ["ktok"])
                        proj_tok(bank[5][:, 0:8], BK[5], lambda k: hT[:, k, i * 128:(i + 1) * 128], hk_, 1536, 8)
                        A(lambda e: e.activation(out=gv[:, 0:4], in_=bank[5][:, 0:4], func=AF.Sigmoid), [BK[5], "gv"], ["gv"])
                        V(lambda e: e.tensor_tensor(out=gv[:, 4:8], in0=bank[5][:, 4:8], in1=rows[:, 0:4], op=ALU.add), [BK[5], "gv"] + RW, ["gv"])
                        softplus(gv[:, 4:8], gv[:, 4:8], ["gv"], "gv")
                        V(lambda e: e.scalar_tensor_tensor(out=gv[:, 4:8], in0=gv[:, 4:8], scalar=-1.0, in1=rows[:, 8:12], op0=ALU.mult, op1=ALU.mult), ["gv"] + RW, ["gv"])
                        V(lambda e: e.tensor_scalar(out=gv[:, 32:36], in0=gv[:, 0:4], scalar1=-1.0, scalar2=None, op0=ALU.mult), ["gv"], ["gv"])
                        proj_tok(bank[6][:, :], BK[6], lambda k: hT[:, k, i * 128:(i + 1) * 128], hk_, 1544, 512)
                        A(lambda e: e.activation(out=zc[:], in_=bank[6][:, :], func=AF.Silu), [BK[6]], ["zc"])
                        PE(lambda e: e.matmul(bank[5][:, 16:20], lhsT=Umat, rhs=gv[:, 4:8], start=True, stop=True), ["cm", "gv"], [BK[5]])
                        PE(lambda e: e.matmul(bank[5][:, 32:36], lhsT=onesf[:], rhs=gv[:, 4:8], start=True, stop=True), ["onesf", "gv"], [BK[5]])
                        V(lambda e: e.tensor_copy(out=gv[:, 8:12], in_=bank[5][:, 16:20]), [BK[5], "gv"], ["gv"])
                        V(lambda e: e.tensor_scalar(out=gv[:, 12:16], in0=bank[5][:, 16:20], scalar1=-1.0, scalar2=None, op0=ALU.mult), [BK[5], "gv"], ["gv"])
                        A(lambda e: e.activation(out=gv[:, 16:20], in_=bank[5][:, 16:20], func=AF.Exp), [BK[5], "gv"], ["gv"])
                        V(lambda e: e.tensor_scalar(out=gv[:, 20:24], in0=gv[:, 16:20], scalar1=-1.0, scalar2=None, op0=ALU.mult), ["gv"], ["gv"])
                        A(lambda e: e.activation(out=gv[:, 24:28], in_=bank[5][:, 32:36], func=AF.Exp), [BK[5], "gv"], ["gv"])
                        V(lambda e: e.tensor_tensor(out=gv[:, 28:32], in0=bank[5][:, 32:36], in1=gv[:, 8:12], op=ALU.subtract), [BK[5], "gv"], ["gv"])
                        A(lambda e: e.activation(out=gv[:, 28:32], in_=gv[:, 28:32], func=AF.Exp), ["gv"], ["gv"])
                        V(lambda e: e.tensor_tensor(out=kd[:], in0=ktok[:], in1=gv[:, 28:32].unsqueeze(2).to_broadcast([128, 4, 128]), op=ALU.mult), ["ktok", "gv"], ["kd"])
                        for h in range(4):
                            sb_ = 2 + (h % 2)
                            G(lambda e: e.tensor_scalar(out=gU[:], in0=Umat, scalar1=gv[:, 4 + h:5 + h], scalar2=None, op0=ALU.mult), ["cm", "gv"], ["gU"])
                            def f(e):
                                e.matmul(bank[sb_][:, 0:128], lhsT=onesf[:], rhs=gU[:], start=True, stop=False)
                                return e.matmul(bank[sb_][:, 0:128], lhsT=ident, rhs=negmask, start=False, stop=True)
                            PE(f, ["onesf", "gU", "cm"], [BK[sb_]])
                            A(lambda e: e.activation(out=DTm[:, h, :], in_=bank[sb_][:, 0:128], func=AF.Exp, bias=gv[:, 12 + h:13 + h]), [BK[sb_], "gv"], ["DTm%d" % h])
                            transpose_f32(bank[sb_][:, 128:256], DTm[:, h, :], ["DTm%d" % h], [BK[sb_]])
                            V(lambda e: e.tensor_tensor(out=Dm[:], in0=bank[sb_][:, 128:256], in1=strictL, op=ALU.mult), [BK[sb_], "cm"], ["Dm"])
                            PE(lambda e: e.matmul(bank[sb_][:, 256:384], lhsT=qkb[:, 4 + h, :], rhs=qkb[:, 4 + h, :], start=True, stop=True), ["qkb"], [BK[sb_]])
                            V(lambda e: e.scalar_tensor_tensor(out=NX[:, h, :], in0=bank[sb_][:, 256:384], scalar=gv[:, 32 + h:33 + h], in1=Dm[:], op0=ALU.mult, op1=ALU.mult),
                              [BK[sb_], "gv", "Dm"], ["NX"])
                            PE(lambda e: e.matmul(bank[sb_][:, 384:512], lhsT=qkb[:, 4 + h, :], rhs=qkb[:, h, :], start=True, stop=True), ["qkb"], [BK[sb_]])
                            V(lambda e: e.tensor_tensor(out=attT[:, h, :], in0=bank[sb_][:, 384:512], in1=DTm[:, h, :], op=ALU.mult), [BK[sb_], "DTm%d" % h], ["attT"])
                        def f(e):
                            for h in range(4):
                                ins = e.transpose(out=trb[1][:, h * 128:(h + 1) * 128], in_=NX[:, h, :], identity=identb[:])
                            return ins
                        PE(f, ["NX", "identb"], [TRB[1]])
                        A(lambda e: e.activation(out=NY[:], in_=trb[1][:, :].rearrange("p (h d) -> p h d", d=128), func=AF.Copy), [TRB[1]], ["NY"])
                        mk = lambda j: gmask[:, j, :].unsqueeze(1).to_broadcast([128, 4, 128])
                        idb4 = identb[:].unsqueeze(1).to_broadcast([128, 4, 128])
                        V(lambda e: e.tensor_tensor(out=XY[0][:], in0=NX[:], in1=mk(0), op=ALU.mult), ["NX", "gmask"], ["XY0"])
                        G(lambda e: e.tensor_tensor(out=XY[1][:], in0=NY[:], in1=mk(0), op=ALU.mult), ["NY", "gmask"], ["XY1"])
                        V(lambda e: e.tensor_tensor(out=Pn[:], in0=XY[0][:], in1=idb4, op=ALU.add), ["XY0", "identb"], ["Pn"])
                        G(lambda e: e.tensor_tensor(out=Pt[:], in0=XY[1][:], in1=idb4, op=ALU.add), ["XY1", "identb"], ["Pt"])
                        cx, cy = 0, 1
                        def mm4(bk, lh, rh):
                            def f(e):
                                for h in range(4):
                                    ins = e.matmul(bank[bk][:, h * 128:(h + 1) * 128], lhsT=lh[:, h, :], rhs=rh[:, h, :], start=True, stop=True)
                                return ins
                            return f
                        b4 = lambda bk: bank[bk][:, :].rearrange("p (h d) -> p h d", d=128)
                        for stp in range(3):
                            nx_, ny_ = cx + 2, cy + 2
                            if nx_ > 3:
                                nx_, ny_ = nx_ - 4, ny_ - 4
                            PE(mm4(2, XY[cy], XY[cx]), ["XY%d" % cy, "XY%d" % cx], [BK[2]])
                            A(lambda e: e.activation(out=XY[nx_][:], in_=b4(2), func=AF.Copy), [BK[2]], ["XY%d" % nx_])
                            PE(mm4(3, XY[cx], XY[cy]), ["XY%d" % cy, "XY%d" % cx], [BK[3]])
                            A(lambda e: e.activation(out=XY[ny_][:], in_=b4(3), func=AF.Copy), [BK[3]], ["XY%d" % ny_])
                            PE(mm4(4, XY[ny_], Pn), ["XY%d" % ny_, "Pn"], [BK[4]])
                            PE(mm4(1, XY[nx_], Pt), ["XY%d" % nx_, "Pt"], [BK[1]])
                            V(lambda e: e.tensor_tensor(out=Pn[:], in0=b4(4), in1=Pn[:], op=ALU.add), [BK[4], "Pn"], ["Pn"])
                            V(lambda e: e.tensor_tensor(out=Pt[:], in0=b4(1), in1=Pt[:], op=ALU.add), [BK[1], "Pt"], ["Pt"])
                            cx, cy = nx_, ny_
                        YE = XY[0]; Zb = XY[1]
                        for lvl in range(3):
                            G(lambda e: e.tensor_tensor(out=YE[:], in0=NY[:], in1=mk(1 + lvl), op=ALU.mult), ["NY", "gmask"], ["XY0"])
                            PE(mm4(2, YE, Pn), ["XY0", "Pn"], [BK[2]])
                            A(lambda e: e.activation(out=Zb[:], in_=b4(2), func=AF.Copy), [BK[2]], ["XY1"])
                            if lvl < 2:
                                PE(mm4(3, Pt, Zb), ["Pt", "XY1"], [BK[3]])
                            PE(mm4(4, Zb, Pt), ["Pt", "XY1"], [BK[4]])
                            if lvl < 2:
                                V(lambda e: e.tensor_tensor(out=Pn[:], in0=b4(3), in1=Pn[:], op=ALU.add), [BK[3], "Pn"], ["Pn"])
                            V(lambda e: e.tensor_tensor(out=Pt[:], in0=b4(4), in1=Pt[:], op=ALU.add), [BK[4], "Pt"], ["Pt"])
                        TT = Pt; TTk = "Pt"
                        def f(e):
                            for h in range(4):
                                ins = e.matmul(bank[5][:, h * 128:(h + 1) * 128], lhsT=qkb[:, 4 + h, :], rhs=Sgb[:, h, :], start=True, stop=True)
                            return ins
                        PE(f, ["qkb", "Sgb"], [BK[5]])
                        V(lambda e: e.tensor_tensor(out=rr[:], in0=bank[5][:, :].rearrange("p (h d) -> p h d", d=128), in1=gv[:, 20:24].unsqueeze(2).to_broadcast([128, 4, 128]), op=ALU.mult),
                          [BK[5], "gv"], ["rr"])
                        V(lambda e: e.tensor_tensor(out=rr[:], in0=rr[:], in1=vtok[:], op=ALU.add), ["rr", "vtok"], ["rr"])
                        V(lambda e: e.tensor_tensor(out=rb[:], in0=rr[:], in1=gv[:, 0:4].unsqueeze(2).to_broadcast([128, 4, 128]), op=ALU.mult), ["rr", "gv"], ["rb"])
                        def f(e):
                            for h in range(4):
                                ins = e.matmul(bank[6][:, h * 128:(h + 1) * 128], lhsT=TT[:, h, :], rhs=rb[:, h, :], start=True, stop=True)
                            return ins
                        PE(f, [TTk, "rb"], [BK[6]])
                        A(lambda e: e.activation(out=vnb[:], in_=bank[6][:, :].rearrange("p (h d) -> p h d", d=128), func=AF.Copy), [BK[6]], ["vnb"])
                        def f(e):
                            for h in range(4):
                                ins = e.matmul(bank[5][:, h * 128:(h + 1) * 128], lhsT=qkb[:, h, :], rhs=Sgb[:, h, :], start=True, stop=True)
                            return ins
                        PE(f, ["qkb", "Sgb"], [BK[5]])
                        V(lambda e: e.tensor_tensor(out=o1[:], in0=bank[5][:, :].rearrange("p (h d) -> p h d", d=128), in1=gv[:, 16:20].unsqueeze(2).to_broadcast([128, 4, 128]), op=ALU.mult),
                          [BK[5], "gv"], ["rr"])
                        def f(e):
                            for h in range(4):
                                ins = e.matmul(bank[6][:, h * 128:(h + 1) * 128], lhsT=attT[:, h, :], rhs=vnb[:, h, :], start=True, stop=True)
                            return ins
                        PE(f, ["attT", "vnb"], [BK[6]])
                        V(lambda e: e.tensor_tensor(out=o1[:], in0=o1[:], in1=bank[6][:, :].rearrange("p (h d) -> p h d", d=128), op=ALU.add), ["rr", BK[6]], ["rr"])
                        def f(e):
                            for h in range(4):
                                ins = e.matmul(bank[5][:, h * 128:(h + 1) * 128], lhsT=kd[:, h, :], rhs=vnb[:, h, :], start=True, stop=True)
                            return ins
                        PE(f, ["kd", "vnb"], [BK[5]])
                        V(lambda e: e.tensor_tensor(out=Sg[:], in0=Sg[:], in1=gv[:, 24:28].unsqueeze(2).to_broadcast([128, 4, 128]), op=ALU.mult), ["Sg", "gv"], ["Sg"])
                        V(lambda e: e.tensor_tensor(out=Sg[:], in0=Sg[:], in1=bank[5][:, :].rearrange("p (h d) -> p h d", d=128), op=ALU.add), ["Sg", BK[5]], ["Sg"])
                        A(lambda e: e.activation(out=Sgb[:], in_=Sg[:], func=AF.Copy), ["Sg"], ["Sgb"])
                        for h in range(4):
                            A(lambda e: e.activation(out=jk[:], in_=o1[:, h, :], func=AF.Square, accum_out=s4[:, h:h + 1]), ["rr"], ["jkC", "s4C_%d" % h])
                        rstd_from_ss(s4[:, 4:8], s4[:, 0:4], 128, ["s4C_%d" % h for h in range(4)], "s4Cr")
                        for h in range(4):
                            V(lambda e: e.scalar_tensor_tensor(out=o1[:, h, :], in0=o1[:, h, :], scalar=s4[:, 4 + h:5 + h], in1=ngc[:], op0=ALU.mult, op1=ALU.mult), ["rr", "s4Cr", "ngc"], ["rr"])
                        V(lambda e: e.tensor_tensor(out=ob[:], in0=o1[:].rearrange("p h d -> p (h d)"), in1=zc[:], op=ALU.mult), ["rr", "zc"], ["obC"])
                        o_to_T(ob[:], "obC", oTt, "oTt")
                        outproj_update_prompt(i, oTt, "oTt")
                    S.dma("sp", gdn_p[l].rearrange("h k v -> k h v"), Sg[:], reads=["Sg"], slot="gdnp")
                    S.barrier()
            if do_sample:
                sample_finish()

        with ExitStack() as fs:
            junk = T(fs, "junkF", [128, D])
            ssq = T(fs, "ssqF", [128, 2])
            yt = [T(fs, "yt%d" % i, [128, D]) for i in range(2)]
            fin_g = T(fs, "fin_g", [128, D])
            bcast_row(fin_g[:], final_g[0:1, :], D, "fin_g")
            for i in range(NT):
                b = i % 2
                A(lambda e: e.activation(out=junk[:], in_=x_sb[:, i, :], func=AF.Square, accum_out=ssq[:, 0:1]), ["x%d" % i], ["junkF", "ssqF"])
                rstd_from_ss(ssq[:, 1:2], ssq[:, 0:1], D, ["ssqF"], "ssqF")
                V(lambda e: e.scalar_tensor_tensor(out=yt[b][:], in0=x_sb[:, i, :], scalar=ssq[:, 1:2], in1=fin_g[:], op0=ALU.mult, op1=ALU.mult), ["x%d" % i, "ssqF", "fin_g"], ["yt%d" % b])
                S.dma("sp", y_p[i * 128:(i + 1) * 128, :], yt[b][:], reads=["yt%d" % b], slot="yp%d" % b)
            A(lambda e: e.activation(out=junk[0:NO, :], in_=xo[:], func=AF.Square, accum_out=ssq[0:NO, 0:1]), ["xo"], ["junkF", "ssqF"])
            rstd_from_ss(ssq[0:NO, 1:2], ssq[0:NO, 0:1], D, ["ssqF"], "ssqF")
            V(lambda e: e.scalar_tensor_tensor(out=yt[0][0:NO, :], in0=xo[:], scalar=ssq[0:NO, 1:2], in1=fin_g[0:NO, :], op0=ALU.mult, op1=ALU.mult), ["xo", "ssqF", "fin_g"], ["yt0"])
            S.dma("sp", y_s, yt[0][0:NO, :], reads=["yt0"], slot="ys")
            S.finish("sp")
        print("program built: ops=%d waits=%d" % (S.nops, S.nwaits), flush=True)
    return nc


def _bucket(n):
    n = np.asarray(n)
    nn = np.maximum(n, 0)
    exact = 16
    large = exact + (np.log(np.maximum(nn, 1).astype(np.float32) / np.float32(exact)).astype(np.float32)
                     / np.float32(math.log(128 / exact)) * np.float32(32 - exact)).astype(np.int32)
    return np.where(nn < exact, nn, np.minimum(large, 31))


def make_consts(core):
    cm = np.zeros((128, 5, 128), np.float32)
    r = np.arange(128)
    cm[:, 0, :] = np.eye(128)
    cm[:, 1, :] = (r[:, None] <= r[None, :])
    cm[:, 2, :] = np.where(r[:, None] <= r[None, :], 0.0, -1.0e6)
    cm[:, 3, :] = (r[None, :] < r[:, None])
    cm[:, 4, :] = np.eye(128)[::-1]
    oh = np.zeros((33, 512), np.float32)
    n = np.arange(512) - 127
    bk = _bucket(n)
    for i in range(512):
        if n[i] < 0:
            oh[32, i] = 1.0
        else:
            oh[bk[i], i] = 1.0
    p = np.arange(128)
    ohs = np.zeros((32, 128), np.float32)
    ohs[_bucket(128 - p), p] = 1.0
    cv = np.zeros((128, 16), np.float32)
    cv[:, 0] = p
    cv[:, 1] = p + NPOOL * 128
    for g in range(8):
        cv[:, 8 + g] = (p // 16 == g)
    return cm, oh, ohs, cv


def make_selown(core):
    so = np.zeros((NO, NS), np.float32)
    for i in range(NO):
        so[i, (NO * core + i) % NS] = 1.0
    return so


def make_gmask():
    r = np.arange(128)
    bd = lambda b: (r[:, None] // b == r[None, :] // b).astype(np.float32)
    g = np.zeros((128, 4, 128), np.float32)
    g[:, 0] = bd(16); g[:, 1] = bd(32) - bd(16); g[:, 2] = bd(64) - bd(32); g[:, 3] = 1.0 - bd(64)
    return g


_CACHE = {}
_RUN_KW = {}


def kernel(x_prompt, x_sample, c_prompt, c_sample, cache_k, cache_v, state_ssm, state_ssm_conv,
           state_gdn, state_gdn_conv, page_table, w_ada, b_ada, w_in, w_out, rel_bias, attn_lambda,
           attn_subln_g, ssm_conv_w, ssm_conv_b, ssm_dt_bias, ssm_A_log, ssm_D, ssm_norm_g,
           gdn_conv_w, gdn_dt_bias, gdn_A_log, gdn_norm_g, final_norm_g):
    f = lambda a: np.ascontiguousarray(np.asarray(a, dtype=np.float32))
    if "nc" not in _CACHE:
        _CACHE["nc"] = build_program()
    nc = _CACHE["nc"]
    x_prompt = f(x_prompt); x_sample = f(x_sample).reshape(NS, D); c_prompt = f(c_prompt); c_sample = f(c_sample)
    cache_k = np.asarray(cache_k); cache_v = np.asarray(cache_v)
    lamc = np.array([[0.8 - 0.6 * math.exp(-0.3 * 0), 0.8 - 0.6 * math.exp(-0.3 * 1), 0, 0]], np.float32)
    shared = {
        "xs_all": x_sample, "cs_all": c_sample,
        "w_ada": f(w_ada), "b_ada": f(b_ada), "w_in": f(w_in), "w_out": f(w_out), "rel_bias": f(rel_bias),
        "attn_lambda": f(attn_lambda).reshape(2, 256), "subln": f(attn_subln_g),
        "ssm_conv_w": f(ssm_conv_w).reshape(2, 4096), "ssm_conv_b": f(ssm_conv_b), "ssm_dt_bias": f(ssm_dt_bias),
        "ssm_A_log": f(ssm_A_log), "ssm_D": f(ssm_D), "ssm_norm_g": f(ssm_norm_g),
        "gdn_conv_w": f(gdn_conv_w).reshape(2, 6144), "gdn_dt_bias": f(gdn_dt_bias), "gdn_A_log": f(gdn_A_log),
        "gdn_norm_g": f(gdn_norm_g), "final_g": f(final_norm_g).reshape(1, D), "lamc": lamc, "gmask": make_gmask(),
    }
    in_maps = []
    poolk_full = np.ascontiguousarray(cache_k, dtype=np.float32).reshape(-1, 512)
    poolv_full = np.ascontiguousarray(cache_v, dtype=np.float32).reshape(-1, 512)
    for c in range(NCORE):
        cm, oh, ohs, cv = make_consts(c)
        m = dict(shared)
        m.update({
            "xp": x_prompt[c], "cp": c_prompt[c:c + 1],
            "xs_own": x_sample[NO * c:NO * (c + 1)], "cs_own": c_sample[NO * c:NO * (c + 1)],
            "poolk": poolk_full, "poolv": poolv_full,
            "pt": np.ascontiguousarray(np.asarray(page_table, dtype=np.int32)[NO * c:NO * (c + 1)].reshape(1, NO * NPAGE)),
            "st_ssm": f(state_ssm[:, NO * c:NO * (c + 1)]), "st_sconv": f(state_ssm_conv[:, NO * c:NO * (c + 1)]),
            "st_gdn": f(state_gdn[:, NO * c:NO * (c + 1)]), "st_gconv": f(state_gdn_conv[:, NO * c:NO * (c + 1)]),
            "cmat": cm, "oh1d": oh, "ohs": ohs, "cvec": cv, "selown": make_selown(c),
        })
        in_maps.append(m)
    res = run_bass_kernel_spmd(nc, in_maps, core_ids=list(range(NCORE)), **_RUN_KW)
    _CACHE["last_exec_time_ns"] = getattr(res, "exec_time_ns", None)
    R = res.results
    cat = lambda name, ax: np.concatenate([np.asarray(R[c][name])[None] if ax is None else np.asarray(R[c][name]) for c in range(NCORE)], axis=0 if ax is None else ax)
    y_prompt = np.stack([R[c]["y_p"] for c in range(NCORE)], 0)
    y_sample = np.concatenate([R[c]["y_s"] for c in range(NCORE)], 0).reshape(-1, 1, D)
    k_prompt = np.stack([R[c]["k_p"] for c in range(NCORE)], 1).reshape(2, NCORE, L, 4, 128)
    v_prompt = np.stack([R[c]["v_p"] for c in range(NCORE)], 1).reshape(2, NCORE, L, 4, 128)
    ssm_prompt = np.stack([R[c]["ssm_p"] for c in range(NCORE)], 1)
    sconv_prompt = np.stack([R[c]["sconv_p"] for c in range(NCORE)], 1)
    gdn_prompt = np.stack([R[c]["gdn_p"] for c in range(NCORE)], 1)
    gconv_prompt = np.stack([R[c]["gconv_p"] for c in range(NCORE)], 1)
    k_sample = np.concatenate([R[c]["k_s"] for c in range(NCORE)], 1).reshape(2, -1, 1, 4, 128)
    v_sample = np.concatenate([R[c]["v_s"] for c in range(NCORE)], 1).reshape(2, -1, 1, 4, 128)
    ssm_sample = np.concatenate([R[c]["ssm_s"] for c in range(NCORE)], 1)
    sconv_sample = np.concatenate([R[c]["sconv_s"] for c in range(NCORE)], 1)
    gdn_sample = np.concatenate([R[c]["gdn_s"] for c in range(NCORE)], 1)
    gconv_sample = np.concatenate([R[c]["gconv_s"] for c in range(NCORE)], 1)
    outs = (y_prompt, y_sample, k_prompt, v_prompt, ssm_prompt, sconv_prompt, gdn_prompt, gconv_prompt,
            k_sample, v_sample, ssm_sample, sconv_sample, gdn_sample, gconv_sample)
    return tuple(np.ascontiguousarray(o, dtype=np.float32) for o in outs)
```

```python
import math
from contextlib import ExitStack
import numpy as np
import concourse.bass as bass
import concourse.mybir as mybir
from concourse.bass_utils import run_bass_kernel_spmd

F32 = mybir.dt.float32
BF16 = mybir.dt.bfloat16
I32 = mybir.dt.int32
ALU = mybir.AluOpType
AF = mybir.ActivationFunctionType
AX = mybir.AxisListType

NCORE = 8
D = 1024
L = 2048
NT = 16
DEPTH = 2
NS = 32
NO = 4
NPAGE = 64
NPOOL = 2560
TPC = 16
OQ, OK_, OV, OXBC, ODT, OQKV, OBETA, OA, OZ = 0, 512, 1024, 1536, 2560, 2568, 4104, 4108, 4112
NIN = 5648
EPS = 1e-6
RSW = 4104
DBG_TILES = None
DBG_STOP = 0
DBG_A2 = None
DBG_VAR = 0


class Sched:
    ENG = ("pe", "dve", "act", "pool", "sp")

    def __init__(self, nc, stack):
        self.nc = nc
        self.stack = stack
        self.engs = {"pe": nc.tensor, "dve": nc.vector, "act": nc.scalar, "pool": nc.gpsimd, "sp": nc.sync}
        self.sem = {}
        self.cnt = {}
        for e in self.ENG:
            self.sem[e] = stack.enter_context(nc.semaphore("prog_" + e))
            self.cnt[e] = 0
        self.seen = {e: {} for e in self.ENG}
        self.last_write = {}
        self.readers = {}
        self.nops = 0
        self.nwaits = 0
        self.misc_map = {}
        self.excl = set()

    DEDICATED = ("wbuf", "wout", "wa", "kbuf", "vbuf", "kvout", "yp", "qbc", "rsin_", "hst", "ssms", "Sall", "gdns")
    NMISC = 12

    def _slot(self, name, q="sp"):
        if not name.startswith(self.DEDICATED):
            key = (name, q == "pool")
            if key not in self.misc_map:
                n = sum(1 for k in self.misc_map if k[1] == key[1])
                self.misc_map[key] = ("miscp%d" % (n % 3)) if key[1] else ("misc%d" % (n % self.NMISC))
            name = self.misc_map[key]
        k = "d_" + name
        if k not in self.sem:
            self.sem[k] = self.stack.enter_context(self.nc.semaphore(k))
            self.cnt[k] = 0
        return k

    def _wait(self, eng, tok):
        if tok is None:
            return
        k, v = tok
        if self.seen[eng].get(k, 0) >= v:
            return
        if eng == "pe" and k == "pe":
            return
        self.engs[eng].wait_ge(self.sem[k], v)
        self.seen[eng][k] = v
        self.nwaits += 1

    def _deps(self, eng, reads, writes):
        for b in reads:
            self._wait(eng, self.last_write.get(b))
            if b in self.excl:
                for k, v in self.readers.get(b, {}).items():
                    if k != eng:
                        self._wait(eng, (k, v))
        for b in writes:
            self._wait(eng, self.last_write.get(b))
            for k, v in self.readers.get(b, {}).items():
                self._wait(eng, (k, v))

    def _mark(self, tok, reads, writes):
        k, v = tok
        for b in reads:
            r = self.readers.setdefault(b, {})
            r[k] = max(r.get(k, 0), v)
        for b in writes:
            self.last_write[b] = tok
            self.readers[b] = {}

    def op(self, eng, fn, reads=(), writes=()):
        self._deps(eng, reads, writes)
        inst = fn(self.engs[eng])
        self.cnt[eng] += 1
        inst.then_inc(self.sem[eng], 1)
        self._mark((eng, self.cnt[eng]), reads, writes)
        self.nops += 1
        return inst

    def dma(self, q, out, in_, reads=(), writes=(), slot=None, fn=None):
        sk = self._slot(slot, q)
        self._deps(q, reads, writes)
        if self.cnt[sk] > 0:
            self._wait(q, (sk, self.cnt[sk]))
        if fn is None:
            inst = self.engs[q].dma_start(out=out, in_=in_)
        else:
            inst = fn(self.engs[q])
        self.cnt[sk] += 16
        inst.then_inc(self.sem[sk], 16)
        self._mark((sk, self.cnt[sk]), reads, writes)
        self.nops += 1
        return inst

    def dma_group(self, q, items, reads=(), writes=(), slot=None):
        sk = self._slot(slot, q)
        self._deps(q, reads, writes)
        if self.cnt[sk] > 0:
            self._wait(q, (sk, self.cnt[sk]))
        for (o, i) in items:
            inst = self.engs[q].dma_start(out=o, in_=i)
            self.cnt[sk] += 16
            inst.then_inc(self.sem[sk], 16)
        self._mark((sk, self.cnt[sk]), reads, writes)
        self.nops += len(items)

    def barrier(self):
        for e in self.ENG:
            for k in list(self.sem.keys()):
                if k != e and self.cnt[k] > 0:
                    self._wait(e, (k, self.cnt[k]))

    def finish(self, eng="sp"):
        for k in list(self.sem.keys()):
            if self.cnt[k] > 0 and k != eng:
                self._wait(eng, (k, self.cnt[k]))


def build_program(npool=NPOOL, do_sample=True, do_branches="ABC", depth=DEPTH):
    nc = bass.Bass("TRN2", target_bir_lowering=False)

    def din(name, shape, dt=F32):
        return nc.dram_tensor(name, list(shape), dt, kind="ExternalInput").ap()

    def dout(name, shape, dt=F32):
        return nc.dram_tensor(name, list(shape), dt, kind="ExternalOutput").ap()

    xp = din("xp", [L, D]); cp = din("cp", [1, D])
    xs_all = din("xs_all", [NS, D]); cs_all = din("cs_all", [NS, D])
    xs_own = din("xs_own", [NO, D]); cs_own = din("cs_own", [NO, D])
    poolk = din("poolk", [2 * npool * 128, 512]); poolv = din("poolv", [2 * npool * 128, 512])
    pt = din("pt", [1, NO * NPAGE], I32)
    st_ssm = din("st_ssm", [2, NO, 8, 64, 128]); st_sconv = din("st_sconv", [2, NO, 3, 1024])
    st_gdn = din("st_gdn", [2, NO, 4, 128, 128]); st_gconv = din("st_gconv", [2, NO, 3, 1536])
    w_ada = din("w_ada", [2, D, 3 * D]); b_ada = din("b_ada", [2, 3 * D])
    w_in = din("w_in", [2, D, NIN]); w_out = din("w_out", [2, 1536, D])
    rel_bias = din("rel_bias", [32, 4]); attn_lambda = din("attn_lambda", [2, 256])
    subln = din("subln", [2, 128])
    ssm_conv_w = din("ssm_conv_w", [2, 4 * 1024]); ssm_conv_b = din("ssm_conv_b", [2, 1024])
    ssm_dt_bias = din("ssm_dt_bias", [2, 8]); ssm_A_log = din("ssm_A_log", [2, 8]); ssm_D = din("ssm_D", [2, 8])
    ssm_norm_g = din("ssm_norm_g", [2, 512])
    gdn_conv_w = din("gdn_conv_w", [2, 4 * 1536]); gdn_dt_bias = din("gdn_dt_bias", [2, 4])
    gdn_A_log = din("gdn_A_log", [2, 4]); gdn_norm_g = din("gdn_norm_g", [2, 128])
    final_g = din("final_g", [1, D])
    cmat = din("cmat", [128, 5, 128])
    oh1d = din("oh1d", [33, 512])
    ohs = din("ohs", [32, 128])
    cvec = din("cvec", [128, 16])
    lamc = din("lamc", [1, 4])
    gmask_d = din("gmask", [128, 4, 128])
    selown_d = din("selown", [NO, NS])

    y_p = dout("y_p", [L, D]); y_s = dout("y_s", [NO, D])
    k_p = dout("k_p", [2, L, 512]); v_p = dout("v_p", [2, L, 512])
    ssm_p = dout("ssm_p", [2, 8, 64, 128]); sconv_p = dout("sconv_p", [2, 3, 1024])
    gdn_p = dout("gdn_p", [2, 4, 128, 128]); gconv_p = dout("gconv_p", [2, 3, 1536])
    k_s = dout("k_s", [2, NO, 512]); v_s = dout("v_s", [2, NO, 512])
    ssm_s = dout("ssm_s", [2, NO, 8, 64, 128]); sconv_s = dout("sconv_s", [2, NO, 3, 1024])
    gdn_s = dout("gdn_s", [2, NO, 4, 128, 128]); gconv_s = dout("gconv_s", [2, NO, 3, 1536])

    bvec_d = nc.dram_tensor("bvec_d", [4, 512], F32)
    rs_out = [nc.dram_tensor("rs_loc%d" % l, [NO, RSW], F32) for l in range(2)]
    ag_in = nc.dram_tensor("ag_in", [NS, D], F32)
    ag_out = nc.dram_tensor("ag_out", [NS, D], F32)
    q_d = [nc.dram_tensor("q_d%d" % l, [NO, 512], F32) for l in range(2)]
    gate_d = nc.dram_tensor("gate_d", [NO, D], F32)

    with ExitStack() as st:
        S = Sched(nc, st)
        cc_sem = st.enter_context(nc.semaphore("cc_sem"))
        cc_cnt = [0]

        _uid = [0]

        def T(stack, name, shape, dt=F32):
            _uid[0] += 1
            return stack.enter_context(nc.sbuf_tensor("%s_%d" % (name, _uid[0]), list(shape), dt))

        def V(fn, r=(), w=()): return S.op("dve", fn, r, w)
        def A(fn, r=(), w=()): return S.op("act", fn, r, w)
        def G(fn, r=(), w=()): return S.op("pool", fn, r, w)
        def PE(fn, r=(), w=()): return S.op("pe", fn, r, w)

        bank = [st.enter_context(nc.psum_tensor("bk%d" % i, [128, 512], F32)) for i in range(7)]
        trbank = st.enter_context(nc.psum_tensor("trbank", [128, 512], F32))
        _trv = trbank[:, :].bitcast(BF16)
        trb = [_trv[:, 0:512], _trv[:, 512:1024]]
        BK = ["bk%d" % i for i in range(7)]
        TRB = ["trb", "trb"]
        S.excl = set(BK) | {"trb"}

        x_sb = T(st, "x_sb", [128, NT, D])
        hT = T(st, "hT", [128, 8, L], BF16)
        wbuf = T(st, "wbuf", [128, 8, 2056], BF16)
        wout = T(st, "wout", [128, 4, D], BF16)
        cm = T(st, "cm", [128, 5, 128])
        identb = T(st, "identb", [128, 128], BF16)
        onesb = T(st, "onesb", [128, 128], BF16)
        onesf = T(st, "onesf", [128, 128])
        BT = T(st, "BT", [128, 4, 256])
        c31 = T(st, "c31", [128, 4])
        cv = T(st, "cv", [128, 16])
        lamc_sb = T(st, "lamc_sb", [128, 4])
        xo = T(st, "xo", [NO, D])
        cT_p = T(st, "cT_p", [128, 8, 1], BF16)
        cT_o = T(st, "cT_o", [128, 8, NO], BF16)
        cT_a = T(st, "cT_a", [128, 8, NS], BF16)
        sel4 = T(st, "sel4", [NO, NO, 128])
        eye4 = T(st, "eye4", [128, NO, NO])
        idx = T(st, "idx", [128, 2, NO * NPAGE], I32)
        bias_sm = T(st, "bias_sm", [128, NPAGE, 8])
        b0row = T(st, "b0row", [NO, 4])
        hTo = T(st, "hTo", [128, 8, NO], BF16)
        hTa = T(st, "hTa", [128, 8, NS], BF16)
        scT = T(st, "scT", [128, 16])
        gate_bc = T(st, "gate_bc", [128, D])
        lamv = T(st, "lamv", [128, 4])
        sgl = T(st, "sgl", [128, 128])
        ident = cm[:, 0, :]
        Umat = cm[:, 1, :]
        negmask = cm[:, 2, :]
        strictL = cm[:, 3, :]
        Jmat = cm[:, 4, :]

        def bcast_row(dst, src_row_ap, n, key, q="sp"):
            P = dst.shape[0]
            S.dma(q, dst, src_row_ap.to_broadcast([P, n]), writes=[key], slot=key)

        def transpose_f32(dst_psum, src, rk, wk, rows=None):
            PE(lambda e: e.transpose(out=dst_psum, in_=src, identity=ident if rows is None else cm[0:rows, 0, 0:rows]),
               list(rk) + ["cm"], wk)

        def rstd_from_ss(dst, ss, n, keys_r, key_w, shape_eng="act"):
            A(lambda e: e.activation(out=dst, in_=ss, func=AF.Sqrt, scale=1.0 / n, bias=epsc[0:dst.shape[0], 0:1]), list(keys_r) + ["epsc"], [key_w])
            V(lambda e: e.reciprocal(out=dst, in_=dst), [key_w], [key_w])

        def softplus(dst, src, kr, kw):
            A(lambda e: e.activation(out=dst, in_=src, func=AF.Exp), kr, [kw])
            A(lambda e: e.activation(out=dst, in_=dst, func=AF.Ln, bias=onec[0:dst.shape[0], 0:1]), [kw, "epsc"], [kw])

        def collective(kind, op, in_t, out_t, rkeys, wkeys):
            for k in rkeys:
                S._wait("pool", S.last_write.get(k))
            for k in wkeys:
                S._wait("pool", S.last_write.get(k))
                for kk, vv in S.readers.get(k, {}).items():
                    S._wait("pool", (kk, vv))
            cc_cnt[0] += 1
            nc.gpsimd.collective_compute(kind, op, replica_groups=[list(range(NCORE))],
                                         ins=[in_t.ap().opt()], outs=[out_t.ap().opt()]).then_inc(cc_sem, 1)
            nc.gpsimd.wait_ge(cc_sem, cc_cnt[0])
            G(lambda e: e.memset(ccj[:], 0.0), [], ["ccj"] + list(wkeys))

        print("sbuf bytes remaining after persistent:", nc.sbuf_bytes_remaining, flush=True)
        epsc = T(st, "epsc", [128, 3])
        onec = epsc[:, 1:2]
        ccj = T(st, "ccj", [1, 2])
        V(lambda e: e.memset(epsc[:, 0:1], EPS), [], ["epsc"])
        V(lambda e: e.memset(epsc[:, 1:2], 1.0), ["epsc"], ["epsc"])
        V(lambda e: e.memset(epsc[:, 2:3], 128.0 * EPS), ["epsc"], ["epsc"])
        S.dma("sp", cm[:], cmat, writes=["cm"], slot="cm")
        S.dma("sp", cv[:], cvec, writes=["cv"], slot="cv")
        bcast_row(lamc_sb[:], lamc[0:1, :], 4, "lamc_sb")
        bcast_row(c31[:], rel_bias[31:32, :], 4, "c31")
        bcast_row(b0row[:], rel_bias[0:1, :], 4, "b0row")
        V(lambda e: e.tensor_copy(out=identb[:], in_=ident), ["cm"], ["identb"])
        V(lambda e: e.memset(onesb[:], 1.0), [], ["onesb"])
        V(lambda e: e.memset(onesf[:], 1.0), [], ["onesf"])
        V(lambda e: e.tensor_copy(out=sel4[:], in_=cm[0:NO, 0, 0:NO].unsqueeze(2).to_broadcast([NO, NO, 128])), ["cm"], ["sel4"])
        with ExitStack() as cs:
            tb = T(cs, "tb_tab", [33, 4])
            oh = T(cs, "oh_sb", [33, 512])
            bv = T(cs, "bv_sb", [4, 512])
            hk = T(cs, "hk_sb", [128, 4, 384])
            ohs_sb = T(cs, "ohs_sb", [32, 128])
            i4 = T(cs, "i4", [NO, NO * NO])
            V(lambda e: e.memset(i4[:], 0.0), [], ["i4"])
            for j in range(NO):
                V(lambda e: e.tensor_copy(out=i4[:, j * NO + j:j * NO + j + 1], in_=onesf[0:NO, 0:1]), ["i4", "onesf"], ["i4"])
            PE(lambda e: e.matmul(bank[0][:, 0:16], lhsT=sel4[:, 0, :], rhs=i4[:], start=True, stop=True), ["sel4", "i4"], [BK[0]])
            V(lambda e: e.tensor_copy(out=eye4[:].rearrange("p a b -> p (a b)"), in_=bank[0][:, 0:16]), [BK[0]], ["eye4"])
            V(lambda e: e.memset(tb[:], -30000.0), [], ["tb"])
            S.dma("sp", tb[0:32, :], rel_bias, reads=["tb"], writes=["tb"], slot="tb")
            S.dma("sp", oh[:], oh1d, writes=["oh"], slot="oh")
            S.dma("sp", ohs_sb[:], ohs, writes=["ohs"], slot="ohs")
            PE(lambda e: e.matmul(bank[1][0:4, :], lhsT=tb[:], rhs=oh[:], start=True, stop=True), ["tb", "oh"], [BK[1]])
            V(lambda e: e.tensor_copy(out=bv[:], in_=bank[1][0:4, :]), [BK[1]], ["bv"])
            S.dma("sp", bvec_d.ap(), bv[:], reads=["bv"], writes=["bvec_d"], slot="bvd")
            hsrc = bass.AP(tensor=bvec_d.ap().tensor, offset=0, ap=[[1, 128], [512, 4], [1, 384]])
            S.dma("sp", hk[:], hsrc, reads=["bvec_d"], writes=["hk"], slot="hk")
            for h in range(4):
                PE(lambda e: e.matmul(bank[2][:, 0:384], lhsT=Jmat, rhs=hk[:, h, :], start=True, stop=True), ["cm", "hk"], [BK[2]])
                V(lambda e: e.tensor_copy(out=BT[:, h, :], in_=bank[2][:, 0:256]), [BK[2]], ["BT"])
            for m_ in range(2):
                V(lambda e: e.tensor_copy(out=bias_sm[:].rearrange("p t (h m) -> p t h m", m=2)[:, :, :, m_], in_=c31[:].unsqueeze(1).to_broadcast([128, NPAGE, 4])), ["c31", "bias_sm"], ["bias_sm"])
            PE(lambda e: e.matmul(bank[3][:, 0:4], lhsT=ohs_sb[:, :], rhs=tb[0:32, :], start=True, stop=True), ["ohs", "tb"], [BK[3]])
            V(lambda e: e.tensor_copy(out=bias_sm[:, NPAGE - 1, :].rearrange("p (h m) -> p h m", m=2), in_=bank[3][:, 0:4].unsqueeze(2).to_broadcast([128, 4, 2])),
              [BK[3], "bias_sm"], ["bias_sm"])
            S.barrier()
        with ExitStack() as cs:
            if do_sample:
                ptb = T(cs, "ptb", [128, NO * NPAGE], I32)
                ptf = T(cs, "ptf", [128, NO * NPAGE])
                ptg = T(cs, "ptg", [128, NO * NPAGE])
                S.dma("sp", ptb[:], pt[0:1, :].to_broadcast([128, NO * NPAGE]), writes=["ptb"], slot="ptb")
                V(lambda e: e.tensor_copy(out=ptf[:], in_=ptb[:]), ["ptb"], ["ptf"])
                for l in range(2):
                    V(lambda e: e.tensor_scalar(out=ptg[:], in0=ptf[:], scalar1=128.0, scalar2=cv[:, l:l + 1], op0=ALU.mult, op1=ALU.add), ["ptf", "cv", "ptg"], ["ptg"])
                    V(lambda e: e.tensor_copy(out=idx[:, l, :], in_=ptg[:]), ["ptg"], ["idx"])
            S.barrier()
        with ExitStack() as cs:
            craw = T(cs, "craw", [NS, D])
            csil = T(cs, "csil", [NS, D], BF16)
            for (src, n, dst, nm) in ((cp, 1, cT_p, "cT_p"), (cs_own, NO, cT_o, "cT_o"), (cs_all, NS, cT_a, "cT_a")):
                S.dma("sp", craw[0:n, :], src, writes=["craw"], slot="craw")
                A(lambda e: e.activation(out=csil[0:n, :], in_=craw[0:n, :], func=AF.Silu), ["craw"], ["csil"])
                for k in range(8):
                    PE(lambda e: e.transpose(out=trb[0][:, k * NS:k * NS + n], in_=csil[0:n, k * 128:(k + 1) * 128], identity=identb[0:n, 0:n]),
                       ["csil", "identb"], [TRB[0]])
                V(lambda e: e.tensor_copy(out=dst[:], in_=trb[0][:, 0:8 * NS].rearrange("p (k s) -> p k s", s=NS)[:, :, 0:n]), [TRB[0]], [nm])
            S.barrier()

        for i in range(NT):
            S.dma("sp", x_sb[:, i, :], xp[i * 128:(i + 1) * 128, :], writes=["x%d" % i], slot="x%d" % i)
        S.dma("sp", xo[:], xs_own, writes=["xo"], slot="xo")

        def load_w(l, segs, key="wbuf"):
            items = []
            o = 0
            for (c0, n) in segs:
                for k in range(8):
                    items.append((wbuf[:, k, o:o + n], w_in[l, k * 128:(k + 1) * 128, c0:c0 + n]))
                o += n
            S.dma_group("pool", items, writes=[key], slot="wbuf")

        def load_wout(l, r0):
            items = [(wout[:, k, :], w_out[l, r0 + k * 128:r0 + (k + 1) * 128, :]) for k in range(4)]
            S.dma_group("pool", items, writes=["wout"], slot="wout")

        def proj_tok(dst_psum, dkey, lhs_fn, lkeys, wcol0, ncols, M=128):
            def f(e):
                for k in range(8):
                    i = e.matmul(dst_psum, lhsT=lhs_fn(k), rhs=wbuf[:, k, wcol0:wcol0 + ncols], start=(k == 0), stop=(k == 7))
                return i
            PE(f, list(lkeys) + ["wbuf"], [dkey])

        def proj_feat(dst_psum, dkey, wcol0, rhs_fn, rkeys, mcols=128):
            def f(e):
                for k in range(8):
                    i = e.matmul(dst_psum, lhsT=wbuf[:, k, wcol0:wcol0 + mcols], rhs=rhs_fn(k), start=(k == 0), stop=(k == 7))
                return i
            PE(f, list(rkeys) + ["wbuf"], [dkey])

        def outproj_update_prompt(i, oT, okey, banks=(5, 6)):
            for half in range(2):
                bk = banks[half]
                def f(e):
                    for k in range(4):
                        ins = e.matmul(bank[bk][:, :], lhsT=oT[:, k, :], rhs=wout[:, k, half * 512:(half + 1) * 512], start=(k == 0), stop=(k == 3))
                    return ins
                PE(f, [okey, "wout"], [BK[bk]])
                V(lambda e: e.tensor_tensor(out=optmp[:, :], in0=bank[bk][:, :], in1=gate_bc[:, half * 512:(half + 1) * 512], op=ALU.mult),
                  [BK[bk], "gate_bc"], ["optmp"])
                G(lambda e: e.tensor_tensor(out=x_sb[:, i, half * 512:(half + 1) * 512], in0=x_sb[:, i, half * 512:(half + 1) * 512], in1=optmp[:, :], op=ALU.add),
                  ["optmp", "x%d" % i], ["x%d" % i])

        def o_to_T(o_bf, okey, oT, oTkey, M=128):
            def f(e):
                for k in range(4):
                    ins = e.transpose(out=trb[1][:, k * 128:k * 128 + M], in_=o_bf[:, k * 128:(k + 1) * 128], identity=identb[0:M, 0:M])
                return ins
            PE(f, [okey, "identb"], [TRB[1]])
            A(lambda e: e.activation(out=oT[:, :, 0:M], in_=trb[1][:, :].rearrange("p (k t) -> p k t", t=128)[:, :, 0:M], func=AF.Copy), [TRB[1]], [oTkey])

        optmp = T(st, "optmp", [128, 512])
        oTt = T(st, "oTt", [128, 4, 128], BF16)
        oTs = T(st, "oTs", [128, 12, NO], BF16)

        def outproj_sample(l, gstack):
            gate_o = T(gstack, "gate_o", [NO, D])
            S.dma("sp", gate_o[:], gate_d.ap(), reads=["gate_d"], writes=["gate_o"], slot="gate_o")
            for br in range(3):
                load_wout(l, br * 512)
                for half in range(2):
                    def f(e):
                        for k in range(4):
                            ins = e.matmul(bank[5 + half][0:NO, :], lhsT=oTs[:, br * 4 + k, :], rhs=wout[:, k, half * 512:(half + 1) * 512], start=(k == 0), stop=(k == 3))
                        return ins
                    PE(f, ["oTs", "wout"], [BK[5 + half]])
                    V(lambda e: e.tensor_tensor(out=optmp[0:NO, :], in0=bank[5 + half][0:NO, :], in1=gate_o[:, half * 512:(half + 1) * 512], op=ALU.mult),
                      [BK[5 + half], "gate_o"], ["optmp"])
                    V(lambda e: e.tensor_tensor(out=xo[:, half * 512:(half + 1) * 512], in0=xo[:, half * 512:(half + 1) * 512], in1=optmp[0:NO, :], op=ALU.add),
                      ["optmp", "xo"], ["xo"])

        for l in range(depth):
            lam_init = 0.8 - 0.6 * math.exp(-0.3 * l)
            with ExitStack() as ps:
                wa = T(ps, "wa", [128, 8, 512], BF16)
                brow = T(ps, "brow", [NS, 512])
                mp = T(ps, "mp", [1, 512])
                mo_ss = T(ps, "mo_ss", [NO, 2 * D])
                gst = T(ps, "gst", [NO, 512])
                moda = T(ps, "moda", [NS, 2 * D])
                lv = T(ps, "lv", [128, 256])
                lvp = T(ps, "lvp", [128, 128])
                l2 = T(ps, "l2", [128, 2])
                sg = T(ps, "sg", [128, 128])
                ssq = T(ps, "ssq", [128, 2])
                bcast_row(lv[:], attn_lambda[l:l + 1, :], 256, "lv")
                V(lambda e: e.tensor_tensor(out=lvp[:].rearrange("p (a d) -> p a d", a=2), in0=lv[:].rearrange("p (a b d) -> p a b d", a=2, b=2)[:, :, 0, :],
                                            in1=lv[:].rearrange("p (a b d) -> p a b d", a=2, b=2)[:, :, 1, :], op=ALU.mult), ["lv"], ["lvp"])
                V(lambda e: e.tensor_reduce(out=l2[:], in_=lvp[:].rearrange("p (a d) -> p a d", a=2), axis=AX.X, op=ALU.add), ["lvp"], ["l2"])
                A(lambda e: e.activation(out=l2[:], in_=l2[:], func=AF.Exp), ["l2"], ["l2"])
                V(lambda e: e.tensor_tensor(out=lamv[:, 0:1], in0=l2[:, 0:1], in1=l2[:, 1:2], op=ALU.subtract), ["l2"], ["lamv"])
                V(lambda e: e.tensor_scalar(out=lamv[:, 0:1], in0=lamv[:, 0:1], scalar1=float(lam_init), scalar2=None, op0=ALU.add), ["lamv"], ["lamv"])
                V(lambda e: e.tensor_scalar(out=lamv[:, 1:2], in0=lamv[:, 0:1], scalar1=-1.0, scalar2=None, op0=ALU.mult), ["lamv"], ["lamv"])
                bcast_row(sg[:], subln[l:l + 1, :], 128, "sg")
                V(lambda e: e.tensor_scalar(out=sgl[:], in0=sg[:], scalar1=float(1.0 - lam_init), scalar2=None, op0=ALU.mult), ["sg"], ["sgl"])
                for cch in range(6):
                    S.dma_group("pool", [(wa[:, k, :], w_ada[l, k * 128:(k + 1) * 128, cch * 512:(cch + 1) * 512]) for k in range(8)], writes=["wa"], slot="wa")
                    bcast_row(brow[:], b_ada[l:l + 1, cch * 512:(cch + 1) * 512], 512, "brow")
                    def f(e):
                        for k in range(8):
                            ins = e.matmul(bank[0][0:1, :], lhsT=cT_p[:, k, :], rhs=wa[:, k, :], start=(k == 0), stop=(k == 7))
                        return ins
                    PE(f, ["wa", "cT_p"], [BK[0]])
                    V(lambda e: e.tensor_tensor(out=mp[:], in0=bank[0][0:1, :], in1=brow[0:1, :], op=ALU.add), [BK[0], "brow"], ["mp"])
                    if cch < 4:
                        def f(e):
                            for j in range(4):
                                ins = e.transpose(out=bank[3][:, cch * 4 + j:cch * 4 + j + 1], in_=mp[0:1, j * 128:(j + 1) * 128], identity=cm[0:1, 0, 0:1])
                            return ins
                        PE(f, ["mp", "cm"], [BK[3]])
                    else:
                        PE(lambda e: e.matmul(bank[4][:, :], lhsT=onesf[0:1, :], rhs=mp[0:1, :], start=True, stop=True), ["onesf", "mp"], [BK[4]])
                        V(lambda e: e.tensor_copy(out=gate_bc[:, (cch - 4) * 512:(cch - 3) * 512], in_=bank[4][:, :]), [BK[4]], ["gate_bc"])
                    def f(e):
                        for k in range(8):
                            ins = e.matmul(bank[1][0:NO, :], lhsT=cT_o[:, k, :], rhs=wa[:, k, :], start=(k == 0), stop=(k == 7))
                        return ins
                    PE(f, ["wa", "cT_o"], [BK[1]])
                    if cch < 4:
                        V(lambda e: e.tensor_tensor(out=mo_ss[:, cch * 512:(cch + 1) * 512], in0=bank[1][0:NO, :], in1=brow[0:NO, :], op=ALU.add), [BK[1], "brow"], ["mo_ss"])
                    else:
                        V(lambda e: e.tensor_tensor(out=gst[:], in0=bank[1][0:NO, :], in1=brow[0:NO, :], op=ALU.add), [BK[1], "brow"], ["gst"])
                        S.dma("sp", gate_d[:, (cch - 4) * 512:(cch - 3) * 512], gst[:], reads=["gst"], writes=["gate_d"], slot="gate_d")
                V(lambda e: e.tensor_copy(out=scT[:], in_=bank[3][:, 0:16]), [BK[3]], ["scT"])
                V(lambda e: e.tensor_scalar(out=scT[:, 8:16], in0=scT[:, 8:16], scalar1=1.0, scalar2=None, op0=ALU.add), ["scT"], ["scT"])
                V(lambda e: e.tensor_scalar(out=mo_ss[:, D:2 * D], in0=mo_ss[:, D:2 * D], scalar1=1.0, scalar2=None, op0=ALU.add), ["mo_ss"], ["mo_ss"])
                S.barrier()
                with ExitStack() as ps2:
                    xall = T(ps2, "xall", [NS, D])
                    hs = T(ps2, "hs_s", [NS, D])
                    hsb = T(ps2, "hsb", [NS, D], BF16)
                    if do_sample:
                        for (xs, xk, n, md, mk, dst, dk) in ((xo, "xo", NO, mo_ss, "mo_ss", hTo, "hTo"),):
                            A(lambda e: e.activation(out=hs[0:n, :], in_=xs[:], func=AF.Square, accum_out=ssq[0:n, 0:1]), [xk], ["hs", "ssq"])
                            rstd_from_ss(ssq[0:n, 1:2], ssq[0:n, 0:1], D, ["ssq"], "ssq")
                            V(lambda e: e.tensor_scalar(out=hs[0:n, :], in0=xs[:], scalar1=ssq[0:n, 1:2], scalar2=None, op0=ALU.mult), [xk, "ssq"], ["hs"])
                            V(lambda e: e.tensor_tensor(out=hs[0:n, :], in0=hs[0:n, :], in1=md[:, D:2 * D], op=ALU.mult), ["hs", mk], ["hs"])
                            V(lambda e: e.tensor_tensor(out=hsb[0:n, :], in0=hs[0:n, :], in1=md[:, 0:D], op=ALU.add), ["hs", mk], ["hsb"])
                            def f(e):
                                for k in range(8):
                                    ins = e.transpose(out=trb[0][:, k * NS:k * NS + n], in_=hsb[0:n, k * 128:(k + 1) * 128], identity=identb[0:n, 0:n])
                                return ins
                            PE(f, ["hsb", "identb"], [TRB[0]])
                            V(lambda e: e.tensor_copy(out=dst[:], in_=trb[0][:, 0:8 * NS].rearrange("p (k s) -> p k s", s=NS)[:, :, 0:n]), [TRB[0]], [dk])
                    S.barrier()
            with ExitStack() as ps:
                hn = T(ps, "hn", [128, D], BF16)
                hs = T(ps, "hs", [128, D])
                ssq = T(ps, "ssq2", [128, 2])
                for i in range(NT):
                    A(lambda e: e.activation(out=hs[:], in_=x_sb[:, i, :], func=AF.Square, accum_out=ssq[:, 0:1]), ["x%d" % i], ["hs", "ssq"])
                    rstd_from_ss(ssq[:, 1:2], ssq[:, 0:1], D, ["ssq"], "ssq")
                    V(lambda e: e.tensor_scalar(out=hn[:], in0=x_sb[:, i, :], scalar1=ssq[:, 1:2], scalar2=None, op0=ALU.mult), ["x%d" % i, "ssq"], ["hn"])
                    for half in range(2):
                        def f(e):
                            for k in range(4):
                                kk = half * 4 + k
                                ins = e.transpose(out=trb[half][:, k * 128:(k + 1) * 128], in_=hn[:, kk * 128:(kk + 1) * 128], identity=identb[:])
                            return ins
                        PE(f, ["hn", "identb"], [TRB[half]])
                        for k in range(4):
                            kk = half * 4 + k
                            A(lambda e: e.activation(out=hT[:, kk, i * 128:(i + 1) * 128], in_=trb[half][:, k * 128:(k + 1) * 128], func=AF.Identity,
                                                     scale=scT[:, 8 + kk:9 + kk], bias=scT[:, kk:kk + 1]), [TRB[half], "scT"], ["hT%d" % i])
                S.barrier()

            def proj_rows(dst, dkey, segs):
                for (w0, n, d0) in segs:
                    proj_tok(bank[0][0:NO, 0:n], BK[0], lambda k: hTo[:, k, :], ["hTo"], w0, n)
                    V(lambda e: e.tensor_copy(out=dst[:, d0:d0 + n], in_=bank[0][0:NO, 0:n]), [BK[0]], [dkey])

            def gated_norm_rows(ss, o3, okey, ng, width, gvec, gkey, zs_ap, zkey, br):
                tq = T(ss, "gn_tq%d" % br, [NO, 512])
                t4 = T(ss, "gn_t4%d" % br, [NO, 8])
                ob4 = T(ss, "gn_ob%d" % br, [NO, 512], BF16)
                tq3 = tq[:, :].rearrange("s (g d) -> s g d", d=width)
                V(lambda e: e.tensor_tensor(out=tq3, in0=o3, in1=o3, op=ALU.mult), [okey], ["gn_tq"])
                V(lambda e: e.tensor_reduce(out=t4[:, 0:ng], in_=tq3, axis=AX.X, op=ALU.add), ["gn_tq"], ["gn_t4"])
                rstd_from_ss(t4[:, 0:ng], t4[:, 0:ng], width, ["gn_t4"], "gn_t4")
                V(lambda e: e.tensor_tensor(out=tq3, in0=o3, in1=t4[:, 0:ng].unsqueeze(2).to_broadcast([NO, ng, width]), op=ALU.mult), [okey, "gn_t4"], ["gn_tq"])
                V(lambda e: e.tensor_tensor(out=tq3, in0=tq3, in1=gvec, op=ALU.mult), ["gn_tq", gkey], ["gn_tq"])
                if zs_ap is not None:
                    V(lambda e: e.tensor_tensor(out=ob4[:], in0=tq[:], in1=zs_ap, op=ALU.mult), ["gn_tq", zkey], ["gn_ob"])
                else:
                    V(lambda e: e.tensor_copy(out=ob4[:], in_=tq[:]), ["gn_tq"], ["gn_ob"])
                def f(e):
                    for k in range(4):
                        ins = e.transpose(out=trb[0][:, k * NO:(k + 1) * NO], in_=ob4[:, k * 128:(k + 1) * 128], identity=identb[0:NO, 0:NO])
                    return ins
                PE(f, ["gn_ob", "identb"], [TRB[0]])
                V(lambda e: e.tensor_copy(out=oTs[:, br * 4:(br + 1) * 4, :].rearrange("p k s -> p (k s)"), in_=trb[0][:, 0:4 * NO]), [TRB[0]], ["oTs"])

            def sample_attn():
              with ExitStack() as stack2:
                pj = T(stack2, "pjA", [NO, 2048])
                with ExitStack() as ss:
                    qall = T(ss, "qall", [NO, 512])
                    proj_rows(pj, "pjA", [(0, 512, 0), (512, 512, 512), (1024, 512, 1024), (1536, 512, 1536)])
                    A(lambda e: e.activation(out=qall[:], in_=pj[:, 0:512], func=AF.Copy, scale=0.125), ["pjA"], ["qall"])
                    S.dma("sp", q_d[l].ap(), qall[:], reads=["qall"], writes=["q_d"], slot="q_d")
                    S.dma("sp", k_s[l], pj[:, 512:1024], reads=["pjA"], slot="ks")
                    S.dma("sp", v_s[l], pj[:, 1024:1536], reads=["pjA"], slot="vs")
                    NSL = 4
                    kbuf = [T(ss, "kbuf%d" % i, [128, 512], BF16) for i in range(NSL)]
                    vbuf = [T(ss, "vbuf%d" % i, [128, 512], BF16) for i in range(2 * NSL)]
                    qbc = [T(ss, "qbc%d" % i, [128, 512]) for i in range(2)]
                    prod = [T(ss, "prod%d" % i, [128, 512]) for i in range(2)]
                    lg = [T(ss, "lg%d" % i, [128, 8, 8]) for i in range(2)]
                    pexp = [T(ss, "pexp%d" % i, [128, 8, 8], BF16) for i in range(2)]
                    osb = [T(ss, "osb%d" % i, [8, 512]) for i in range(2)]
                    ssb = [T(ss, "ssb%d" % i, [1, 8]) for i in range(2)]
                    gi = 0
                    for sv in range(NO * 8):
                        s, g8 = sv // 8, sv % 8
                        b2 = s % 2
                        if g8 == 0:
                            S.dma("sp", qbc[b2][:], q_d[l][s:s + 1, :].to_broadcast([128, 512]), reads=["q_d"], writes=["qbc%d" % b2], slot="qbc%d" % b2)
                        vslots = []
                        for t in range(8):
                            sl = gi % NSL
                            vsl = gi % (2 * NSL)
                            gi += 1
                            vslots.append(vsl)
                            col = sv * 8 + t
                            S.dma("pool", None, None, writes=["kbuf%d" % sl], slot="kbuf%d" % sl, reads=["idx"],
                                  fn=lambda e: e.indirect_dma_start(out=kbuf[sl][:], out_offset=None, in_=poolk[:, :],
                                                                    in_offset=bass.IndirectOffsetOnAxis(ap=idx[:, l, col:col + 1], axis=0)))
                            S.dma("pool", None, None, writes=["vbuf%d" % vsl], slot="vbuf%d" % vsl, reads=["idx"],
                                  fn=lambda e: e.indirect_dma_start(out=vbuf[vsl][:], out_offset=None, in_=poolv[:, :],
                                                                    in_offset=bass.IndirectOffsetOnAxis(ap=idx[:, l, col:col + 1], axis=0)))
                            pb = t % 2
                            V(lambda e: e.tensor_tensor(out=prod[pb][:], in0=kbuf[sl][:], in1=qbc[b2][:], op=ALU.mult), ["kbuf%d" % sl, "qbc%d" % b2], ["prod%d" % pb])
                            V(lambda e: e.tensor_reduce(out=lg[b2][:, t, :], in_=prod[pb][:].rearrange("p (a d) -> p a d", d=64), axis=AX.X, op=ALU.add),
                              ["prod%d" % pb], ["lg%d" % b2])
                        V(lambda e: e.tensor_tensor(out=lg[b2][:], in0=lg[b2][:], in1=bias_sm[:, g8 * 8:(g8 + 1) * 8, :], op=ALU.add), ["lg%d" % b2, "bias_sm"], ["lg%d" % b2])
                        A(lambda e: e.activation(out=pexp[b2][:], in_=lg[b2][:], func=AF.Exp), ["lg%d" % b2], ["pexp%d" % b2])
                        def f(e):
                            for t in range(8):
                                ins = e.matmul(bank[4 + b2][0:8, :], lhsT=pexp[b2][:, t, :], rhs=vbuf[vslots[t]][:], start=(t == 0 and g8 == 0), stop=(t == 7 and g8 == 7))
                            return ins
                        PE(f, ["pexp%d" % b2] + ["vbuf%d" % v_ for v_ in vslots], [BK[4 + b2]])
                        PE(lambda e: e.matmul(bank[6][0:1, g8 * 64:(g8 + 1) * 64], lhsT=onesb[:, 0:1], rhs=pexp[b2][:].rearrange("p t a -> p (t a)"), start=True, stop=True),
                           ["onesb", "pexp%d" % b2], [BK[6]])
                        if g8 == 7:
                            A(lambda e: e.activation(out=osb[b2][:], in_=bank[4 + b2][0:8, :], func=AF.Copy), [BK[4 + b2]], ["osb%d" % b2])
                            V(lambda e: e.tensor_reduce(out=ssb[b2][:], in_=bank[6][0:1, :].rearrange("p (t a) -> p a t", a=8), axis=AX.X, op=ALU.add),
                              [BK[6]], ["ssb%d" % b2])
                            S.dma("sp", rs_out[l][s:s + 1, 0:4096].rearrange("o (a c) -> (o a) c", a=8), osb[b2][:], reads=["osb%d" % b2], writes=["rs_out"], slot="rsin_o%d" % b2)
                            S.dma("sp", rs_out[l][s:s + 1, 4096:4104], ssb[b2][:], reads=["ssb%d" % b2], writes=["rs_out"], slot="rsin_s%d" % b2)
                    S.barrier()
                    ss.close()
                    ss = stack2
                    Op = T(ss, "Op", [NO, 8, 128])
                    sm = T(ss, "sm", [NO, 8])
                    pn = T(ss, "pn", [NO, 8])
                    tq = T(ss, "tqA", [NO, 512])
                    oa = T(ss, "oa", [NO, 4, 128])
                    zs = T(ss, "zsA", [NO, 512])
                    Op4 = Op[:, :, :].rearrange("s (h m) d -> s h m d", m=2)
                    S.dma_group("sp", [(Op4[:, :, m, :], bass.AP(tensor=rs_out[l].ap().tensor, offset=m * 512, ap=[[RSW, NO], [1152, 4], [1, 128]])) for m in range(2)],
                                reads=["rs_out"], writes=["Op"], slot="Op")
                    S.dma("sp", sm[:], rs_out[l][:, 4096:4104], reads=["rs_out"], writes=["sm"], slot="sm")
                    V(lambda e: e.tensor_tensor(out=tq[:], in0=pj[:, 0:512], in1=pj[:, 512:1024], op=ALU.mult), ["pjA"], ["tqA"])
                    V(lambda e: e.tensor_reduce(out=pn[:], in_=tq[:].rearrange("s (a d) -> s a d", d=64), axis=AX.X, op=ALU.add), ["tqA"], ["pn"])
                    pn3 = pn[:, :].rearrange("s (h m) -> s h m", m=2)
                    sm3 = sm[:, :].rearrange("s (h m) -> s h m", m=2)
                    V(lambda e: e.scalar_tensor_tensor(out=pn3, in0=pn3, scalar=0.125, in1=b0row[:].unsqueeze(2).to_broadcast([NO, 4, 2]), op0=ALU.mult, op1=ALU.add), ["pn", "b0row"], ["pn"])
                    A(lambda e: e.activation(out=pn[:], in_=pn[:], func=AF.Exp), ["pn"], ["pn"])
                    V(lambda e: e.tensor_tensor(out=sm[:], in0=sm[:], in1=pn[:], op=ALU.add), ["sm", "pn"], ["sm"])
                    V(lambda e: e.reciprocal(out=sm[:], in_=sm[:]), ["sm"], ["sm"])
                    V(lambda e: e.tensor_scalar(out=sm3[:, :, 1], in0=sm3[:, :, 1], scalar1=lamv[0:NO, 1:2], scalar2=None, op0=ALU.mult), ["sm", "lamv"], ["sm"])
                    tq3 = tq[:, :].rearrange("s (h d) -> s h d", d=128)
                    for m in range(2):
                        V(lambda e: e.tensor_tensor(out=tq3, in0=pj[:, 1024:1536].rearrange("s (h d) -> s h d", d=128),
                                                    in1=pn3[:, :, m].unsqueeze(2).to_broadcast([NO, 4, 128]), op=ALU.mult), ["pjA", "pn"], ["tqA"])
                        V(lambda e: e.tensor_tensor(out=Op4[:, :, m, :], in0=Op4[:, :, m, :], in1=tq3, op=ALU.add), ["Op", "tqA"], ["Op"])
                    V(lambda e: e.tensor_tensor(out=Op[:], in0=Op[:], in1=sm[:].unsqueeze(2).to_broadcast([NO, 8, 128]), op=ALU.mult), ["Op", "sm"], ["Op"])
                    V(lambda e: e.tensor_tensor(out=oa[:], in0=Op4[:, :, 0, :], in1=Op4[:, :, 1, :], op=ALU.add), ["Op"], ["oa"])
                    A(lambda e: e.activation(out=zs[:], in_=pj[:, 1536:2048], func=AF.Silu), ["pjA"], ["zsA"])
                    gated_norm_rows(ss, oa[:], "oa", 4, 128, sgl[0:NO, :].unsqueeze(1).to_broadcast([NO, 4, 128]), "sgl", zs[:], "zsA", 0)
                S.barrier()

            def sample_ssd():
                with ExitStack() as ss:
                    pj = T(ss, "pjB", [NO, 1544])
                    cw = T(ss, "cwB", [NO, 1024])
                    cs_ = T(ss, "csB", [NO, 1024])
                    xa = T(ss, "xaB", [NO, 1024])
                    big = T(ss, "bigB", [NO, 1024])
                    sv = T(ss, "svB", [NO, 32])
                    rowp = T(ss, "rowpB", [NO, 24])
                    yb = T(ss, "ybB", [NO, 512])
                    zs = T(ss, "zsB", [NO, 512])
                    gB = T(ss, "gB", [NO, 512])
                    colsT = T(ss, "colsT", [128, 8, NO])
                    bcb = T(ss, "bcb", [128, 512])
                    hst = T(ss, "hst", [128, 4, 128])
                    ycol = T(ss, "ycol", [128, 4, NO])
                    hj = T(ss, "hj", [128, 128])
                    proj_rows(pj, "pjB", [(0, 512, 0), (512, 512, 512), (1024, 8, 1024), (1032, 512, 1032)])
                    bcast_row(cw[:], ssm_conv_w[l:l + 1, 3 * 1024:4 * 1024], 1024, "cwB")
                    V(lambda e: e.tensor_tensor(out=xa[:], in0=pj[:, 0:1024], in1=cw[:], op=ALU.mult), ["pjB", "cwB"], ["xaB"])
                    for i in range(3):
                        bcast_row(cw[:], ssm_conv_w[l:l + 1, i * 1024:(i + 1) * 1024], 1024, "cwB")
                        S.dma("sp", cs_[:], st_sconv[l, :, i, :], writes=["csB"], slot="csB")
                        V(lambda e: e.tensor_tensor(out=big[:], in0=cs_[:], in1=cw[:], op=ALU.mult), ["csB", "cwB"], ["bigB"])
                        V(lambda e: e.tensor_tensor(out=xa[:], in0=xa[:], in1=big[:], op=ALU.add), ["xaB", "bigB"], ["xaB"])
                        if i >= 1:
                            S.dma("sp", sconv_s[l][:, i - 1, :], cs_[:], reads=["csB"], slot="scs0")
                    S.dma("sp", sconv_s[l][:, 2, :], pj[:, 0:1024], reads=["pjB"], slot="scs1")
                    bcast_row(cw[:], ssm_conv_b[l:l + 1, :], 1024, "cwB")
                    V(lambda e: e.tensor_tensor(out=xa[:], in0=xa[:], in1=cw[:], op=ALU.add), ["xaB", "cwB"], ["xaB"])
                    A(lambda e: e.activation(out=xa[:], in_=xa[:], func=AF.Silu), ["xaB"], ["xaB"])
                    bcast_row(rowp[:, 0:8], ssm_dt_bias[l:l + 1, :], 8, "rowpB_a")
                    bcast_row(rowp[:, 8:16], ssm_A_log[l:l + 1, :], 8, "rowpB_b")
                    bcast_row(rowp[:, 16:24], ssm_D[l:l + 1, :], 8, "rowpB_c")
                    RP = ["rowpB_a", "rowpB_b", "rowpB_c"]
                    V(lambda e: e.tensor_tensor(out=sv[:, 0:8], in0=pj[:, 1024:1032], in1=rowp[:, 0:8], op=ALU.add), ["pjB"] + RP, ["svB"])
                    softplus(sv[:, 0:8], sv[:, 0:8], ["svB"], "svB")
                    A(lambda e: e.activation(out=sv[:, 8:16], in_=rowp[:, 8:16], func=AF.Exp), RP + ["svB"], ["svB"])
                    V(lambda e: e.scalar_tensor_tensor(out=sv[:, 16:24], in0=sv[:, 8:16], scalar=-1.0, in1=sv[:, 0:8], op0=ALU.mult, op1=ALU.mult), ["svB"], ["svB"])
                    A(lambda e: e.activation(out=sv[:, 16:24], in_=sv[:, 16:24], func=AF.Exp), ["svB"], ["svB"])
                    V(lambda e: e.tensor_tensor(out=big[:, 0:512].rearrange("s (h p) -> s h p", p=64), in0=xa[:, 0:512].rearrange("s (h p) -> s h p", p=64),
                                                in1=sv[:, 0:8].unsqueeze(2).to_broadcast([NO, 8, 64]), op=ALU.mult), ["xaB", "svB"], ["bigB"])
                    V(lambda e: e.tensor_copy(out=big[:, 512:1024].rearrange("s (h p) -> s h p", p=64), in_=sv[:, 16:24].unsqueeze(2).to_broadcast([NO, 8, 64])), ["svB", "bigB"], ["bigB"])
                    def f(e):
                        for j in range(8):
                            ins = e.transpose(out=bank[0][:, j * NO:(j + 1) * NO], in_=big[:, j * 128:(j + 1) * 128], identity=cm[0:NO, 0, 0:NO])
                        return ins
                    PE(f, ["bigB", "cm"], [BK[0]])
                    V(lambda e: e.tensor_copy(out=colsT[:].rearrange("p j s -> p (j s)"), in_=bank[0][:, 0:8 * NO]), [BK[0]], ["colsT"])
                    for s in range(NO):
                        S.dma("sp", hst[:], st_ssm[l, s].rearrange("h p n -> (h p) n").rearrange("(j q) n -> q j n", q=128), writes=["hst"], slot="hst")
                        PE(lambda e: e.matmul(bank[1][:, :], lhsT=sel4[:, s, :], rhs=xa[:, 512:1024], start=True, stop=True), ["sel4", "xaB"], [BK[1]])
                        A(lambda e: e.activation(out=bcb[:], in_=bank[1][:, :], func=AF.Copy), [BK[1]], ["bcb"])
                        for j in range(4):
                            g = j // 2
                            V(lambda e: e.tensor_scalar(out=hst[:, j, :], in0=hst[:, j, :], scalar1=colsT[:, 4 + j, s:s + 1], scalar2=None, op0=ALU.mult), ["hst", "colsT"], ["hst"])
                            V(lambda e: e.scalar_tensor_tensor(out=hst[:, j, :], in0=bcb[:, g * 128:(g + 1) * 128], scalar=colsT[:, j, s:s + 1], in1=hst[:, j, :],
                                                               op0=ALU.mult, op1=ALU.add), ["hst", "bcb", "colsT"], ["hst"])
                            V(lambda e: e.tensor_tensor(out=hj[:], in0=hst[:, j, :], in1=bcb[:, 256 + g * 128:256 + (g + 1) * 128], op=ALU.mult), ["hst", "bcb"], ["hj"])
                            V(lambda e: e.tensor_reduce(out=ycol[:, j, s:s + 1], in_=hj[:], axis=AX.X, op=ALU.add), ["hj"], ["ycol"])
                        S.dma("sp", ssm_s[l, s].rearrange("h p n -> (h p) n").rearrange("(j q) n -> q j n", q=128), hst[:], reads=["hst"], slot="ssms")
                    def f(e):
                        for j in range(4):
                            ins = e.transpose(out=bank[2][0:NO, j * 128:(j + 1) * 128], in_=ycol[:, j, :], identity=ident)
                        return ins
                    PE(f, ["ycol", "cm"], [BK[2]])
                    V(lambda e: e.tensor_tensor(out=big[:, 0:512].rearrange("s (h p) -> s h p", p=64), in0=xa[:, 0:512].rearrange("s (h p) -> s h p", p=64),
                                                in1=rowp[:, 16:24].unsqueeze(2).to_broadcast([NO, 8, 64]), op=ALU.mult), ["xaB"] + RP, ["bigB"])
                    V(lambda e: e.tensor_tensor(out=yb[:], in0=bank[2][0:NO, :], in1=big[:, 0:512], op=ALU.add), [BK[2], "bigB"], ["ybB"])
                    A(lambda e: e.activation(out=zs[:], in_=pj[:, 1032:1544], func=AF.Silu), ["pjB"], ["zsB"])
                    V(lambda e: e.tensor_tensor(out=yb[:], in0=yb[:], in1=zs[:], op=ALU.mult), ["ybB", "zsB"], ["ybB"])
                    bcast_row(gB[:], ssm_norm_g[l:l + 1, :], 512, "gB")
                    gated_norm_rows(ss, yb[:, :].rearrange("s (g d) -> s g d", d=256), "ybB", 2, 256, gB[:, :].rearrange("s (g d) -> s g d", d=256), "gB", None, None, 1)
                S.barrier()

            def sample_gdn():
                with ExitStack() as ss:
                    pj = T(ss, "pjC", [NO, 2056])
                    cw = T(ss, "cwC", [NO, 1536])
                    cs_ = T(ss, "csC", [NO, 1536])
                    sv = T(ss, "svC", [NO, 32])
                    rowp = T(ss, "rowpC", [NO, 8])
                    zs = T(ss, "zsC", [NO, 512])
                    gC = T(ss, "gC", [NO, 128])
                    qkT = T(ss, "qkT", [128, 8, NO])
                    qkTm = T(ss, "qkTm", [128, 8, NO, NO])
                    kmr = T(ss, "kmr", [NO, 512])
                    egb = T(ss, "egb", [128, NO, 4])
                    vn = T(ss, "vn", [NO, 512])
                    Sall = T(ss, "Sall", [128, NO, 4, 128])
                    proj_rows(pj, "pjC", [(0, 512, 0), (512, 512, 512), (1024, 512, 1024), (1536, 8, 1536), (1544, 512, 1544)])
                    xa = pj[:, 0:1536]
                    S.dma("sp", gconv_s[l][:, 2, :], pj[:, 0:1536], reads=["pjC"], slot="gcs1")
                    for s in range(NO):
                        S.dma("sp", Sall[:, s, :, :], st_gdn[l, s].rearrange("h k v -> k h v"), writes=["Sall%d" % s], slot="Sall%d" % s)
                    bcast_row(cw[:], gdn_conv_w[l:l + 1, 3 * 1536:4 * 1536], 1536, "cwC")
                    V(lambda e: e.tensor_tensor(out=xa, in0=xa, in1=cw[:], op=ALU.mult), ["pjC", "cwC"], ["pjC"])
                    for i in range(3):
                        bcast_row(cw[:], gdn_conv_w[l:l + 1, i * 1536:(i + 1) * 1536], 1536, "cwC")
                        S.dma("sp", cs_[:], st_gconv[l, :, i, :], writes=["csC"], slot="csC")
                        if i >= 1:
                            S.dma("sp", gconv_s[l][:, i - 1, :], cs_[:], reads=["csC"], slot="gcs0")
                        V(lambda e: e.tensor_tensor(out=cs_[:], in0=cs_[:], in1=cw[:], op=ALU.mult), ["csC", "cwC"], ["csC"])
                        V(lambda e: e.tensor_tensor(out=xa, in0=xa, in1=cs_[:], op=ALU.add), ["pjC", "csC"], ["pjC"])
                    A(lambda e: e.activation(out=xa, in_=xa, func=AF.Silu), ["pjC"], ["pjC"])
                    V(lambda e: e.tensor_tensor(out=cs_[:, 0:1024], in0=xa[:, 0:1024], in1=xa[:, 0:1024], op=ALU.mult), ["pjC", "csC"], ["csC"])
                    V(lambda e: e.tensor_reduce(out=sv[:, 0:8], in_=cs_[:, 0:1024].rearrange("s (g d) -> s g d", d=128), axis=AX.X, op=ALU.add), ["csC"], ["svC"])
                    A(lambda e: e.activation(out=sv[:, 0:8], in_=sv[:, 0:8], func=AF.Sqrt, bias=epsc[0:NO, 0:1]), ["svC", "epsc"], ["svC"])
                    V(lambda e: e.reciprocal(out=sv[:, 0:8], in_=sv[:, 0:8]), ["svC"], ["svC"])
                    V(lambda e: e.tensor_scalar(out=sv[:, 0:4], in0=sv[:, 0:4], scalar1=float(128 ** -0.5), scalar2=None, op0=ALU.mult), ["svC"], ["svC"])
                    V(lambda e: e.tensor_tensor(out=xa[:, 0:1024].rearrange("s (g d) -> s g d", d=128), in0=xa[:, 0:1024].rearrange("s (g d) -> s g d", d=128),
                                                in1=sv[:, 0:8].unsqueeze(2).to_broadcast([NO, 8, 128]), op=ALU.mult), ["pjC", "svC"], ["pjC"])
                    A(lambda e: e.activation(out=sv[:, 8:12], in_=pj[:, 1536:1540], func=AF.Sigmoid), ["pjC", "svC"], ["svC"])
                    bcast_row(rowp[:, 0:4], gdn_dt_bias[l:l + 1, :], 4, "rowpC_a")
                    bcast_row(rowp[:, 4:8], gdn_A_log[l:l + 1, :], 4, "rowpC_b")
                    RQ = ["rowpC_a", "rowpC_b"]
                    V(lambda e: e.tensor_tensor(out=sv[:, 12:16], in0=pj[:, 1540:1544], in1=rowp[:, 0:4], op=ALU.add), ["pjC", "svC"] + RQ, ["svC"])
                    softplus(sv[:, 12:16], sv[:, 12:16], ["svC"], "svC")
                    A(lambda e: e.activation(out=sv[:, 16:20], in_=rowp[:, 4:8], func=AF.Exp), RQ + ["svC"], ["svC"])
                    V(lambda e: e.scalar_tensor_tensor(out=sv[:, 12:16], in0=sv[:, 16:20], scalar=-1.0, in1=sv[:, 12:16], op0=ALU.mult, op1=ALU.mult), ["svC"], ["svC"])
                    A(lambda e: e.activation(out=sv[:, 12:16], in_=sv[:, 12:16], func=AF.Exp), ["svC"], ["svC"])
                    def f(e):
                        for j in range(8):
                            ins = e.transpose(out=bank[0][:, j * NO:(j + 1) * NO], in_=xa[:, j * 128:(j + 1) * 128], identity=cm[0:NO, 0, 0:NO])
                        return ins
                    PE(f, ["pjC", "cm"], [BK[0]])
                    V(lambda e: e.tensor_copy(out=qkT[:].rearrange("p j s -> p (j s)"), in_=bank[0][:, 0:8 * NO]), [BK[0]], ["qkT"])
                    V(lambda e: e.tensor_tensor(out=qkTm[:], in0=qkT[:].unsqueeze(2).to_broadcast([128, 8, NO, NO]),
                                                in1=eye4[:].unsqueeze(1).to_broadcast([128, 8, NO, NO]), op=ALU.mult), ["qkT", "eye4"], ["qkTm"])
                    for s in range(NO):
                        PE(lambda e: e.matmul(bank[1][:, s * 4:(s + 1) * 4], lhsT=sel4[:, s, :], rhs=sv[:, 12:16], start=True, stop=True), ["sel4", "svC"], [BK[1]])
                    V(lambda e: e.tensor_copy(out=egb[:].rearrange("p s h -> p (s h)"), in_=bank[1][:, 0:16]), [BK[1]], ["egb"])
                    SA = ["Sall%d" % s for s in range(NO)]
                    for h in range(4):
                        def f(e):
                            for s in range(NO):
                                ins = e.matmul(bank[2][0:NO, h * 128:(h + 1) * 128], lhsT=qkTm[:, 4 + h, s, :], rhs=Sall[:, s, h, :], start=(s == 0), stop=(s == NO - 1))
                            return ins
                        PE(f, ["qkTm"] + SA, [BK[2]])
                    V(lambda e: e.tensor_tensor(out=vn[:].rearrange("s (h d) -> s h d", d=128), in0=bank[2][0:NO, :].rearrange("s (h d) -> s h d", d=128),
                                                in1=sv[:, 12:16].unsqueeze(2).to_broadcast([NO, 4, 128]), op=ALU.mult), [BK[2], "svC"], ["vn"])
                    V(lambda e: e.tensor_tensor(out=vn[:], in0=xa[:, 1024:1536], in1=vn[:], op=ALU.subtract), ["pjC", "vn"], ["vn"])
                    V(lambda e: e.tensor_tensor(out=vn[:].rearrange("s (h d) -> s h d", d=128), in0=vn[:].rearrange("s (h d) -> s h d", d=128),
                                                in1=sv[:, 8:12].unsqueeze(2).to_broadcast([NO, 4, 128]), op=ALU.mult), ["vn", "svC"], ["vn"])
                    for s in range(NO):
                        V(lambda e: e.tensor_scalar(out=kmr[:], in0=xa[:, 512:1024], scalar1=cm[0:NO, 0, s:s + 1], scalar2=None, op0=ALU.mult), ["pjC", "cm"], ["kmr"])
                        for h in range(4):
                            PE(lambda e: e.matmul(bank[3][:, h * 128:(h + 1) * 128], lhsT=kmr[:, h * 128:(h + 1) * 128], rhs=vn[:, h * 128:(h + 1) * 128], start=True, stop=True),
                               ["kmr", "vn"], [BK[3]])
                            V(lambda e: e.scalar_tensor_tensor(out=Sall[:, s, h, :], in0=Sall[:, s, h, :], scalar=egb[:, s, h:h + 1], in1=bank[3][:, h * 128:(h + 1) * 128],
                                                               op0=ALU.mult, op1=ALU.add), ["Sall%d" % s, "egb", BK[3]], ["Sall%d" % s])
                        S.dma("sp", gdn_s[l, s].rearrange("h k v -> k h v"), Sall[:, s, :, :], reads=["Sall%d" % s], slot="gdns%d" % s)
                    for h in range(4):
                        def f(e):
                            for s in range(NO):
                                ins = e.matmul(bank[4][0:NO, h * 128:(h + 1) * 128], lhsT=qkTm[:, h, s, :], rhs=Sall[:, s, h, :], start=(s == 0), stop=(s == NO - 1))
                            return ins
                        PE(f, ["qkTm"] + SA, [BK[4]])
                    og = vn
                    V(lambda e: e.tensor_copy(out=og[:], in_=bank[4][0:NO, :]), [BK[4], "vn"], ["vn"])
                    A(lambda e: e.activation(out=zs[:], in_=pj[:, 1544:2056], func=AF.Silu), ["pjC"], ["zsC"])
                    bcast_row(gC[:], gdn_norm_g[l:l + 1, :], 128, "gC")
                    gated_norm_rows(ss, og[:, :].rearrange("s (h d) -> s h d", d=128), "vn", 4, 128, gC[:, :].unsqueeze(1).to_broadcast([NO, 4, 128]), "gC", zs[:], "zsC", 2)
                S.barrier()

            def sample_finish():
                with ExitStack() as gstack:
                    outproj_sample(l, gstack)
                    S.barrier()
                S.barrier()

            if "A" in do_branches or do_sample:
                load_w(l, [(0, 1536), (OZ, 512)])
                if do_sample:
                    sample_attn()
            if "A" in do_branches:
                with ExitStack() as bs:
                    kT = T(bs, "kT", [128, 4, L], BF16)
                    vbf = T(bs, "vbf", [128, NT, 4, 130], BF16)
                    kvf = T(bs, "kvf", [128, 1, 512])
                    qT = T(bs, "qT", [128, 2, 4, 128], BF16)
                    pT = [T(bs, "pT%d" % i, [128, 4, 128], BF16) for i in range(2)]
                    sdg = T(bs, "sdg", [128, 2, 128])
                    za = T(bs, "za", [128, 512])
                    oh_ = T(bs, "oh_", [128, 512])
                    ob = T(bs, "ob", [128, 512], BF16)
                    r8 = T(bs, "r8", [128, 8])
                    s4 = T(bs, "s4", [128, 8])
                    jk = T(bs, "jk", [128, 128])
                    load_wout(l, 0)
                    V(lambda e: e.memset(vbf[:], 1.0), [], ["vbf"])
                    V(lambda e: e.memset(qT[:], 0.0), [], ["qT"])
                    for i in range(NT):
                        for h in range(4):
                            proj_feat(bank[0][:, h * 128:(h + 1) * 128], BK[0], 512 + h * 128, lambda k: hT[:, k, i * 128:(i + 1) * 128], ["hT%d" % i])
                        A(lambda e: e.activation(out=kT[:, :, i * 128:(i + 1) * 128], in_=bank[0][:, :].rearrange("p (h t) -> p h t", t=128), func=AF.Copy), [BK[0]], ["kT%d" % i])
                        for (j, c0, dst) in ((0, 512, k_p), (1, 1024, v_p)):
                            proj_tok(bank[1 + j][:, :], BK[1 + j], lambda k: hT[:, k, i * 128:(i + 1) * 128], ["hT%d" % i], c0, 512)
                            V(lambda e: e.tensor_copy(out=kvf[:, 0, :], in_=bank[1 + j][:, :]), [BK[1 + j]], ["kvf"])
                            S.dma("sp", dst[l, i * 128:(i + 1) * 128, :], kvf[:, 0, :], reads=["kvf"], slot="kvout")
                            if j == 1:
                                A(lambda e: e.activation(out=vbf[:, i, :, 0:128], in_=bank[2][:, :].rearrange("p (h d) -> p h d", d=128), func=AF.Copy), [BK[2]], ["vbf%d" % i, "vbf"])
                    for i in range((NT if DBG_A2 is None else DBG_A2) if DBG_STOP != 1 else 0):
                        for h in range(4):
                            proj_feat(bank[0][:, h * 128:(h + 1) * 128], BK[0], h * 128, lambda k: hT[:, k, i * 128:(i + 1) * 128], ["hT%d" % i])
                        A(lambda e: e.activation(out=qT[0:64, 0, :, :], in_=bank[0][0:64, :].rearrange("p (h t) -> p h t", t=128), func=AF.Copy, scale=0.125), [BK[0], "qT"], ["qT"])
                        A(lambda e: e.activation(out=qT[64:128, 1, :, :], in_=bank[0][64:128, :].rearrange("p (h t) -> p h t", t=128), func=AF.Copy, scale=0.125), [BK[0], "qT"], ["qT"])
                        proj_tok(bank[1][:, :], BK[1], lambda k: hT[:, k, i * 128:(i + 1) * 128], ["hT%d" % i], 1536, 512)
                        A(lambda e: e.activation(out=za[:], in_=bank[1][:, :], func=AF.Silu), [BK[1]], ["za"])
                        cnt = 0
                        if DBG_STOP == 2:
                            continue
                        for h in range(4):
                            for m in range(2 if DBG_STOP != 5 else 1):
                                hm = h * 2 + m
                                ob_bank = 4 + hm // 3
                                ocol = (hm % 3) * 130
                                ngroups = (i + 1 + 3) // 4
                                for g in range(ngroups):
                                    j0 = g * 4
                                    nb = min(4, i + 1 - j0)
                                    sb = 2 + (cnt % 2); pb = cnt % 2; cnt += 1
                                    def f(e):
                                        for jj in range(nb):
                                            j = j0 + jj
                                            ins = e.matmul(bank[sb][:, jj * 128:(jj + 1) * 128], lhsT=kT[:, h, j * 128:(j + 1) * 128],
                                                           rhs=qT[:, m, h, :], start=True, stop=True)
                                        return ins
                                    PE(f, ["qT"] + ["kT%d" % (j0 + jj) for jj in range(nb)], [BK[sb]])
                                    nfar = max(0, min(nb, (i - 1) - j0))
                                    if nfar > 0 and DBG_VAR != 1:
                                        A(lambda e: e.activation(out=pT[pb][:, 0:nfar, :], in_=bank[sb][:, 0:nfar * 128].rearrange("p (j t) -> p j t", t=128), func=AF.Exp,
                                                                 bias=c31[:, h:h + 1]), [BK[sb], "c31"], ["pT%d" % pb])
                                    for jj in range(nfar, nb if DBG_VAR != 2 else nfar):
                                        j = j0 + jj
                                        x_ = i - j
                                        V(lambda e: e.tensor_tensor(out=sdg[:, x_, :], in0=bank[sb][:, jj * 128:(jj + 1) * 128], in1=BT[:, h, x_ * 128:(x_ + 1) * 128], op=ALU.add),
                                          [BK[sb], "BT"], ["sdg%d" % x_])
                                        A(lambda e: e.activation(out=pT[pb][:, jj, :], in_=sdg[:, x_, :], func=AF.Exp), ["sdg%d" % x_], ["pT%d" % pb])
                                    if DBG_STOP == 4:
                                        continue
                                    def f2(e):
                                        for jj in range(nb):
                                            j = j0 + jj
                                            ins = e.matmul(bank[ob_bank][:, ocol:ocol + 129], lhsT=pT[pb][:, jj, :], rhs=vbf[:, j, h, 0:129], start=(j == 0), stop=(j == i))
                                        return ins
                                    PE(f2, ["pT%d" % pb] + ["vbf%d" % (j0 + jj) for jj in range(nb)], [BK[ob_bank]])
                        if DBG_STOP in (3, 4, 5):
                            continue
                        for hm in range(8):
                            ob_bank = 4 + hm // 3; ocol = (hm % 3) * 130
                            V(lambda e: e.reciprocal(out=r8[:, hm:hm + 1], in_=bank[ob_bank][:, ocol + 128:ocol + 129]), [BK[ob_bank]], ["r8_%d" % hm])
                            if hm % 2 == 1:
                                V(lambda e: e.tensor_scalar(out=r8[:, hm:hm + 1], in0=r8[:, hm:hm + 1], scalar1=lamv[:, 1:2], scalar2=None, op0=ALU.mult), ["r8_%d" % hm, "lamv"], ["r8_%d" % hm])
                        for h in range(4):
                            b0 = 4 + (2 * h) // 3; c0 = ((2 * h) % 3) * 130
                            b1 = 4 + (2 * h + 1) // 3; c1 = ((2 * h + 1) % 3) * 130
                            V(lambda e: e.tensor_scalar(out=oh_[:, h * 128:(h + 1) * 128], in0=bank[b0][:, c0:c0 + 128], scalar1=r8[:, 2 * h:2 * h + 1], scalar2=None, op0=ALU.mult),
                              [BK[b0], "r8_%d" % (2 * h)], ["oh_%d" % h])
                            V(lambda e: e.scalar_tensor_tensor(out=oh_[:, h * 128:(h + 1) * 128], in0=bank[b1][:, c1:c1 + 128], scalar=r8[:, 2 * h + 1:2 * h + 2],
                                                               in1=oh_[:, h * 128:(h + 1) * 128], op0=ALU.mult, op1=ALU.add), [BK[b1], "r8_%d" % (2 * h + 1), "oh_%d" % h], ["oh_%d" % h])
                            A(lambda e: e.activation(out=jk[:], in_=oh_[:, h * 128:(h + 1) * 128], func=AF.Square, accum_out=s4[:, h:h + 1]), ["oh_%d" % h], ["jk", "s4_%d" % h])
                        rstd_from_ss(s4[:, 4:8], s4[:, 0:4], 128, ["s4_%d" % h for h in range(4)], "s4r")
                        for h in range(4):
                            V(lambda e: e.scalar_tensor_tensor(out=oh_[:, h * 128:(h + 1) * 128], in0=oh_[:, h * 128:(h + 1) * 128], scalar=s4[:, 4 + h:5 + h], in1=sgl[:],
                                                               op0=ALU.mult, op1=ALU.mult), ["oh_%d" % h, "s4r", "sgl"], ["oh_%d" % h])
                        V(lambda e: e.tensor_tensor(out=ob[:], in0=oh_[:], in1=za[:], op=ALU.mult), ["oh_%d" % h for h in range(4)] + ["za"], ["ob"])
                        o_to_T(ob[:], "ob", oTt, "oTt")
                        outproj_update_prompt(i, oTt, "oTt", banks=(0, 1))
                    S.barrier()

            if "B" in do_branches or do_sample:
                load_w(l, [(OXBC, 1032), (OZ + 512, 512)])
                if do_sample:
                    sample_ssd()
            if "B" in do_branches:
                with ExitStack() as bs:
                    NCH = 8
                    xc = T(bs, "xc", [128, NCH, 131])
                    cwT = T(bs, "cwT", [128, 40])
                    cwr = T(bs, "cwr", [40, 128])
                    xact = T(bs, "xact", [128, NCH, 128])
                    acc = T(bs, "acc", [128, 128])
                    xact_b = T(bs, "xact_b", [128, 4, 128], BF16)
                    xs_tok = T(bs, "xs_tok", [128, 512])
                    Btok = T(bs, "Btok", [128, 2, 128], BF16)
                    rows = T(bs, "rows", [128, 32])
                    dtv = T(bs, "dtv", [128, 48])
                    aU = T(bs, "aU", [128, 128])
                    Es = T(bs, "Es", [128, 8, 128])
                    MT = T(bs, "MT", [128, 8, 128], BF16)
                    xdt = T(bs, "xdt", [128, 8, 64], BF16)
                    xdtw = T(bs, "xdtw", [128, 8, 64], BF16)
                    hS = T(bs, "hS", [128, 8, 64])
                    hSb = T(bs, "hSb", [128, 8, 64], BF16)
                    ysb = T(bs, "ysb", [128, 512])
                    zb = T(bs, "zb", [128, 512])
                    ng = T(bs, "ng", [128, 512])
                    ob = T(bs, "obB", [128, 512], BF16)
                    s2 = T(bs, "s2", [128, 4])
                    jk = T(bs, "jkB", [128, 256])
                    st3 = T(bs, "st3", [3, 1024])
                    hout = T(bs, "hout", [64, 8, 128])
                    load_wout(l, 512)
                    S.dma("sp", cwr[0:32, :], ssm_conv_w[l:l + 1, :].rearrange("o (a p) -> (o a) p", p=128), writes=["cwr"], slot="cwr")
                    S.dma("sp", cwr[32:40, :], ssm_conv_b[l:l + 1, :].rearrange("o (a p) -> (o a) p", p=128), writes=["cwr"], slot="cwr")
                    transpose_f32(bank[0][:, 0:40], cwr[:], ["cwr"], [BK[0]], rows=40)
                    V(lambda e: e.tensor_copy(out=cwT[:], in_=bank[0][:, 0:40]), [BK[0]], ["cwT"])
                    bcast_row(rows[:, 0:8], ssm_dt_bias[l:l + 1, :], 8, "rows_a")
                    bcast_row(rows[:, 8:16], ssm_A_log[l:l + 1, :], 8, "rows_b")
                    bcast_row(rows[:, 16:24], ssm_D[l:l + 1, :], 8, "rows_c")
                    RW = ["rows_a", "rows_b", "rows_c"]
                    A(lambda e: e.activation(out=rows[:, 24:32], in_=rows[:, 8:16], func=AF.Exp), RW, ["rows_d"])
                    RW = RW + ["rows_d"]
                    bcast_row(ng[:], ssm_norm_g[l:l + 1, :], 512, "ngB")
                    V(lambda e: e.memset(xc[:], 0.0), [], ["xc"])
                    V(lambda e: e.memset(hS[:], 0.0), [], ["hS"])
                    V(lambda e: e.memset(hSb[:], 0.0), [], ["hSb"])
                    for i in range(NT):
                        hk_ = ["hT%d" % i]
                        rhs_fn = lambda k: hT[:, k, i * 128:(i + 1) * 128]
                        for half in range(2):
                            for c in range(4):
                                proj_feat(bank[half][:, c * 128:(c + 1) * 128], BK[half], (half * 4 + c) * 128, rhs_fn, hk_)
                            A(lambda e: e.activation(out=xc[:, half * 4:(half + 1) * 4, 3:131], in_=bank[half][:, :].rearrange("p (c t) -> p c t", t=128), func=AF.Copy),
                              [BK[half]], ["xc"])
                        for c in range(NCH):
                            G(lambda e: e.tensor_scalar(out=acc[:], in0=xc[:, c, 3:131], scalar1=cwT[:, 24 + c:25 + c], scalar2=None, op0=ALU.mult), ["xc", "cwT"], ["acc"])
                            for tap in range(3):
                                V(lambda e: e.scalar_tensor_tensor(out=acc[:], in0=xc[:, c, tap:tap + 128], scalar=cwT[:, tap * 8 + c:tap * 8 + c + 1], in1=acc[:],
                                                                   op0=ALU.mult, op1=ALU.add), ["xc", "cwT", "acc"], ["acc"])
                            A(lambda e: e.activation(out=xact[:, c, :], in_=acc[:], func=AF.Silu, bias=cwT[:, 32 + c:33 + c]), ["acc", "cwT"], ["xact%d" % c])
                        if i == NT - 1:
                            for c in range(NCH):
                                transpose_f32(bank[6][0:3, c * 128:(c + 1) * 128] if c < 4 else bank[5][0:3, (c - 4) * 128:(c - 3) * 128], xc[:, c, 128:131], ["xc"], [BK[6] if c < 4 else BK[5]])
                            V(lambda e: e.tensor_copy(out=st3[:, 0:512], in_=bank[6][0:3, :]), [BK[6]], ["st3"])
                            V(lambda e: e.tensor_copy(out=st3[:, 512:1024], in_=bank[5][0:3, :]), [BK[5], "st3"], ["st3"])
                            S.dma("sp", sconv_p[l], st3[:], reads=["st3"], slot="sconvp")
                        V(lambda e: e.tensor_copy(out=xc[:, :, 0:3], in_=xc[:, :, 128:131]), ["xc"], ["xc"])
                        XA = ["xact%d" % c for c in range(NCH)]
                        proj_tok(bank[2][:, 0:8], BK[2], lambda k: hT[:, k, i * 128:(i + 1) * 128], hk_, 1024, 8)
                        V(lambda e: e.tensor_tensor(out=dtv[:, 0:8], in0=bank[2][:, 0:8], in1=rows[:, 0:8], op=ALU.add), [BK[2]] + RW, ["dtv"])
                        softplus(dtv[:, 0:8], dtv[:, 0:8], ["dtv"], "dtv")
                        V(lambda e: e.scalar_tensor_tensor(out=dtv[:, 8:16], in0=dtv[:, 0:8], scalar=-1.0, in1=rows[:, 24:32], op0=ALU.mult, op1=ALU.mult), ["dtv"] + RW, ["dtv"])
                        proj_tok(bank[3][:, :], BK[3], lambda k: hT[:, k, i * 128:(i + 1) * 128], hk_, 1032, 512)
                        A(lambda e: e.activation(out=zb[:], in_=bank[3][:, :], func=AF.Silu), [BK[3]], ["zb"])
                        for c in range(4):
                            transpose_f32(bank[4][:, c * 128:(c + 1) * 128], xact[:, c, :], ["xact%d" % c], [BK[4]])
                        V(lambda e: e.tensor_copy(out=xs_tok[:], in_=bank[4][:, :]), [BK[4]], ["xs_tok"])
                        for c in range(2):
                            transpose_f32(bank[5][:, c * 128:(c + 1) * 128], xact[:, 4 + c, :], ["xact%d" % (4 + c)], [BK[5]])
                        A(lambda e: e.activation(out=Btok[:], in_=bank[5][:, 0:256].rearrange("p (g n) -> p g n", n=128), func=AF.Copy), [BK[5]], ["Btok"])
                        A(lambda e: e.activation(out=xact_b[:], in_=xact[:, 4:8, :], func=AF.Copy), XA, ["xact_b"])
                        V(lambda e: e.tensor_tensor(out=xdt[:], in0=xs_tok[:].rearrange("p (h q) -> p h q", q=64), in1=dtv[:, 0:8].unsqueeze(2).to_broadcast([128, 8, 64]), op=ALU.mult),
                          ["xs_tok", "dtv"], ["xdt"])
                        PE(lambda e: e.matmul(bank[2][:, 16:24], lhsT=Umat, rhs=dtv[:, 8:16], start=True, stop=True), ["cm", "dtv"], [BK[2]])
                        PE(lambda e: e.matmul(bank[2][:, 32:40], lhsT=onesf[:], rhs=dtv[:, 8:16], start=True, stop=True), ["onesf", "dtv"], [BK[2]])
                        V(lambda e: e.tensor_copy(out=dtv[:, 16:24], in_=bank[2][:, 16:24]), [BK[2], "dtv"], ["dtv"])
                        V(lambda e: e.tensor_scalar(out=dtv[:, 24:32], in0=bank[2][:, 16:24], scalar1=-1.0, scalar2=None, op0=ALU.mult), [BK[2], "dtv"], ["dtv"])
                        A(lambda e: e.activation(out=dtv[:, 32:40], in_=bank[2][:, 16:24], func=AF.Exp), [BK[2], "dtv"], ["dtv"])
                        A(lambda e: e.activation(out=dtv[:, 40:48], in_=bank[2][:, 32:40], func=AF.Exp), [BK[2], "dtv"], ["dtv"])
                        for g in range(2):
                            PE(lambda e: e.matmul(bank[3][:, g * 128:(g + 1) * 128], lhsT=xact_b[:, g, :], rhs=xact_b[:, 2 + g, :], start=True, stop=True), ["xact_b"], [BK[3]])
                        for h in range(8):
                            g = h // 4
                            sb_ = h % 2
                            G(lambda e: e.tensor_scalar(out=aU[:], in0=Umat, scalar1=dtv[:, 8 + h:9 + h], scalar2=None, op0=ALU.mult), ["cm", "dtv"], ["aU"])
                            def f(e):
                                e.matmul(bank[sb_][:, 0:128], lhsT=onesf[:], rhs=aU[:], start=True, stop=False)
                                return e.matmul(bank[sb_][:, 0:128], lhsT=ident, rhs=negmask, start=False, stop=True)
                            PE(f, ["onesf", "aU", "cm"], [BK[sb_]])
                            A(lambda e: e.activation(out=Es[:, h, :], in_=bank[sb_][:, 0:128], func=AF.Exp, bias=dtv[:, 24 + h:25 + h]), [BK[sb_], "dtv"], ["Es%d" % h])
                            V(lambda e: e.tensor_tensor(out=MT[:, h, :], in0=Es[:, h, :], in1=bank[3][:, g * 128:(g + 1) * 128], op=ALU.mult), ["Es%d" % h, BK[3]], ["MT%d" % h])
                        ES = ["Es%d" % h for h in range(8)]
                        V(lambda e: e.tensor_tensor(out=xdtw[:], in0=xdt[:], in1=Es[:, :, 127:128].to_broadcast([128, 8, 64]), op=ALU.mult), ["xdt"] + ES, ["xdtw"])
                        def f(e):
                            for h in range(8):
                                ins = e.matmul(bank[4][:, h * 64:(h + 1) * 64], lhsT=MT[:, h, :], rhs=xdt[:, h, :], start=True, stop=True)
                            return ins
                        PE(f, ["MT%d" % h for h in range(8)] + ["xdt"], [BK[4]])
                        def f(e):
                            for h in range(8):
                                ins = e.matmul(bank[5][:, h * 64:(h + 1) * 64], lhsT=xact_b[:, 2 + h // 4, :], rhs=hSb[:, h, :], start=True, stop=True)
                            return ins
                        PE(f, ["xact_b", "hSb"], [BK[5]])
                        V(lambda e: e.tensor_tensor(out=ysb[:].rearrange("p (h q) -> p h q", q=64), in0=bank[5][:, :].rearrange("p (h q) -> p h q", q=64),
                                                    in1=dtv[:, 32:40].unsqueeze(2).to_broadcast([128, 8, 64]), op=ALU.mult), [BK[5], "dtv"], ["ysb"])
                        V(lambda e: e.tensor_tensor(out=ysb[:], in0=ysb[:], in1=bank[4][:, :], op=ALU.add), ["ysb", BK[4]], ["ysb"])
                        def f(e):
                            for h in range(8):
                                ins = e.matmul(bank[6][:, h * 64:(h + 1) * 64], lhsT=Btok[:, h // 4, :], rhs=xdtw[:, h, :], start=True, stop=True)
                            return ins
                        PE(f, ["Btok", "xdtw"], [BK[6]])
                        V(lambda e: e.tensor_tensor(out=hS[:], in0=hS[:], in1=dtv[:, 40:48].unsqueeze(2).to_broadcast([128, 8, 64]), op=ALU.mult), ["hS", "dtv"], ["hS"])
                        V(lambda e: e.tensor_tensor(out=hS[:], in0=hS[:], in1=bank[6][:, :].rearrange("p (h q) -> p h q", q=64), op=ALU.add), ["hS", BK[6]], ["hS"])
                        A(lambda e: e.activation(out=hSb[:], in_=hS[:], func=AF.Copy), ["hS"], ["hSb"])
                        V(lambda e: e.tensor_tensor(out=xs_tok[:].rearrange("p (h q) -> p h q", q=64), in0=xs_tok[:].rearrange("p (h q) -> p h q", q=64),
                                                    in1=rows[:, 16:24].unsqueeze(2).to_broadcast([128, 8, 64]), op=ALU.mult), ["xs_tok", "xdt"] + RW, ["xs_tok"])
                        V(lambda e: e.tensor_tensor(out=ysb[:], in0=ysb[:], in1=xs_tok[:], op=ALU.add), ["ysb", "xs_tok"], ["ysb"])
                        V(lambda e: e.tensor_tensor(out=ysb[:], in0=ysb[:], in1=zb[:], op=ALU.mult), ["ysb", "zb"], ["ysb"])
                        for g in range(2):
                            A(lambda e: e.activation(out=jk[:], in_=ysb[:, g * 256:(g + 1) * 256], func=AF.Square, accum_out=s2[:, g:g + 1]), ["ysb"], ["jkB", "s2_%d" % g])
                        rstd_from_ss(s2[:, 2:4], s2[:, 0:2], 256, ["s2_0", "s2_1"], "s2r")
                        for g in range(2):
                            V(lambda e: e.scalar_tensor_tensor(out=ob[:, g * 256:(g + 1) * 256], in0=ysb[:, g * 256:(g + 1) * 256], scalar=s2[:, 2 + g:3 + g], in1=ng[:, g * 256:(g + 1) * 256],
                                                               op0=ALU.mult, op1=ALU.mult), ["ysb", "s2r", "ngB"], ["obB"])
                        o_to_T(ob[:], "obB", oTt, "oTt")
                        outproj_update_prompt(i, oTt, "oTt")
                    for h in range(8):
                        transpose_f32(bank[h % 2][0:64, 0:128], hS[:, h, :], ["hS"], [BK[h % 2]])
                        V(lambda e: e.tensor_copy(out=hout[:, h, :], in_=bank[h % 2][0:64, 0:128]), [BK[h % 2]], ["hout"])
                    S.dma("sp", ssm_p[l].rearrange("h p n -> p h n"), hout[:], reads=["hout"], slot="ssmp")
                    S.barrier()

            if "C" in do_branches or do_sample:
                load_w(l, [(OQKV, 1544), (OZ + 1024, 512)])
                if do_sample:
                    sample_gdn()
            if "C" in do_branches:
                with ExitStack() as bs:
                    NCH = 12
                    xc = T(bs, "xcC", [128, 4, 131])
                    hist = T(bs, "histC", [128, NCH, 3])
                    cwT = T(bs, "cwTC", [128, 48])
                    cwr = T(bs, "cwrC", [48, 128])
                    xact = T(bs, "xactC", [128, 4, 128])
                    acc = T(bs, "accC", [128, 128])
                    sq = T(bs, "sqC", [128, 4, 128])
                    qkb = T(bs, "qkb", [128, 8, 128], BF16)
                    ktok = T(bs, "ktok", [128, 4, 128], BF16)
                    kd = T(bs, "kd", [128, 4, 128], BF16)
                    vtok = T(bs, "vtok", [128, 4, 128])
                    rows = T(bs, "rowsC", [128, 16])
                    gv = T(bs, "gv", [128, 40])
                    gU = T(bs, "gU", [128, 128])
                    DTm = T(bs, "DTm", [128, 4, 128])
                    Dm = T(bs, "Dm", [128, 128])
                    NX = T(bs, "NX", [128, 4, 128], BF16)
                    NY = T(bs, "NY", [128, 4, 128], BF16)
                    XY = [T(bs, "XY%d" % i, [128, 4, 128], BF16) for i in range(4)]
                    Pn = T(bs, "Pn", [128, 4, 128], BF16)
                    Pt = T(bs, "Pt", [128, 4, 128], BF16)
                    gmask = T(bs, "gmask", [128, 4, 128], BF16)
                    S.dma("pool", gmask[:], gmask_d, writes=["gmask"], slot="gmask")
                    attT = T(bs, "attT", [128, 4, 128], BF16)
                    Sg = T(bs, "Sg", [128, 4, 128])
                    Sgb = T(bs, "Sgb", [128, 4, 128], BF16)
                    rr = T(bs, "rr", [128, 4, 128])
                    rb = T(bs, "rb", [128, 4, 128], BF16)
                    vnb = T(bs, "vnb", [128, 4, 128], BF16)
                    o1 = rr
                    zc = T(bs, "zc", [128, 512])
                    ngc = T(bs, "ngc", [128, 128])
                    ob = T(bs, "obC", [128, 512], BF16)
                    s4 = T(bs, "s4C", [128, 8])
                    jk = T(bs, "jkC", [128, 128])
                    st3 = T(bs, "st3C", [3, 512])
                    load_wout(l, 1024)
                    S.dma("sp", cwr[:], gdn_conv_w[l:l + 1, :].rearrange("o (a p) -> (o a) p", p=128), writes=["cwrC"], slot="cwrC")
                    transpose_f32(bank[0][:, 0:48], cwr[:], ["cwrC"], [BK[0]], rows=48)
                    V(lambda e: e.tensor_copy(out=cwT[:], in_=bank[0][:, 0:48]), [BK[0]], ["cwTC"])
                    bcast_row(rows[:, 0:4], gdn_dt_bias[l:l + 1, :], 4, "rowsC_a")
                    bcast_row(rows[:, 4:8], gdn_A_log[l:l + 1, :], 4, "rowsC_b")
                    RW = ["rowsC_a", "rowsC_b"]
                    A(lambda e: e.activation(out=rows[:, 8:12], in_=rows[:, 4:8], func=AF.Exp), RW, ["rowsC_c"])
                    RW = RW + ["rowsC_c"]
                    bcast_row(ngc[:], gdn_norm_g[l:l + 1, :], 128, "ngc")
                    V(lambda e: e.memset(hist[:], 0.0), [], ["histC"])
                    V(lambda e: e.memset(Sg[:], 0.0), [], ["Sg"])
                    V(lambda e: e.memset(Sgb[:], 0.0), [], ["Sgb"])
                    for i in range(NT if DBG_TILES is None else DBG_TILES):
                        hk_ = ["hT%d" % i]
                        rhs_fn = lambda k: hT[:, k, i * 128:(i + 1) * 128]
                        for q4 in range(3):
                            bk_ = q4 % 2
                            for c in range(4):
                                proj_feat(bank[bk_][:, c * 128:(c + 1) * 128], BK[bk_], (q4 * 4 + c) * 128, rhs_fn, hk_)
                            A(lambda e: e.activation(out=xc[:, :, 3:131], in_=bank[bk_][:, :].rearrange("p (c t) -> p c t", t=128), func=AF.Copy), [BK[bk_]], ["xcC"])
                            V(lambda e: e.tensor_copy(out=xc[:, :, 0:3], in_=hist[:, q4 * 4:(q4 + 1) * 4, :]), ["histC", "xcC"], ["xcC"])
                            for c in range(4):
                                cc = q4 * 4 + c
                                G(lambda e: e.tensor_scalar(out=acc[:], in0=xc[:, c, 3:131], scalar1=cwT[:, 36 + cc:37 + cc], scalar2=None, op0=ALU.mult), ["xcC", "cwTC"], ["accC"])
                                for tap in range(3):
                                    V(lambda e: e.scalar_tensor_tensor(out=acc[:], in0=xc[:, c, tap:tap + 128], scalar=cwT[:, tap * 12 + cc:tap * 12 + cc + 1], in1=acc[:],
                                                                       op0=ALU.mult, op1=ALU.add), ["xcC", "cwTC", "accC"], ["accC"])
                                A(lambda e: e.activation(out=xact[:, c, :], in_=acc[:], func=AF.Silu), ["accC"], ["xactC"])
                            if i == NT - 1:
                                for c in range(4):
                                    transpose_f32(bank[4][0:3, c * 128:(c + 1) * 128], xc[:, c, 128:131], ["xcC"], [BK[4]])
                                V(lambda e: e.tensor_copy(out=st3[:], in_=bank[4][0:3, :]), [BK[4]], ["st3C"])
                                S.dma("sp", gconv_p[l][:, q4 * 512:(q4 + 1) * 512], st3[:], reads=["st3C"], slot="gconvp")
                            V(lambda e: e.tensor_copy(out=hist[:, q4 * 4:(q4 + 1) * 4, :], in_=xc[:, :, 128:131]), ["xcC"], ["histC"])
                            if q4 < 2:
                                A(lambda e: e.activation(out=sq[:], in_=xact[:], func=AF.Square), ["xactC"], ["sqC"])
                                PE(lambda e: e.matmul(bank[2 + q4][:, :], lhsT=onesf[:], rhs=sq[:].rearrange("p c t -> p (c t)"), start=True, stop=True), ["onesf", "sqC"], [BK[2 + q4]])
                                A(lambda e: e.activation(out=sq[:], in_=bank[2 + q4][:, :].rearrange("p (c t) -> p c t", t=128), func=AF.Sqrt,
                                                         bias=(epsc[:, 2:3] if q4 == 0 else epsc[:, 0:1]), scale=(128.0 if q4 == 0 else 1.0)), [BK[2 + q4], "epsc"], ["sqC"])
                                V(lambda e: e.reciprocal(out=sq[:], in_=sq[:]), ["sqC"], ["sqC"])
                                V(lambda e: e.tensor_tensor(out=qkb[:, q4 * 4:(q4 + 1) * 4, :], in0=xact[:], in1=sq[:], op=ALU.mult), ["xactC", "sqC"], ["qkb"])
                            else:
                                for h in range(4):
                                    transpose_f32(bank[4][:, h * 128:(h + 1) * 128], xact[:, h, :], ["xactC"], [BK[4]])
                                V(lambda e: e.tensor_copy(out=vtok[:], in_=bank[4][:, :].rearrange("p (h d) -> p h d", d=128)), [BK[4]], ["vtok"])
                        def f(e):
                            for h in range(4):
                                ins = e.transpose(out=trb[0][:, h * 128:(h + 1) * 128], in_=qkb[:, 4 + h, :], identity=identb[:])
                            return ins
                        PE(f, ["qkb", "identb"], [TRB[0]])
                        A(lambda e: e.activation(out=ktok[:], in_=trb[0][:, :].rearrange("p (h d) -> p h d", d=128), func=AF.Copy), [TRB[0]], ["ktok"])
                        proj_tok(bank[5][:, 0:8], BK[5], lambda k: hT[:, k, i * 128:(i + 1) * 128], hk_, 1536, 8)
                        A(lambda e: e.activation(out=gv[:, 0:4], in_=bank[5][:, 0:4], func=AF.Sigmoid), [BK[5], "gv"], ["gv"])
                        V(lambda e: e.tensor_tensor(out=gv[:, 4:8], in0=bank[5][:, 4:8], in1=rows[:, 0:4], op=ALU.add), [BK[5], "gv"] + RW, ["gv"])
                        softplus(gv[:, 4:8], gv[:, 4:8], ["gv"], "gv")
                        V(lambda e: e.scalar_tensor_tensor(out=gv[:, 4:8], in0=gv[:, 4:8], scalar=-1.0, in1=rows[:, 8:12], op0=ALU.mult, op1=ALU.mult), ["gv"] + RW, ["gv"])
                        V(lambda e: e.tensor_scalar(out=gv[:, 32:36], in0=gv[:, 0:4], scalar1=-1.0, scalar2=None, op0=ALU.mult), ["gv"], ["gv"])
                        proj_tok(bank[6][:, :], BK[6], lambda k: hT[:, k, i * 128:(i + 1) * 128], hk_, 1544, 512)
                        A(lambda e: e.activation(out=zc[:], in_=bank[6][:, :], func=AF.Silu), [BK[6]], ["zc"])
                        PE(lambda e: e.matmul(bank[5][:, 16:20], lhsT=Umat, rhs=gv[:, 4:8], start=True, stop=True), ["cm", "gv"], [BK[5]])
                        PE(lambda e: e.matmul(bank[5][:, 32:36], lhsT=onesf[:], rhs=gv[:, 4:8], start=True, stop=True), ["onesf", "gv"], [BK[5]])
                        V(lambda e: e.tensor_copy(out=gv[:, 8:12], in_=bank[5][:, 16:20]), [BK[5], "gv"], ["gv"])
                        V(lambda e: e.tensor_scalar(out=gv[:, 12:16], in0=bank[5][:, 16:20], scalar1=-1.0, scalar2=None, op0=ALU.mult), [BK[5], "gv"], ["gv"])
                        A(lambda e: e.activation(out=gv[:, 16:20], in_=bank[5][:, 16:20], func=AF.Exp), [BK[5], "gv"], ["gv"])
                        V(lambda e: e.tensor_scalar(out=gv[:, 20:24], in0=gv[:, 16:20], scalar1=-1.0, scalar2=None, op0=ALU.mult), ["gv"], ["gv"])
                        A(lambda e: e.activation(out=gv[:, 24:28], in_=bank[5][:, 32:36], func=AF.Exp), [BK[5], "gv"], ["gv"])
                        V(lambda e: e.tensor_tensor(out=gv[:, 28:32], in0=bank[5][:, 32:36], in1=gv[:, 8:12], op=ALU.subtract), [BK[5], "gv"], ["gv"])
                        A(lambda e: e.activation(out=gv[:, 28:32], in_=gv[:, 28:32], func=AF.Exp), ["gv"], ["gv"])
                        V(lambda e: e.tensor_tensor(out=kd[:], in0=ktok[:], in1=gv[:, 28:32].unsqueeze(2).to_broadcast([128, 4, 128]), op=ALU.mult), ["ktok", "gv"], ["kd"])
                        for h in range(4):
                            sb_ = 2 + (h % 2)
                            G(lambda e: e.tensor_scalar(out=gU[:], in0=Umat, scalar1=gv[:, 4 + h:5 + h], scalar2=None, op0=ALU.mult), ["cm", "gv"], ["gU"])
                            def f(e):
                                e.matmul(bank[sb_][:, 0:128], lhsT=onesf[:], rhs=gU[:], start=True, stop=False)
                                return e.matmul(bank[sb_][:, 0:128], lhsT=ident, rhs=negmask, start=False, stop=True)
                            PE(f, ["onesf", "gU", "cm"], [BK[sb_]])
                            A(lambda e: e.activation(out=DTm[:, h, :], in_=bank[sb_][:, 0:128], func=AF.Exp, bias=gv[:, 12 + h:13 + h]), [BK[sb_], "gv"], ["DTm%d" % h])
                            transpose_f32(bank[sb_][:, 128:256], DTm[:, h, :], ["DTm%d" % h], [BK[sb_]])
                            V(lambda e: e.tensor_tensor(out=Dm[:], in0=bank[sb_][:, 128:256], in1=strictL, op=ALU.mult), [BK[sb_], "cm"], ["Dm"])
                            PE(lambda e: e.matmul(bank[sb_][:, 256:384], lhsT=qkb[:, 4 + h, :], rhs=qkb[:, 4 + h, :], start=True, stop=True), ["qkb"], [BK[sb_]])
                            V(lambda e: e.scalar_tensor_tensor(out=NX[:, h, :], in0=bank[sb_][:, 256:384], scalar=gv[:, 32 + h:33 + h], in1=Dm[:], op0=ALU.mult, op1=ALU.mult),
                              [BK[sb_], "gv", "Dm"], ["NX"])
                            PE(lambda e: e.matmul(bank[sb_][:, 384:512], lhsT=qkb[:, 4 + h, :], rhs=qkb[:, h, :], start=True, stop=True), ["qkb"], [BK[sb_]])
                            V(lambda e: e.tensor_tensor(out=attT[:, h, :], in0=bank[sb_][:, 384:512], in1=DTm[:, h, :], op=ALU.mult), [BK[sb_], "DTm%d" % h], ["attT"])
                        def f(e):
                            for h in range(4):
                                ins = e.transpose(out=trb[1][:, h * 128:(h + 1) * 128], in_=NX[:, h, :], identity=identb[:])
                            return ins
                        PE(f, ["NX", "identb"], [TRB[1]])
                        A(lambda e: e.activation(out=NY[:], in_=trb[1][:, :].rearrange("p (h d) -> p h d", d=128), func=AF.Copy), [TRB[1]], ["NY"])
                        mk = lambda j: gmask[:, j, :].unsqueeze(1).to_broadcast([128, 4, 128])
                        idb4 = identb[:].unsqueeze(1).to_broadcast([128, 4, 128])
                        V(lambda e: e.tensor_tensor(out=XY[0][:], in0=NX[:], in1=mk(0), op=ALU.mult), ["NX", "gmask"], ["XY0"])
                        G(lambda e: e.tensor_tensor(out=XY[1][:], in0=NY[:], in1=mk(0), op=ALU.mult), ["NY", "gmask"], ["XY1"])
                        V(lambda e: e.tensor_tensor(out=Pn[:], in0=XY[0][:], in1=idb4, op=ALU.add), ["XY0", "identb"], ["Pn"])
                        G(lambda e: e.tensor_tensor(out=Pt[:], in0=XY[1][:], in1=idb4, op=ALU.add), ["XY1", "identb"], ["Pt"])
                        cx, cy = 0, 1
                        def mm4(bk, lh, rh):
                            def f(e):
                                for h in range(4):
                                    ins = e.matmul(bank[bk][:, h * 128:(h + 1) * 128], lhsT=lh[:, h, :], rhs=rh[:, h, :], start=True, stop=True)
                                return ins
                            return f
                        b4 = lambda bk: bank[bk][:, :].rearrange("p (h d) -> p h d", d=128)
                        for stp in range(3):
                            nx_, ny_ = cx + 2, cy + 2
                            if nx_ > 3:
                                nx_, ny_ = nx_ - 4, ny_ - 4
                            PE(mm4(2, XY[cy], XY[cx]), ["XY%d" % cy, "XY%d" % cx], [BK[2]])
                            A(lambda e: e.activation(out=XY[nx_][:], in_=b4(2), func=AF.Copy), [BK[2]], ["XY%d" % nx_])
                            PE(mm4(3, XY[cx], XY[cy]), ["XY%d" % cy, "XY%d" % cx], [BK[3]])
                            A(lambda e: e.activation(out=XY[ny_][:], in_=b4(3), func=AF.Copy), [BK[3]], ["XY%d" % ny_])
                            PE(mm4(4, XY[ny_], Pn), ["XY%d" % ny_, "Pn"], [BK[4]])
                            PE(mm4(1, XY[nx_], Pt), ["XY%d" % nx_, "Pt"], [BK[1]])
                            V(lambda e: e.tensor_tensor(out=Pn[:], in0=b4(4), in1=Pn[:], op=ALU.add), [BK[4], "Pn"], ["Pn"])
                            V(lambda e: e.tensor_tensor(out=Pt[:], in0=b4(1), in1=Pt[:], op=ALU.add), [BK[1], "Pt"], ["Pt"])
                            cx, cy = nx_, ny_
                        YE = XY[0]; Zb = XY[1]
                        for lvl in range(3):
                            G(lambda e: e.tensor_tensor(out=YE[:], in0=NY[:], in1=mk(1 + lvl), op=ALU.mult), ["NY", "gmask"], ["XY0"])
                            PE(mm4(2, YE, Pn), ["XY0", "Pn"], [BK[2]])
                            A(lambda e: e.activation(out=Zb[:], in_=b4(2), func=AF.Copy), [BK[2]], ["XY1"])
                            if lvl < 2:
                                PE(mm4(3, Pt, Zb), ["Pt", "XY1"], [BK[3]])
                            PE(mm4(4, Zb, Pt), ["Pt", "XY1"], [BK[4]])
                            if lvl < 2:
                                V(lambda e: e.tensor_tensor(out=Pn[:], in0=b4(3), in1=Pn[:], op=ALU.add), [BK[3], "Pn"], ["Pn"])
                            V(lambda e: e.tensor_tensor(out=Pt[:], in0=b4(4), in1=Pt[:], op=ALU.add), [BK[4], "Pt"], ["Pt"])
                        TT = Pt; TTk = "Pt"
                        def f(e):
                            for h in range(4):
                                ins = e.matmul(bank[5][:, h * 128:(h + 1) * 128], lhsT=qkb[:, 4 + h, :], rhs=Sgb[:, h, :], start=True, stop=True)
                            return ins
                        PE(f, ["qkb", "Sgb"], [BK[5]])
                        V(lambda e: e.tensor_tensor(out=rr[:], in0=bank[5][:, :].rearrange("p (h d) -> p h d", d=128), in1=gv[:, 20:24].unsqueeze(2).to_broadcast([128, 4, 128]), op=ALU.mult),
                          [BK[5], "gv"], ["rr"])
                        V(lambda e: e.tensor_tensor(out=rr[:], in0=rr[:], in1=vtok[:], op=ALU.add), ["rr", "vtok"], ["rr"])
                        V(lambda e: e.tensor_tensor(out=rb[:], in0=rr[:], in1=gv[:, 0:4].unsqueeze(2).to_broadcast([128, 4, 128]), op=ALU.mult), ["rr", "gv"], ["rb"])
                        def f(e):
                            for h in range(4):
                                ins = e.matmul(bank[6][:, h * 128:(h + 1) * 128], lhsT=TT[:, h, :], rhs=rb[:, h, :], start=True, stop=True)
                            return ins
                        PE(f, [TTk, "rb"], [BK[6]])
                        A(lambda e: e.activation(out=vnb[:], in_=bank[6][:, :].rearrange("p (h d) -> p h d", d=128), func=AF.Copy), [BK[6]], ["vnb"])
                        def f(e):
                            for h in range(4):
                                ins = e.matmul(bank[5][:, h * 128:(h + 1) * 128], lhsT=qkb[:, h, :], rhs=Sgb[:, h, :], start=True, stop=True)
                            return ins
                        PE(f, ["qkb", "Sgb"], [BK[5]])
                        V(lambda e: e.tensor_tensor(out=o1[:], in0=bank[5][:, :].rearrange("p (h d) -> p h d", d=128), in1=gv[:, 16:20].unsqueeze(2).to_broadcast([128, 4, 128]), op=ALU.mult),
                          [BK[5], "gv"], ["rr"])
                        def f(e):
                            for h in range(4):
                                ins = e.matmul(bank[6][:, h * 128:(h + 1) * 128], lhsT=attT[:, h, :], rhs=vnb[:, h, :], start=True, stop=True)
                            return ins
                        PE(f, ["attT", "vnb"], [BK[6]])
                        V(lambda e: e.tensor_tensor(out=o1[:], in0=o1[:], in1=bank[6][:, :].rearrange("p (h d) -> p h d", d=128), op=ALU.add), ["rr", BK[6]], ["rr"])
                        def f(e):
                            for h in range(4):
                                ins = e.matmul(bank[5][:, h * 128:(h + 1) * 128], lhsT=kd[:, h, :], rhs=vnb[:, h, :], start=True, stop=True)
                            return ins
                        PE(f, ["kd", "vnb"], [BK[5]])
                        V(lambda e: e.tensor_tensor(out=Sg[:], in0=Sg[:], in1=gv[:, 24:28].unsqueeze(2).to_broadcast([128, 4, 128]), op=ALU.mult), ["Sg", "gv"], ["Sg"])
                        V(lambda e: e.tensor_tensor(out=Sg[:], in0=Sg[:], in1=bank[5][:, :].rearrange("p (h d) -> p h d", d=128), op=ALU.add), ["Sg", BK[5]], ["Sg"])
                        A(lambda e: e.activation(out=Sgb[:], in_=Sg[:], func=AF.Copy), ["Sg"], ["Sgb"])
                        for h in range(4):
                            A(lambda e: e.activation(out=jk[:], in_=o1[:, h, :], func=AF.Square, accum_out=s4[:, h:h + 1]), ["rr"], ["jkC", "s4C_%d" % h])
                        rstd_from_ss(s4[:, 4:8], s4[:, 0:4], 128, ["s4C_%d" % h for h in range(4)], "s4Cr")
                        for h in range(4):
                            V(lambda e: e.scalar_tensor_tensor(out=o1[:, h, :], in0=o1[:, h, :], scalar=s4[:, 4 + h:5 + h], in1=ngc[:], op0=ALU.mult, op1=ALU.mult), ["rr", "s4Cr", "ngc"], ["rr"])
                        V(lambda e: e.tensor_tensor(out=ob[:], in0=o1[:].rearrange("p h d -> p (h d)"), in1=zc[:], op=ALU.mult), ["rr", "zc"], ["obC"])
                        o_to_T(ob[:], "obC", oTt, "oTt")
                        outproj_update_prompt(i, oTt, "oTt")
                    S.dma("sp", gdn_p[l].rearrange("h k v -> k h v"), Sg[:], reads=["Sg"], slot="gdnp")
                    S.barrier()
            if do_sample:
                sample_finish()

        with ExitStack() as fs:
            junk = T(fs, "junkF", [128, D])
            ssq = T(fs, "ssqF", [128, 2])
            yt = [T(fs, "yt%d" % i, [128, D]) for i in range(2)]
            fin_g = T(fs, "fin_g", [128, D])
            bcast_row(fin_g[:], final_g[0:1, :], D, "fin_g")
            for i in range(NT):
                b = i % 2
                A(lambda e: e.activation(out=junk[:], in_=x_sb[:, i, :], func=AF.Square, accum_out=ssq[:, 0:1]), ["x%d" % i], ["junkF", "ssqF"])
                rstd_from_ss(ssq[:, 1:2], ssq[:, 0:1], D, ["ssqF"], "ssqF")
                V(lambda e: e.scalar_tensor_tensor(out=yt[b][:], in0=x_sb[:, i, :], scalar=ssq[:, 1:2], in1=fin_g[:], op0=ALU.mult, op1=ALU.mult), ["x%d" % i, "ssqF", "fin_g"], ["yt%d" % b])
                S.dma("sp", y_p[i * 128:(i + 1) * 128, :], yt[b][:], reads=["yt%d" % b], slot="yp%d" % b)
            A(lambda e: e.activation(out=junk[0:NO, :], in_=xo[:], func=AF.Square, accum_out=ssq[0:NO, 0:1]), ["xo"], ["junkF", "ssqF"])
            rstd_from_ss(ssq[0:NO, 1:2], ssq[0:NO, 0:1], D, ["ssqF"], "ssqF")
            V(lambda e: e.scalar_tensor_tensor(out=yt[0][0:NO, :], in0=xo[:], scalar=ssq[0:NO, 1:2], in1=fin_g[0:NO, :], op0=ALU.mult, op1=ALU.mult), ["xo", "ssqF", "fin_g"], ["yt0"])
            S.dma("sp", y_s, yt[0][0:NO, :], reads=["yt0"], slot="ys")
            S.finish("sp")
        print("program built: ops=%d waits=%d" % (S.nops, S.nwaits), flush=True)
    return nc


def _bucket(n):
    n = np.asarray(n)
    nn = np.maximum(n, 0)
    exact = 16
    large = exact + (np.log(np.maximum(nn, 1).astype(np.float32) / np.float32(exact)).astype(np.float32)
                     / np.float32(math.log(128 / exact)) * np.float32(32 - exact)).astype(np.int32)
    return np.where(nn < exact, nn, np.minimum(large, 31))


def make_consts(core):
    cm = np.zeros((128, 5, 128), np.float32)
    r = np.arange(128)
    cm[:, 0, :] = np.eye(128)
    cm[:, 1, :] = (r[:, None] <= r[None, :])
    cm[:, 2, :] = np.where(r[:, None] <= r[None, :], 0.0, -1.0e6)
    cm[:, 3, :] = (r[None, :] < r[:, None])
    cm[:, 4, :] = np.eye(128)[::-1]
    oh = np.zeros((33, 512), np.float32)
    n = np.arange(512) - 127
    bk = _bucket(n)
    for i in range(512):
        if n[i] < 0:
            oh[32, i] = 1.0
        else:
            oh[bk[i], i] = 1.0
    p = np.arange(128)
    ohs = np.zeros((32, 128), np.float32)
    ohs[_bucket(128 - p), p] = 1.0
    cv = np.zeros((128, 16), np.float32)
    cv[:, 0] = p
    cv[:, 1] = p + NPOOL * 128
    for g in range(8):
        cv[:, 8 + g] = (p // 16 == g)
    return cm, oh, ohs, cv


def make_selown(core):
    so = np.zeros((NO, NS), np.float32)
    for i in range(NO):
        so[i, (NO * core + i) % NS] = 1.0
    return so


def make_gmask():
    r = np.arange(128)
    bd = lambda b: (r[:, None] // b == r[None, :] // b).astype(np.float32)
    g = np.zeros((128, 4, 128), np.float32)
    g[:, 0] = bd(16); g[:, 1] = bd(32) - bd(16); g[:, 2] = bd(64) - bd(32); g[:, 3] = 1.0 - bd(64)
    return g


_CACHE = {}
_RUN_KW = {}


def kernel(x_prompt, x_sample, c_prompt, c_sample, cache_k, cache_v, state_ssm, state_ssm_conv,
           state_gdn, state_gdn_conv, page_table, w_ada, b_ada, w_in, w_out, rel_bias, attn_lambda,
           attn_subln_g, ssm_conv_w, ssm_conv_b, ssm_dt_bias, ssm_A_log, ssm_D, ssm_norm_g,
           gdn_conv_w, gdn_dt_bias, gdn_A_log, gdn_norm_g, final_norm_g):
    f = lambda a: np.ascontiguousarray(np.asarray(a, dtype=np.float32))
    if "nc" not in _CACHE:
        _CACHE["nc"] = build_program()
    nc = _CACHE["nc"]
    x_prompt = f(x_prompt); x_sample = f(x_sample).reshape(NS, D); c_prompt = f(c_prompt); c_sample = f(c_sample)
    cache_k = np.asarray(cache_k); cache_v = np.asarray(cache_v)
    lamc = np.array([[0.8 - 0.6 * math.exp(-0.3 * 0), 0.8 - 0.6 * math.exp(-0.3 * 1), 0, 0]], np.float32)
    shared = {
        "xs_all": x_sample, "cs_all": c_sample,
        "w_ada": f(w_ada), "b_ada": f(b_ada), "w_in": f(w_in), "w_out": f(w_out), "rel_bias": f(rel_bias),
        "attn_lambda": f(attn_lambda).reshape(2, 256), "subln": f(attn_subln_g),
        "ssm_conv_w": f(ssm_conv_w).reshape(2, 4096), "ssm_conv_b": f(ssm_conv_b), "ssm_dt_bias": f(ssm_dt_bias),
        "ssm_A_log": f(ssm_A_log), "ssm_D": f(ssm_D), "ssm_norm_g": f(ssm_norm_g),
        "gdn_conv_w": f(gdn_conv_w).reshape(2, 6144), "gdn_dt_bias": f(gdn_dt_bias), "gdn_A_log": f(gdn_A_log),
        "gdn_norm_g": f(gdn_norm_g), "final_g": f(final_norm_g).reshape(1, D), "lamc": lamc, "gmask": make_gmask(),
    }
    in_maps = []
    poolk_full = np.ascontiguousarray(cache_k, dtype=np.float32).reshape(-1, 512)
    poolv_full = np.ascontiguousarray(cache_v, dtype=np.float32).reshape(-1, 512)
    for c in range(NCORE):
        cm, oh, ohs, cv = make_consts(c)
        m = dict(shared)
        m.update({
            "xp": x_prompt[c], "cp": c_prompt[c:c + 1],
            "xs_own": x_sample[NO * c:NO * (c + 1)], "cs_own": c_sample[NO * c:NO * (c + 1)],
            "poolk": poolk_full, "poolv": poolv_full,
            "pt": np.ascontiguousarray(np.asarray(page_table, dtype=np.int32)[NO * c:NO * (c + 1)].reshape(1, NO * NPAGE)),
            "st_ssm": f(state_ssm[:, NO * c:NO * (c + 1)]), "st_sconv": f(state_ssm_conv[:, NO * c:NO * (c + 1)]),
            "st_gdn": f(state_gdn[:, NO * c:NO * (c + 1)]), "st_gconv": f(state_gdn_conv[:, NO * c:NO * (c + 1)]),
            "cmat": cm, "oh1d": oh, "ohs": ohs, "cvec": cv, "selown": make_selown(c),
        })
        in_maps.append(m)
    res = run_bass_kernel_spmd(nc, in_maps, core_ids=list(range(NCORE)), **_RUN_KW)
    _CACHE["last_exec_time_ns"] = getattr(res, "exec_time_ns", None)
    R = res.results
    cat = lambda name, ax: np.concatenate([np.asarray(R[c][name])[None] if ax is None else np.asarray(R[c][name]) for c in range(NCORE)], axis=0 if ax is None else ax)
    y_prompt = np.stack([R[c]["y_p"] for c in range(NCORE)], 0)
    y_sample = np.concatenate([R[c]["y_s"] for c in range(NCORE)], 0).reshape(-1, 1, D)
    k_prompt = np.stack([R[c]["k_p"] for c in range(NCORE)], 1).reshape(2, NCORE, L, 4, 128)
    v_prompt = np.stack([R[c]["v_p"] for c in range(NCORE)], 1).reshape(2, NCORE, L, 4, 128)
    ssm_prompt = np.stack([R[c]["ssm_p"] for c in range(NCORE)], 1)
    sconv_prompt = np.stack([R[c]["sconv_p"] for c in range(NCORE)], 1)
    gdn_prompt = np.stack([R[c]["gdn_p"] for c in range(NCORE)], 1)
    gconv_prompt = np.stack([R[c]["gconv_p"] for c in range(NCORE)], 1)
    k_sample = np.concatenate([R[c]["k_s"] for c in range(NCORE)], 1).reshape(2, -1, 1, 4, 128)
    v_sample = np.concatenate([R[c]["v_s"] for c in range(NCORE)], 1).reshape(2, -1, 1, 4, 128)
    ssm_sample = np.concatenate([R[c]["ssm_s"] for c in range(NCORE)], 1)
    sconv_sample = np.concatenate([R[c]["sconv_s"] for c in range(NCORE)], 1)
    gdn_sample = np.concatenate([R[c]["gdn_s"] for c in range(NCORE)], 1)
    gconv_sample = np.concatenate([R[c]["gconv_s"] for c in range(NCORE)], 1)
    outs = (y_prompt, y_sample, k_prompt, v_prompt, ssm_prompt, sconv_prompt, gdn_prompt, gconv_prompt,
            k_sample, v_sample, ssm_sample, sconv_sample, gdn_sample, gconv_sample)
    return tuple(np.ascontiguousarray(o, dtype=np.float32) for o in outs)
```

```python
import math
from contextlib import ExitStack
import numpy as np
import concourse.bass as bass
import concourse.mybir as mybir
from concourse.bass_utils import run_bass_kernel_spmd

F32 = mybir.dt.float32
BF16 = mybir.dt.bfloat16
I32 = mybir.dt.int32
ALU = mybir.AluOpType
AF = mybir.ActivationFunctionType
AX = mybir.AxisListType

NCORE = 8
D = 1024
L = 2048
NT = 16
DEPTH = 2
NS = 32
NO = 4
NPAGE = 64
NPOOL = 2560
TPC = 16
OQ, OK_, OV, OXBC, ODT, OQKV, OBETA, OA, OZ = 0, 512, 1024, 1536, 2560, 2568, 4104, 4108, 4112
NIN = 5648
EPS = 1e-6
RSW = 4104
DBG_TILES = None
DBG_STOP = 0
DBG_A2 = None
DBG_VAR = 0


class Sched:
    ENG = ("pe", "dve", "act", "pool", "sp")

    def __init__(self, nc, stack):
        self.nc = nc
        self.stack = stack
        self.engs = {"pe": nc.tensor, "dve": nc.vector, "act": nc.scalar, "pool": nc.gpsimd, "sp": nc.sync}
        self.sem = {}
        self.cnt = {}
        for e in self.ENG:
            self.sem[e] = stack.enter_context(nc.semaphore("prog_" + e))
            self.cnt[e] = 0
        self.seen = {e: {} for e in self.ENG}
        self.last_write = {}
        self.readers = {}
        self.nops = 0
        self.nwaits = 0
        self.misc_map = {}
        self.excl = set()

    DEDICATED = ("wbuf", "wout", "wa", "kbuf", "vbuf", "kvout", "yp", "qbc", "rsin_", "hst", "ssms", "Sall", "gdns")
    NMISC = 12

    def _slot(self, name, q="sp"):
        if not name.startswith(self.DEDICATED):
            key = (name, q == "pool")
            if key not in self.misc_map:
                n = sum(1 for k in self.misc_map if k[1] == key[1])
                self.misc_map[key] = ("miscp%d" % (n % 3)) if key[1] else ("misc%d" % (n % self.NMISC))
            name = self.misc_map[key]
        k = "d_" + name
        if k not in self.sem:
            self.sem[k] = self.stack.enter_context(self.nc.semaphore(k))
            self.cnt[k] = 0
        return k

    def _wait(self, eng, tok):
        if tok is None:
            return
        k, v = tok
        if self.seen[eng].get(k, 0) >= v:
            return
        if eng == "pe" and k == "pe":
            return
        self.engs[eng].wait_ge(self.sem[k], v)
        self.seen[eng][k] = v
        self.nwaits += 1

    def _deps(self, eng, reads, writes):
        for b in reads:
            self._wait(eng, self.last_write.get(b))
            if b in self.excl:
                for k, v in self.readers.get(b, {}).items():
                    if k != eng:
                        self._wait(eng, (k, v))
        for b in writes:
            self._wait(eng, self.last_write.get(b))
            for k, v in self.readers.get(b, {}).items():
                self._wait(eng, (k, v))

    def _mark(self, tok, reads, writes):
        k, v = tok
        for b in reads:
            r = self.readers.setdefault(b, {})
            r[k] = max(r.get(k, 0), v)
        for b in writes:
            self.last_write[b] = tok
            self.readers[b] = {}

    def op(self, eng, fn, reads=(), writes=()):
        self._deps(eng, reads, writes)
        inst = fn(self.engs[eng])
        self.cnt[eng] += 1
        inst.then_inc(self.sem[eng], 1)
        self._mark((eng, self.cnt[eng]), reads, writes)
        self.nops += 1
        return inst

    def dma(self, q, out, in_, reads=(), writes=(), slot=None, fn=None):
        sk = self._slot(slot, q)
        self._deps(q, reads, writes)
        if self.cnt[sk] > 0:
            self._wait(q, (sk, self.cnt[sk]))
        if fn is None:
            inst = self.engs[q].dma_start(out=out, in_=in_)
        else:
            inst = fn(self.engs[q])
        self.cnt[sk] += 16
        inst.then_inc(self.sem[sk], 16)
        self._mark((sk, self.cnt[sk]), reads, writes)
        self.nops += 1
        return inst

    def dma_group(self, q, items, reads=(), writes=(), slot=None):
        sk = self._slot(slot, q)
        self._deps(q, reads, writes)
        if self.cnt[sk] > 0:
            self._wait(q, (sk, self.cnt[sk]))
        for (o, i) in items:
            inst = self.engs[q].dma_start(out=o, in_=i)
            self.cnt[sk] += 16
            inst.then_inc(self.sem[sk], 16)
        self._mark((sk, self.cnt[sk]), reads, writes)
        self.nops += len(items)

    def barrier(self):
        for e in self.ENG:
            for k in list(self.sem.keys()):
                if k != e and self.cnt[k] > 0:
                    self._wait(e, (k, self.cnt[k]))

    def finish(self, eng="sp"):
        for k in list(self.sem.keys()):
            if self.cnt[k] > 0 and k != eng:
                self._wait(eng, (k, self.cnt[k]))


def build_program(npool=NPOOL, do_sample=True, do_branches="ABC", depth=DEPTH):
    nc = bass.Bass("TRN2", target_bir_lowering=False)

    def din(name, shape, dt=F32):
        return nc.dram_tensor(name, list(shape), dt, kind="ExternalInput").ap()

    def dout(name, shape, dt=F32):
        return nc.dram_tensor(name, list(shape), dt, kind="ExternalOutput").ap()

    xp = din("xp", [L, D]); cp = din("cp", [1, D])
    xs_all = din("xs_all", [NS, D]); cs_all = din("cs_all", [NS, D])
    xs_own = din("xs_own", [NO, D]); cs_own = din("cs_own", [NO, D])
    poolk = din("poolk", [2 * npool * 128, 512]); poolv = din("poolv", [2 * npool * 128, 512])
    pt = din("pt", [1, NO * NPAGE], I32)
    st_ssm = din("st_ssm", [2, NO, 8, 64, 128]); st_sconv = din("st_sconv", [2, NO, 3, 1024])
    st_gdn = din("st_gdn", [2, NO, 4, 128, 128]); st_gconv = din("st_gconv", [2, NO, 3, 1536])
    w_ada = din("w_ada", [2, D, 3 * D]); b_ada = din("b_ada", [2, 3 * D])
    w_in = din("w_in", [2, D, NIN]); w_out = din("w_out", [2, 1536, D])
    rel_bias = din("rel_bias", [32, 4]); attn_lambda = din("attn_lambda", [2, 256])
    subln = din("subln", [2, 128])
    ssm_conv_w = din("ssm_conv_w", [2, 4 * 1024]); ssm_conv_b = din("ssm_conv_b", [2, 1024])
    ssm_dt_bias = din("ssm_dt_bias", [2, 8]); ssm_A_log = din("ssm_A_log", [2, 8]); ssm_D = din("ssm_D", [2, 8])
    ssm_norm_g = din("ssm_norm_g", [2, 512])
    gdn_conv_w = din("gdn_conv_w", [2, 4 * 1536]); gdn_dt_bias = din("gdn_dt_bias", [2, 4])
    gdn_A_log = din("gdn_A_log", [2, 4]); gdn_norm_g = din("gdn_norm_g", [2, 128])
    final_g = din("final_g", [1, D])
    cmat = din("cmat", [128, 5, 128])
    oh1d = din("oh1d", [33, 512])
    ohs = din("ohs", [32, 128])
    cvec = din("cvec", [128, 16])
    lamc = din("lamc", [1, 4])
    gmask_d = din("gmask", [128, 4, 128])
    selown_d = din("selown", [NO, NS])

    y_p = dout("y_p", [L, D]); y_s = dout("y_s", [NO, D])
    k_p = dout("k_p", [2, L, 512]); v_p = dout("v_p", [2, L, 512])
    ssm_p = dout("ssm_p", [2, 8, 64, 128]); sconv_p = dout("sconv_p", [2, 3, 1024])
    gdn_p = dout("gdn_p", [2, 4, 128, 128]); gconv_p = dout("gconv_p", [2, 3, 1536])
    k_s = dout("k_s", [2, NO, 512]); v_s = dout("v_s", [2, NO, 512])
    ssm_s = dout("ssm_s", [2, NO, 8, 64, 128]); sconv_s = dout("sconv_s", [2, NO, 3, 1024])
    gdn_s = dout("gdn_s", [2, NO, 4, 128, 128]); gconv_s = dout("gconv_s", [2, NO, 3, 1536])

    bvec_d = nc.dram_tensor("bvec_d", [4, 512], F32)
    rs_out = [nc.dram_tensor("rs_loc%d" % l, [NO, RSW], F32) for l in range(2)]
    ag_in = nc.dram_tensor("ag_in", [NS, D], F32)
    ag_out = nc.dram_tensor("ag_out", [NS, D], F32)
    q_d = [nc.dram_tensor("q_d%d" % l, [NO, 512], F32) for l in range(2)]
    gate_d = nc.dram_tensor("gate_d", [NO, D], F32)

    with ExitStack() as st:
        S = Sched(nc, st)
        cc_sem = st.enter_context(nc.semaphore("cc_sem"))
        cc_cnt = [0]

        _uid = [0]

        def T(stack, name, shape, dt=F32):
            _uid[0] += 1
            return stack.enter_context(nc.sbuf_tensor("%s_%d" % (name, _uid[0]), list(shape), dt))

        def V(fn, r=(), w=()): return S.op("dve", fn, r, w)
        def A(fn, r=(), w=()): return S.op("act", fn, r, w)
        def G(fn, r=(), w=()): return S.op("pool", fn, r, w)
        def PE(fn, r=(), w=()): return S.op("pe", fn, r, w)

        bank = [st.enter_context(nc.psum_tensor("bk%d" % i, [128, 512], F32)) for i in range(7)]
        trbank = st.enter_context(nc.psum_tensor("trbank", [128, 512], F32))
        _trv = trbank[:, :].bitcast(BF16)
        trb = [_trv[:, 0:512], _trv[:, 512:1024]]
        BK = ["bk%d" % i for i in range(7)]
        TRB = ["trb", "trb"]
        S.excl = set(BK) | {"trb"}

        x_sb = T(st, "x_sb", [128, NT, D])
        hT = T(st, "hT", [128, 8, L], BF16)
        wbuf = T(st, "wbuf", [128, 8, 2056], BF16)
        wout = T(st, "wout", [128, 4, D], BF16)
        cm = T(st, "cm", [128, 5, 128])
        identb = T(st, "identb", [128, 128], BF16)
        onesb = T(st, "onesb", [128, 128], BF16)
        onesf = T(st, "onesf", [128, 128])
        BT = T(st, "BT", [128, 4, 256])
        c31 = T(st, "c31", [128, 4])
        cv = T(st, "cv", [128, 16])
        lamc_sb = T(st, "lamc_sb", [128, 4])
        xo = T(st, "xo", [NO, D])
        cT_p = T(st, "cT_p", [128, 8, 1], BF16)
        cT_o = T(st, "cT_o", [128, 8, NO], BF16)
        cT_a = T(st, "cT_a", [128, 8, NS], BF16)
        sel4 = T(st, "sel4", [NO, NO, 128])
        eye4 = T(st, "eye4", [128, NO, NO])
        idx = T(st, "idx", [128, 2, NO * NPAGE], I32)
        bias_sm = T(st, "bias_sm", [128, NPAGE, 8])
        b0row = T(st, "b0row", [NO, 4])
        hTo = T(st, "hTo", [128, 8, NO], BF16)
        hTa = T(st, "hTa", [128, 8, NS], BF16)
        scT = T(st, "scT", [128, 16])
        gate_bc = T(st, "gate_bc", [128, D])
        lamv = T(st, "lamv", [128, 4])
        sgl = T(st, "sgl", [128, 128])
        ident = cm[:, 0, :]
        Umat = cm[:, 1, :]
        negmask = cm[:, 2, :]
        strictL = cm[:, 3, :]
        Jmat = cm[:, 4, :]

        def bcast_row(dst, src_row_ap, n, key, q="sp"):
            P = dst.shape[0]
            S.dma(q, dst, src_row_ap.to_broadcast([P, n]), writes=[key], slot=key)

        def transpose_f32(dst_psum, src, rk, wk, rows=None):
            PE(lambda e: e.transpose(out=dst_psum, in_=src, identity=ident if rows is None else cm[0:rows, 0, 0:rows]),
               list(rk) + ["cm"], wk)

        def rstd_from_ss(dst, ss, n, keys_r, key_w, shape_eng="act"):
            A(lambda e: e.activation(out=dst, in_=ss, func=AF.Sqrt, scale=1.0 / n, bias=epsc[0:dst.shape[0], 0:1]), list(keys_r) + ["epsc"], [key_w])
            V(lambda e: e.reciprocal(out=dst, in_=dst), [key_w], [key_w])

        def softplus(dst, src, kr, kw):
            A(lambda e: e.activation(out=dst, in_=src, func=AF.Exp), kr, [kw])
            A(lambda e: e.activation(out=dst, in_=dst, func=AF.Ln, bias=onec[0:dst.shape[0], 0:1]), [kw, "epsc"], [kw])

        def collective(kind, op, in_t, out_t, rkeys, wkeys):
            for k in rkeys:
                S._wait("pool", S.last_write.get(k))
            for k in wkeys:
                S._wait("pool", S.last_write.get(k))
                for kk, vv in S.readers.get(k, {}).items():
                    S._wait("pool", (kk, vv))
            cc_cnt[0] += 1
            nc.gpsimd.collective_compute(kind, op, replica_groups=[list(range(NCORE))],
                                         ins=[in_t.ap().opt()], outs=[out_t.ap().opt()]).then_inc(cc_sem, 1)
            nc.gpsimd.wait_ge(cc_sem, cc_cnt[0])
            G(lambda e: e.memset(ccj[:], 0.0), [], ["ccj"] + list(wkeys))

        print("sbuf bytes remaining after persistent:", nc.sbuf_bytes_remaining, flush=True)
        epsc = T(st, "epsc", [128, 3])
        onec = epsc[:, 1:2]
        ccj = T(st, "ccj", [1, 2])
        V(lambda e: e.memset(epsc[:, 0:1], EPS), [], ["epsc"])
        V(lambda e: e.memset(epsc[:, 1:2], 1.0), ["epsc"], ["epsc"])
        V(lambda e: e.memset(epsc[:, 2:3], 128.0 * EPS), ["epsc"], ["epsc"])
        S.dma("sp", cm[:], cmat, writes=["cm"], slot="cm")
        S.dma("sp", cv[:], cvec, writes=["cv"], slot="cv")
        bcast_row(lamc_sb[:], lamc[0:1, :], 4, "lamc_sb")
        bcast_row(c31[:], rel_bias[31:32, :], 4, "c31")
        bcast_row(b0row[:], rel_bias[0:1, :], 4, "b0row")
        V(lambda e: e.tensor_copy(out=identb[:], in_=ident), ["cm"], ["identb"])
        V(lambda e: e.memset(onesb[:], 1.0), [], ["onesb"])
        V(lambda e: e.memset(onesf[:], 1.0), [], ["onesf"])
        V(lambda e: e.tensor_copy(out=sel4[:], in_=cm[0:NO, 0, 0:NO].unsqueeze(2).to_broadcast([NO, NO, 128])), ["cm"], ["sel4"])
        with ExitStack() as cs:
            tb = T(cs, "tb_tab", [33, 4])
            oh = T(cs, "oh_sb", [33, 512])
            bv = T(cs, "bv_sb", [4, 512])
            hk = T(cs, "hk_sb", [128, 4, 384])
            ohs_sb = T(cs, "ohs_sb", [32, 128])
            i4 = T(cs, "i4", [NO, NO * NO])
            V(lambda e: e.memset(i4[:], 0.0), [], ["i4"])
            for j in range(NO):
                V(lambda e: e.tensor_copy(out=i4[:, j * NO + j:j * NO + j + 1], in_=onesf[0:NO, 0:1]), ["i4", "onesf"], ["i4"])
            PE(lambda e: e.matmul(bank[0][:, 0:16], lhsT=sel4[:, 0, :], rhs=i4[:], start=True, stop=True), ["sel4", "i4"], [BK[0]])
            V(lambda e: e.tensor_copy(out=eye4[:].rearrange("p a b -> p (a b)"), in_=bank[0][:, 0:16]), [BK[0]], ["eye4"])
            V(lambda e: e.memset(tb[:], -30000.0), [], ["tb"])
            S.dma("sp", tb[0:32, :], rel_bias, reads=["tb"], writes=["tb"], slot="tb")
            S.dma("sp", oh[:], oh1d, writes=["oh"], slot="oh")
            S.dma("sp", ohs_sb[:], ohs, writes=["ohs"], slot="ohs")
            PE(lambda e: e.matmul(bank[1][0:4, :], lhsT=tb[:], rhs=oh[:], start=True, stop=True), ["tb", "oh"], [BK[1]])
            V(lambda e: e.tensor_copy(out=bv[:], in_=bank[1][0:4, :]), [BK[1]], ["bv"])
            S.dma("sp", bvec_d.ap(), bv[:], reads=["bv"], writes=["bvec_d"], slot="bvd")
            hsrc = bass.AP(tensor=bvec_d.ap().tensor, offset=0, ap=[[1, 128], [512, 4], [1, 384]])
            S.dma("sp", hk[:], hsrc, reads=["bvec_d"], writes=["hk"], slot="hk")
            for h in range(4):
                PE(lambda e: e.matmul(bank[2][:, 0:384], lhsT=Jmat, rhs=hk[:, h, :], start=True, stop=True), ["cm", "hk"], [BK[2]])
                V(lambda e: e.tensor_copy(out=BT[:, h, :], in_=bank[2][:, 0:256]), [BK[2]], ["BT"])
            for m_ in range(2):
                V(lambda e: e.tensor_copy(out=bias_sm[:].rearrange("p t (h m) -> p t h m", m=2)[:, :, :, m_], in_=c31[:].unsqueeze(1).to_broadcast([128, NPAGE, 4])), ["c31", "bias_sm"], ["bias_sm"])
            PE(lambda e: e.matmul(bank[3][:, 0:4], lhsT=ohs_sb[:, :], rhs=tb[0:32, :], start=True, stop=True), ["ohs", "tb"], [BK[3]])
            V(lambda e: e.tensor_copy(out=bias_sm[:, NPAGE - 1, :].rearrange("p (h m) -> p h m", m=2), in_=bank[3][:, 0:4].unsqueeze(2).to_broadcast([128, 4, 2])),
              [BK[3], "bias_sm"], ["bias_sm"])
            S.barrier()
        with ExitStack() as cs:
            if do_sample:
                ptb = T(cs, "ptb", [128, NO * NPAGE], I32)
                ptf = T(cs, "ptf", [128, NO * NPAGE])
                ptg = T(cs, "ptg", [128, NO * NPAGE])
                S.dma("sp", ptb[:], pt[0:1, :].to_broadcast([128, NO * NPAGE]), writes=["ptb"], slot="ptb")
                V(lambda e: e.tensor_copy(out=ptf[:], in_=ptb[:]), ["ptb"], ["ptf"])
                for l in range(2):
                    V(lambda e: e.tensor_scalar(out=ptg[:], in0=ptf[:], scalar1=128.0, scalar2=cv[:, l:l + 1], op0=ALU.mult, op1=ALU.add), ["ptf", "cv", "ptg"], ["ptg"])
                    V(lambda e: e.tensor_copy(out=idx[:, l, :], in_=ptg[:]), ["ptg"], ["idx"])
            S.barrier()
        with ExitStack() as cs:
            craw = T(cs, "craw", [NS, D])
            csil = T(cs, "csil", [NS, D], BF16)
            for (src, n, dst, nm) in ((cp, 1, cT_p, "cT_p"), (cs_own, NO, cT_o, "cT_o"), (cs_all, NS, cT_a, "cT_a")):
                S.dma("sp", craw[0:n, :], src, writes=["craw"], slot="craw")
                A(lambda e: e.activation(out=csil[0:n, :], in_=craw[0:n, :], func=AF.Silu), ["craw"], ["csil"])
                for k in range(8):
                    PE(lambda e: e.transpose(out=trb[0][:, k * NS:k * NS + n], in_=csil[0:n, k * 128:(k + 1) * 128], identity=identb[0:n, 0:n]),
                       ["csil", "identb"], [TRB[0]])
                V(lambda e: e.tensor_copy(out=dst[:], in_=trb[0][:, 0:8 * NS].rearrange("p (k s) -> p k s", s=NS)[:, :, 0:n]), [TRB[0]], [nm])
            S.barrier()

        for i in range(NT):
            S.dma("sp", x_sb[:, i, :], xp[i * 128:(i + 1) * 128, :], writes=["x%d" % i], slot="x%d" % i)
        S.dma("sp", xo[:], xs_own, writes=["xo"], slot="xo")

        def load_w(l, segs, key="wbuf"):
            items = []
            o = 0
            for (c0, n) in segs:
                for k in range(8):
                    items.append((wbuf[:, k, o:o + n], w_in[l, k * 128:(k + 1) * 128, c0:c0 + n]))
                o += n
            S.dma_group("pool", items, writes=[key], slot="wbuf")

        def load_wout(l, r0):
            items = [(wout[:, k, :], w_out[l, r0 + k * 128:r0 + (k + 1) * 128, :]) for k in range(4)]
            S.dma_group("pool", items, writes=["wout"], slot="wout")

        def proj_tok(dst_psum, dkey, lhs_fn, lkeys, wcol0, ncols, M=128):
            def f(e):
                for k in range(8):
                    i = e.matmul(dst_psum, lhsT=lhs_fn(k), rhs=wbuf[:, k, wcol0:wcol0 + ncols], start=(k == 0), stop=(k == 7))
                return i
            PE(f, list(lkeys) + ["wbuf"], [dkey])

        def proj_feat(dst_psum, dkey, wcol0, rhs_fn, rkeys, mcols=128):
            def f(e):
                for k in range(8):
                    i = e.matmul(dst_psum, lhsT=wbuf[:, k, wcol0:wcol0 + mcols], rhs=rhs_fn(k), start=(k == 0), stop=(k == 7))
                return i
            PE(f, list(rkeys) + ["wbuf"], [dkey])

        def outproj_update_prompt(i, oT, okey, banks=(5, 6)):
            for half in range(2):
                bk = banks[half]
                def f(e):
                    for k in range(4):
                        ins = e.matmul(bank[bk][:, :], lhsT=oT[:, k, :], rhs=wout[:, k, half * 512:(half + 1) * 512], start=(k == 0), stop=(k == 3))
                    return ins
                PE(f, [okey, "wout"], [BK[bk]])
                V(lambda e: e.tensor_tensor(out=optmp[:, :], in0=bank[bk][:, :], in1=gate_bc[:, half * 512:(half + 1) * 512], op=ALU.mult),
                  [BK[bk], "gate_bc"], ["optmp"])
                G(lambda e: e.tensor_tensor(out=x_sb[:, i, half * 512:(half + 1) * 512], in0=x_sb[:, i, half * 512:(half + 1) * 512], in1=optmp[:, :], op=ALU.add),
                  ["optmp", "x%d" % i], ["x%d" % i])

        def o_to_T(o_bf, okey, oT, oTkey, M=128):
            def f(e):
                for k in range(4):
                    ins = e.transpose(out=trb[1][:, k * 128:k * 128 + M], in_=o_bf[:, k * 128:(k + 1) * 128], identity=identb[0:M, 0:M])
                return ins
            PE(f, [okey, "identb"], [TRB[1]])
            A(lambda e: e.activation(out=oT[:, :, 0:M], in_=trb[1][:, :].rearrange("p (k t) -> p k t", t=128)[:, :, 0:M], func=AF.Copy), [TRB[1]], [oTkey])

        optmp = T(st, "optmp", [128, 512])
        oTt = T(st, "oTt", [128, 4, 128], BF16)
        oTs = T(st, "oTs", [128, 12, NO], BF16)

        def outproj_sample(l, gstack):
            gate_o = T(gstack, "gate_o", [NO, D])
            S.dma("sp", gate_o[:], gate_d.ap(), reads=["gate_d"], writes=["gate_o"], slot="gate_o")
            for br in range(3):
                load_wout(l, br * 512)
                for half in range(2):
                    def f(e):
                        for k in range(4):
                            ins = e.matmul(bank[5 + half][0:NO, :], lhsT=oTs[:, br * 4 + k, :], rhs=wout[:, k, half * 512:(half + 1) * 512], start=(k == 0), stop=(k == 3))
                        return ins
                    PE(f, ["oTs", "wout"], [BK[5 + half]])
                    V(lambda e: e.tensor_tensor(out=optmp[0:NO, :], in0=bank[5 + half][0:NO, :], in1=gate_o[:, half * 512:(half + 1) * 512], op=ALU.mult),
                      [BK[5 + half], "gate_o"], ["optmp"])
                    V(lambda e: e.tensor_tensor(out=xo[:, half * 512:(half + 1) * 512], in0=xo[:, half * 512:(half + 1) * 512], in1=optmp[0:NO, :], op=ALU.add),
                      ["optmp", "xo"], ["xo"])

        for l in range(depth):
            lam_init = 0.8 - 0.6 * math.exp(-0.3 * l)
            with ExitStack() as ps:
                wa = T(ps, "wa", [128, 8, 512], BF16)
                brow = T(ps, "brow", [NS, 512])
                mp = T(ps, "mp", [1, 512])
                mo_ss = T(ps, "mo_ss", [NO, 2 * D])
                gst = T(ps, "gst", [NO, 512])
                moda = T(ps, "moda", [NS, 2 * D])
                lv = T(ps, "lv", [128, 256])
                lvp = T(ps, "lvp", [128, 128])
                l2 = T(ps, "l2", [128, 2])
                sg = T(ps, "sg", [128, 128])
                ssq = T(ps, "ssq", [128, 2])
                bcast_row(lv[:], attn_lambda[l:l + 1, :], 256, "lv")
                V(lambda e: e.tensor_tensor(out=lvp[:].rearrange("p (a d) -> p a d", a=2), in0=lv[:].rearrange("p (a b d) -> p a b d", a=2, b=2)[:, :, 0, :],
                                            in1=lv[:].rearrange("p (a b d) -> p a b d", a=2, b=2)[:, :, 1, :], op=ALU.mult), ["lv"], ["lvp"])
                V(lambda e: e.tensor_reduce(out=l2[:], in_=lvp[:].rearrange("p (a d) -> p a d", a=2), axis=AX.X, op=ALU.add), ["lvp"], ["l2"])
                A(lambda e: e.activation(out=l2[:], in_=l2[:], func=AF.Exp), ["l2"], ["l2"])
                V(lambda e: e.tensor_tensor(out=lamv[:, 0:1], in0=l2[:, 0:1], in1=l2[:, 1:2], op=ALU.subtract), ["l2"], ["lamv"])
                V(lambda e: e.tensor_scalar(out=lamv[:, 0:1], in0=lamv[:, 0:1], scalar1=float(lam_init), scalar2=None, op0=ALU.add), ["lamv"], ["lamv"])
                V(lambda e: e.tensor_scalar(out=lamv[:, 1:2], in0=lamv[:, 0:1], scalar1=-1.0, scalar2=None, op0=ALU.mult), ["lamv"], ["lamv"])
                bcast_row(sg[:], subln[l:l + 1, :], 128, "sg")
                V(lambda e: e.tensor_scalar(out=sgl[:], in0=sg[:], scalar1=float(1.0 - lam_init), scalar2=None, op0=ALU.mult), ["sg"], ["sgl"])
                for cch in range(6):
                    S.dma_group("pool", [(wa[:, k, :], w_ada[l, k * 128:(k + 1) * 128, cch * 512:(cch + 1) * 512]) for k in range(8)], writes=["wa"], slot="wa")
                    bcast_row(brow[:], b_ada[l:l + 1, cch * 512:(cch + 1) * 512], 512, "brow")
                    def f(e):
                        for k in range(8):
                            ins = e.matmul(bank[0][0:1, :], lhsT=cT_p[:, k, :], rhs=wa[:, k, :], start=(k == 0), stop=(k == 7))
                        return ins
                    PE(f, ["wa", "cT_p"], [BK[0]])
                    V(lambda e: e.tensor_tensor(out=mp[:], in0=bank[0][0:1, :], in1=brow[0:1, :], op=ALU.add), [BK[0], "brow"], ["mp"])
                    if cch < 4:
                        def f(e):
                            for j in range(4):
                                ins = e.transpose(out=bank[3][:, cch * 4 + j:cch * 4 + j + 1], in_=mp[0:1, j * 128:(j + 1) * 128], identity=cm[0:1, 0, 0:1])
                            return ins
                        PE(f, ["mp", "cm"], [BK[3]])
                    else:
                        PE(lambda e: e.matmul(bank[4][:, :], lhsT=onesf[0:1, :], rhs=mp[0:1, :], start=True, stop=True), ["onesf", "mp"], [BK[4]])
                        V(lambda e: e.tensor_copy(out=gate_bc[:, (cch - 4) * 512:(cch - 3) * 512], in_=bank[4][:, :]), [BK[4]], ["gate_bc"])
                    def f(e):
                        for k in range(8):
                            ins = e.matmul(bank[1][0:NO, :], lhsT=cT_o[:, k, :], rhs=wa[:, k, :], start=(k == 0), stop=(k == 7))
                        return ins
                    PE(f, ["wa", "cT_o"], [BK[1]])
                    if cch < 4:
                        V(lambda e: e.tensor_tensor(out=mo_ss[:, cch * 512:(cch + 1) * 512], in0=bank[1][0:NO, :], in1=brow[0:NO, :], op=ALU.add), [BK[1], "brow"], ["mo_ss"])
                    else:
                        V(lambda e: e.tensor_tensor(out=gst[:], in0=bank[1][0:NO, :], in1=brow[0:NO, :], op=ALU.add), [BK[1], "brow"], ["gst"])
                        S.dma("sp", gate_d[:, (cch - 4) * 512:(cch - 3) * 512], gst[:], reads=["gst"], writes=["gate_d"], slot="gate_d")
                V(lambda e: e.tensor_copy(out=scT[:], in_=bank[3][:, 0:16]), [BK[3]], ["scT"])
                V(lambda e: e.tensor_scalar(out=scT[:, 8:16], in0=scT[:, 8:16], scalar1=1.0, scalar2=None, op0=ALU.add), ["scT"], ["scT"])
                V(lambda e: e.tensor_scalar(out=mo_ss[:, D:2 * D], in0=mo_ss[:, D:2 * D], scalar1=1.0, scalar2=None, op0=ALU.add), ["mo_ss"], ["mo_ss"])
                S.barrier()
                with ExitStack() as ps2:
                    xall = T(ps2, "xall", [NS, D])
                    hs = T(ps2, "hs_s", [NS, D])
                    hsb = T(ps2, "hsb", [NS, D], BF16)
                    if do_sample:
                        for (xs, xk, n, md, mk, dst, dk) in ((xo, "xo", NO, mo_ss, "mo_ss", hTo, "hTo"),):
                            A(lambda e: e.activation(out=hs[0:n, :], in_=xs[:], func=AF.Square, accum_out=ssq[0:n, 0:1]), [xk], ["hs", "ssq"])
                            rstd_from_ss(ssq[0:n, 1:2], ssq[0:n, 0:1], D, ["ssq"], "ssq")
                            V(lambda e: e.tensor_scalar(out=hs[0:n, :], in0=xs[:], scalar1=ssq[0:n, 1:2], scalar2=None, op0=ALU.mult), [xk, "ssq"], ["hs"])
                            V(lambda e: e.tensor_tensor(out=hs[0:n, :], in0=hs[0:n, :], in1=md[:, D:2 * D], op=ALU.mult), ["hs", mk], ["hs"])
                            V(lambda e: e.tensor_tensor(out=hsb[0:n, :], in0=hs[0:n, :], in1=md[:, 0:D], op=ALU.add), ["hs", mk], ["hsb"])
                            def f(e):
                                for k in range(8):
                                    ins = e.transpose(out=trb[0][:, k * NS:k * NS + n], in_=hsb[0:n, k * 128:(k + 1) * 128], identity=identb[0:n, 0:n])
                                return ins
                            PE(f, ["hsb", "identb"], [TRB[0]])
                            V(lambda e: e.tensor_copy(out=dst[:], in_=trb[0][:, 0:8 * NS].rearrange("p (k s) -> p k s", s=NS)[:, :, 0:n]), [TRB[0]], [dk])
                    S.barrier()
            with ExitStack() as ps:
                hn = T(ps, "hn", [128, D], BF16)
                hs = T(ps, "hs", [128, D])
                ssq = T(ps, "ssq2", [128, 2])
                for i in range(NT):
                    A(lambda e: e.activation(out=hs[:], in_=x_sb[:, i, :], func=AF.Square, accum_out=ssq[:, 0:1]), ["x%d" % i], ["hs", "ssq"])
                    rstd_from_ss(ssq[:, 1:2], ssq[:, 0:1], D, ["ssq"], "ssq")
                    V(lambda e: e.tensor_scalar(out=hn[:], in0=x_sb[:, i, :], scalar1=ssq[:, 1:2], scalar2=None, op0=ALU.mult), ["x%d" % i, "ssq"], ["hn"])
                    for half in range(2):
                        def f(e):
                            for k in range(4):
                                kk = half * 4 + k
                                ins = e.transpose(out=trb[half][:, k * 128:(k + 1) * 128], in_=hn[:, kk * 128:(kk + 1) * 128], identity=identb[:])
                            return ins
                        PE(f, ["hn", "identb"], [TRB[half]])
                        for k in range(4):
                            kk = half * 4 + k
                            A(lambda e: e.activation(out=hT[:, kk, i * 128:(i + 1) * 128], in_=trb[half][:, k * 128:(k + 1) * 128], func=AF.Identity,
                                                     scale=scT[:, 8 + kk:9 + kk], bias=scT[:, kk:kk + 1]), [TRB[half], "scT"], ["hT%d" % i])
                S.barrier()

            def proj_rows(dst, dkey, segs):
                for (w0, n, d0) in segs:
                    proj_tok(bank[0][0:NO, 0:n], BK[0], lambda k: hTo[:, k, :], ["hTo"], w0, n)
                    V(lambda e: e.tensor_copy(out=dst[:, d0:d0 + n], in_=bank[0][0:NO, 0:n]), [BK[0]], [dkey])

            def gated_norm_rows(ss, o3, okey, ng, width, gvec, gkey, zs_ap, zkey, br):
                tq = T(ss, "gn_tq%d" % br, [NO, 512])
                t4 = T(ss, "gn_t4%d" % br, [NO, 8])
                ob4 = T(ss, "gn_ob%d" % br, [NO, 512], BF16)
                tq3 = tq[:, :].rearrange("s (g d) -> s g d", d=width)
                V(lambda e: e.tensor_tensor(out=tq3, in0=o3, in1=o3, op=ALU.mult), [okey], ["gn_tq"])
                V(lambda e: e.tensor_reduce(out=t4[:, 0:ng], in_=tq3, axis=AX.X, op=ALU.add), ["gn_tq"], ["gn_t4"])
                rstd_from_ss(t4[:, 0:ng], t4[:, 0:ng], width, ["gn_t4"], "gn_t4")
                V(lambda e: e.tensor_tensor(out=tq3, in0=o3, in1=t4[:, 0:ng].unsqueeze(2).to_broadcast([NO, ng, width]), op=ALU.mult), [okey, "gn_t4"], ["gn_tq"])
                V(lambda e: e.tensor_tensor(out=tq3, in0=tq3, in1=gvec, op=ALU.mult), ["gn_tq", gkey], ["gn_tq"])
                if zs_ap is not None:
                    V(lambda e: e.tensor_tensor(out=ob4[:], in0=tq[:], in1=zs_ap, op=ALU.mult), ["gn_tq", zkey], ["gn_ob"])
                else:
                    V(lambda e: e.tensor_copy(out=ob4[:], in_=tq[:]), ["gn_tq"], ["gn_ob"])
                def f(e):
                    for k in range(4):
                        ins = e.transpose(out=trb[0][:, k * NO:(k + 1) * NO], in_=ob4[:, k * 128:(k + 1) * 128], identity=identb[0:NO, 0:NO])
                    return ins
                PE(f, ["gn_ob", "identb"], [TRB[0]])
                V(lambda e: e.tensor_copy(out=oTs[:, br * 4:(br + 1) * 4, :].rearrange("p k s -> p (k s)"), in_=trb[0][:, 0:4 * NO]), [TRB[0]], ["oTs"])

            def sample_attn():
              with ExitStack() as stack2:
                pj = T(stack2, "pjA", [NO, 2048])
                with ExitStack() as ss:
                    qall = T(ss, "qall", [NO, 512])
                    proj_rows(pj, "pjA", [(0, 512, 0), (512, 512, 512), (1024, 512, 1024), (1536, 512, 1536)])
                    A(lambda e: e.activation(out=qall[:], in_=pj[:, 0:512], func=AF.Copy, scale=0.125), ["pjA"], ["qall"])
                    S.dma("sp", q_d[l].ap(), qall[:], reads=["qall"], writes=["q_d"], slot="q_d")
                    S.dma("sp", k_s[l], pj[:, 512:1024], reads=["pjA"], slot="ks")
                    S.dma("sp", v_s[l], pj[:, 1024:1536], reads=["pjA"], slot="vs")
                    NSL = 8
                    kbuf = [T(ss, "kbuf%d" % i, [128, 512], BF16) for i in range(NSL)]
                    vbuf = [T(ss, "vbuf%d" % i, [128, 512], BF16) for i in range(8)]
                    qbc = [T(ss, "qbc%d" % i, [128, 512]) for i in range(2)]
                    prod = [T(ss, "prod%d" % i, [128, 512]) for i in range(2)]
                    lg = [T(ss, "lg%d" % i, [128, 8, 8]) for i in range(2)]
                    pexp = [T(ss, "pexp%d" % i, [128, 8, 8], BF16) for i in range(2)]
                    osb = [T(ss, "osb%d" % i, [8, 512]) for i in range(2)]
                    ssb = [T(ss, "ssb%d" % i, [1, 8]) for i in range(2)]
                    gi = 0
                    for sv in range(NO * 8):
                        s, g8 = sv // 8, sv % 8
                        b2 = s % 2
                        if g8 == 0:
                            S.dma("sp", qbc[b2][:], q_d[l][s:s + 1, :].to_broadcast([128, 512]), reads=["q_d"], writes=["qbc%d" % b2], slot="qbc%d" % b2)
                        vslots = []
                        for t in range(8):
                            sl = gi % NSL
                            vsl = gi % 8
                            gi += 1
                            vslots.append(vsl)
                            col = sv * 8 + t
                            S.dma("pool", None, None, writes=["kbuf%d" % sl], slot="kbuf%d" % sl, reads=["idx"],
                                  fn=lambda e: e.indirect_dma_start(out=kbuf[sl][:], out_offset=None, in_=poolk[:, :],
                                                                    in_offset=bass.IndirectOffsetOnAxis(ap=idx[:, l, col:col + 1], axis=0)))
                            S.dma("pool", None, None, writes=["vbuf%d" % vsl], slot="vbuf%d" % vsl, reads=["idx"],
                                  fn=lambda e: e.indirect_dma_start(out=vbuf[vsl][:], out_offset=None, in_=poolv[:, :],
                                                                    in_offset=bass.IndirectOffsetOnAxis(ap=idx[:, l, col:col + 1], axis=0)))
                            pb = t % 2
                            V(lambda e: e.tensor_tensor(out=prod[pb][:], in0=kbuf[sl][:], in1=qbc[b2][:], op=ALU.mult), ["kbuf%d" % sl, "qbc%d" % b2], ["prod%d" % pb])
                            V(lambda e: e.tensor_reduce(out=lg[b2][:, t, :], in_=prod[pb][:].rearrange("p (a d) -> p a d", d=64), axis=AX.X, op=ALU.add),
                              ["prod%d" % pb], ["lg%d" % b2])
                        V(lambda e: e.tensor_tensor(out=lg[b2][:], in0=lg[b2][:], in1=bias_sm[:, g8 * 8:(g8 + 1) * 8, :], op=ALU.add), ["lg%d" % b2, "bias_sm"], ["lg%d" % b2])
                        A(lambda e: e.activation(out=pexp[b2][:], in_=lg[b2][:], func=AF.Exp), ["lg%d" % b2], ["pexp%d" % b2])
                        def f(e):
                            for t in range(8):
                                ins = e.matmul(bank[4 + b2][0:8, :], lhsT=pexp[b2][:, t, :], rhs=vbuf[vslots[t]][:], start=(t == 0 and g8 == 0), stop=(t == 7 and g8 == 7))
                            return ins
                        PE(f, ["pexp%d" % b2] + ["vbuf%d" % v_ for v_ in vslots], [BK[4 + b2]])
                        PE(lambda e: e.matmul(bank[6][0:1, g8 * 64:(g8 + 1) * 64], lhsT=onesb[:, 0:1], rhs=pexp[b2][:].rearrange("p t a -> p (t a)"), start=True, stop=True),
                           ["onesb", "pexp%d" % b2], [BK[6]])
                        if g8 == 7:
                            A(lambda e: e.activation(out=osb[b2][:], in_=bank[4 + b2][0:8, :], func=AF.Copy), [BK[4 + b2]], ["osb%d" % b2])
                            V(lambda e: e.tensor_reduce(out=ssb[b2][:], in_=bank[6][0:1, :].rearrange("p (t a) -> p a t", a=8), axis=AX.X, op=ALU.add),
                              [BK[6]], ["ssb%d" % b2])
                            S.dma("sp", rs_out[l][s:s + 1, 0:4096].rearrange("o (a c) -> (o a) c", a=8), osb[b2][:], reads=["osb%d" % b2], writes=["rs_out"], slot="rsin_o%d" % b2)
                            S.dma("sp", rs_out[l][s:s + 1, 4096:4104], ssb[b2][:], reads=["ssb%d" % b2], writes=["rs_out"], slot="rsin_s%d" % b2)
                    S.barrier()
                    ss.close()
                    ss = stack2
                    Op = T(ss, "Op", [NO, 8, 128])
                    sm = T(ss, "sm", [NO, 8])
                    pn = T(ss, "pn", [NO, 8])
                    tq = T(ss, "tqA", [NO, 512])
                    oa = T(ss, "oa", [NO, 4, 128])
                    zs = T(ss, "zsA", [NO, 512])
                    Op4 = Op[:, :, :].rearrange("s (h m) d -> s h m d", m=2)
                    S.dma_group("sp", [(Op4[:, :, m, :], bass.AP(tensor=rs_out[l].ap().tensor, offset=m * 512, ap=[[RSW, NO], [1152, 4], [1, 128]])) for m in range(2)],
                                reads=["rs_out"], writes=["Op"], slot="Op")
                    S.dma("sp", sm[:], rs_out[l][:, 4096:4104], reads=["rs_out"], writes=["sm"], slot="sm")
                    V(lambda e: e.tensor_tensor(out=tq[:], in0=pj[:, 0:512], in1=pj[:, 512:1024], op=ALU.mult), ["pjA"], ["tqA"])
                    V(lambda e: e.tensor_reduce(out=pn[:], in_=tq[:].rearrange("s (a d) -> s a d", d=64), axis=AX.X, op=ALU.add), ["tqA"], ["pn"])
                    pn3 = pn[:, :].rearrange("s (h m) -> s h m", m=2)
                    sm3 = sm[:, :].rearrange("s (h m) -> s h m", m=2)
                    V(lambda e: e.scalar_tensor_tensor(out=pn3, in0=pn3, scalar=0.125, in1=b0row[:].unsqueeze(2).to_broadcast([NO, 4, 2]), op0=ALU.mult, op1=ALU.add), ["pn", "b0row"], ["pn"])
                    A(lambda e: e.activation(out=pn[:], in_=pn[:], func=AF.Exp), ["pn"], ["pn"])
                    V(lambda e: e.tensor_tensor(out=sm[:], in0=sm[:], in1=pn[:], op=ALU.add), ["sm", "pn"], ["sm"])
                    V(lambda e: e.reciprocal(out=sm[:], in_=sm[:]), ["sm"], ["sm"])
                    V(lambda e: e.tensor_scalar(out=sm3[:, :, 1], in0=sm3[:, :, 1], scalar1=lamv[0:NO, 1:2], scalar2=None, op0=ALU.mult), ["sm", "lamv"], ["sm"])
                    tq3 = tq[:, :].rearrange("s (h d) -> s h d", d=128)
                    for m in range(2):
                        V(lambda e: e.tensor_tensor(out=tq3, in0=pj[:, 1024:1536].rearrange("s (h d) -> s h d", d=128),
                                                    in1=pn3[:, :, m].unsqueeze(2).to_broadcast([NO, 4, 128]), op=ALU.mult), ["pjA", "pn"], ["tqA"])
                        V(lambda e: e.tensor_tensor(out=Op4[:, :, m, :], in0=Op4[:, :, m, :], in1=tq3, op=ALU.add), ["Op", "tqA"], ["Op"])
                    V(lambda e: e.tensor_tensor(out=Op[:], in0=Op[:], in1=sm[:].unsqueeze(2).to_broadcast([NO, 8, 128]), op=ALU.mult), ["Op", "sm"], ["Op"])
                    V(lambda e: e.tensor_tensor(out=oa[:], in0=Op4[:, :, 0, :], in1=Op4[:, :, 1, :], op=ALU.add), ["Op"], ["oa"])
                    A(lambda e: e.activation(out=zs[:], in_=pj[:, 1536:2048], func=AF.Silu), ["pjA"], ["zsA"])
                    gated_norm_rows(ss, oa[:], "oa", 4, 128, sgl[0:NO, :].unsqueeze(1).to_broadcast([NO, 4, 128]), "sgl", zs[:], "zsA", 0)
                S.barrier()

            def sample_ssd():
                with ExitStack() as ss:
                    pj = T(ss, "pjB", [NO, 1544])
                    cw = T(ss, "cwB", [NO, 1024])
                    cs_ = T(ss, "csB", [NO, 1024])
                    xa = T(ss, "xaB", [NO, 1024])
                    big = T(ss, "bigB", [NO, 1024])
                    sv = T(ss, "svB", [NO, 32])
                    rowp = T(ss, "rowpB", [NO, 24])
                    yb = T(ss, "ybB", [NO, 512])
                    zs = T(ss, "zsB", [NO, 512])
                    gB = T(ss, "gB", [NO, 512])
                    colsT = T(ss, "colsT", [128, 8, NO])
                    bcb = T(ss, "bcb", [128, 512])
                    hst = T(ss, "hst", [128, 4, 128])
                    ycol = T(ss, "ycol", [128, 4, NO])
                    hj = T(ss, "hj", [128, 128])
                    proj_rows(pj, "pjB", [(0, 512, 0), (512, 512, 512), (1024, 8, 1024), (1032, 512, 1032)])
                    bcast_row(cw[:], ssm_conv_w[l:l + 1, 3 * 1024:4 * 1024], 1024, "cwB")
                    V(lambda e: e.tensor_tensor(out=xa[:], in0=pj[:, 0:1024], in1=cw[:], op=ALU.mult), ["pjB", "cwB"], ["xaB"])
                    for i in range(3):
                        bcast_row(cw[:], ssm_conv_w[l:l + 1, i * 1024:(i + 1) * 1024], 1024, "cwB")
                        S.dma("sp", cs_[:], st_sconv[l, :, i, :], writes=["csB"], slot="csB")
                        V(lambda e: e.tensor_tensor(out=big[:], in0=cs_[:], in1=cw[:], op=ALU.mult), ["csB", "cwB"], ["bigB"])
                        V(lambda e: e.tensor_tensor(out=xa[:], in0=xa[:], in1=big[:], op=ALU.add), ["xaB", "bigB"], ["xaB"])
                        if i >= 1:
                            S.dma("sp", sconv_s[l][:, i - 1, :], cs_[:], reads=["csB"], slot="scs0")
                    S.dma("sp", sconv_s[l][:, 2, :], pj[:, 0:1024], reads=["pjB"], slot="scs1")
                    bcast_row(cw[:], ssm_conv_b[l:l + 1, :], 1024, "cwB")
                    V(lambda e: e.tensor_tensor(out=xa[:], in0=xa[:], in1=cw[:], op=ALU.add), ["xaB", "cwB"], ["xaB"])
                    A(lambda e: e.activation(out=xa[:], in_=xa[:], func=AF.Silu), ["xaB"], ["xaB"])
                    bcast_row(rowp[:, 0:8], ssm_dt_bias[l:l + 1, :], 8, "rowpB_a")
                    bcast_row(rowp[:, 8:16], ssm_A_log[l:l + 1, :], 8, "rowpB_b")
                    bcast_row(rowp[:, 16:24], ssm_D[l:l + 1, :], 8, "rowpB_c")
                    RP = ["rowpB_a", "rowpB_b", "rowpB_c"]
                    V(lambda e: e.tensor_tensor(out=sv[:, 0:8], in0=pj[:, 1024:1032], in1=rowp[:, 0:8], op=ALU.add), ["pjB"] + RP, ["svB"])
                    softplus(sv[:, 0:8], sv[:, 0:8], ["svB"], "svB")
                    A(lambda e: e.activation(out=sv[:, 8:16], in_=rowp[:, 8:16], func=AF.Exp), RP + ["svB"], ["svB"])
                    V(lambda e: e.scalar_tensor_tensor(out=sv[:, 16:24], in0=sv[:, 8:16], scalar=-1.0, in1=sv[:, 0:8], op0=ALU.mult, op1=ALU.mult), ["svB"], ["svB"])
                    A(lambda e: e.activation(out=sv[:, 16:24], in_=sv[:, 16:24], func=AF.Exp), ["svB"], ["svB"])
                    V(lambda e: e.tensor_tensor(out=big[:, 0:512].rearrange("s (h p) -> s h p", p=64), in0=xa[:, 0:512].rearrange("s (h p) -> s h p", p=64),
                                                in1=sv[:, 0:8].unsqueeze(2).to_broadcast([NO, 8, 64]), op=ALU.mult), ["xaB", "svB"], ["bigB"])
                    V(lambda e: e.tensor_copy(out=big[:, 512:1024].rearrange("s (h p) -> s h p", p=64), in_=sv[:, 16:24].unsqueeze(2).to_broadcast([NO, 8, 64])), ["svB", "bigB"], ["bigB"])
                    def f(e):
                        for j in range(8):
                            ins = e.transpose(out=bank[0][:, j * NO:(j + 1) * NO], in_=big[:, j * 128:(j + 1) * 128], identity=cm[0:NO, 0, 0:NO])
                        return ins
                    PE(f, ["bigB", "cm"], [BK[0]])
                    V(lambda e: e.tensor_copy(out=colsT[:].rearrange("p j s -> p (j s)"), in_=bank[0][:, 0:8 * NO]), [BK[0]], ["colsT"])
                    for s in range(NO):
                        S.dma("sp", hst[:], st_ssm[l, s].rearrange("h p n -> (h p) n").rearrange("(j q) n -> q j n", q=128), writes=["hst"], slot="hst")
                        PE(lambda e: e.matmul(bank[1][:, :], lhsT=sel4[:, s, :], rhs=xa[:, 512:1024], start=True, stop=True), ["sel4", "xaB"], [BK[1]])
                        A(lambda e: e.activation(out=bcb[:], in_=bank[1][:, :], func=AF.Copy), [BK[1]], ["bcb"])
                        for j in range(4):
                            g = j // 2
                            V(lambda e: e.tensor_scalar(out=hst[:, j, :], in0=hst[:, j, :], scalar1=colsT[:, 4 + j, s:s + 1], scalar2=None, op0=ALU.mult), ["hst", "colsT"], ["hst"])
                            V(lambda e: e.scalar_tensor_tensor(out=hst[:, j, :], in0=bcb[:, g * 128:(g + 1) * 128], scalar=colsT[:, j, s:s + 1], in1=hst[:, j, :],
                                                               op0=ALU.mult, op1=ALU.add), ["hst", "bcb", "colsT"], ["hst"])
                            V(lambda e: e.tensor_tensor(out=hj[:], in0=hst[:, j, :], in1=bcb[:, 256 + g * 128:256 + (g + 1) * 128], op=ALU.mult), ["hst", "bcb"], ["hj"])
                            V(lambda e: e.tensor_reduce(out=ycol[:, j, s:s + 1], in_=hj[:], axis=AX.X, op=ALU.add), ["hj"], ["ycol"])
                        S.dma("sp", ssm_s[l, s].rearrange("h p n -> (h p) n").rearrange("(j q) n -> q j n", q=128), hst[:], reads=["hst"], slot="ssms")
                    def f(e):
                        for j in range(4):
                            ins = e.transpose(out=bank[2][0:NO, j * 128:(j + 1) * 128], in_=ycol[:, j, :], identity=ident)
                        return ins
                    PE(f, ["ycol", "cm"], [BK[2]])
                    V(lambda e: e.tensor_tensor(out=big[:, 0:512].rearrange("s (h p) -> s h p", p=64), in0=xa[:, 0:512].rearrange("s (h p) -> s h p", p=64),
                                                in1=rowp[:, 16:24].unsqueeze(2).to_broadcast([NO, 8, 64]), op=ALU.mult), ["xaB"] + RP, ["bigB"])
                    V(lambda e: e.tensor_tensor(out=yb[:], in0=bank[2][0:NO, :], in1=big[:, 0:512], op=ALU.add), [BK[2], "bigB"], ["ybB"])
                    A(lambda e: e.activation(out=zs[:], in_=pj[:, 1032:1544], func=AF.Silu), ["pjB"], ["zsB"])
                    V(lambda e: e.tensor_tensor(out=yb[:], in0=yb[:], in1=zs[:], op=ALU.mult), ["ybB", "zsB"], ["ybB"])
                    bcast_row(gB[:], ssm_norm_g[l:l + 1, :], 512, "gB")
                    gated_norm_rows(ss, yb[:, :].rearrange("s (g d) -> s g d", d=256), "ybB", 2, 256, gB[:, :].rearrange("s (g d) -> s g d", d=256), "gB", None, None, 1)
                S.barrier()

            def sample_gdn():
                with ExitStack() as ss:
                    pj = T(ss, "pjC", [NO, 2056])
                    cw = T(ss, "cwC", [NO, 1536])
                    cs_ = T(ss, "csC", [NO, 1536])
                    sv = T(ss, "svC", [NO, 32])
                    rowp = T(ss, "rowpC", [NO, 8])
                    zs = T(ss, "zsC", [NO, 512])
                    gC = T(ss, "gC", [NO, 128])
                    qkT = T(ss, "qkT", [128, 8, NO])
                    qkTm = T(ss, "qkTm", [128, 8, NO, NO])
                    kmr = T(ss, "kmr", [NO, 512])
                    egb = T(ss, "egb", [128, NO, 4])
                    vn = T(ss, "vn", [NO, 512])
                    Sall = T(ss, "Sall", [128, NO, 4, 128])
                    proj_rows(pj, "pjC", [(0, 512, 0), (512, 512, 512), (1024, 512, 1024), (1536, 8, 1536), (1544, 512, 1544)])
                    xa = pj[:, 0:1536]
                    S.dma("sp", gconv_s[l][:, 2, :], pj[:, 0:1536], reads=["pjC"], slot="gcs1")
                    for s in range(NO):
                        S.dma("sp", Sall[:, s, :, :], st_gdn[l, s].rearrange("h k v -> k h v"), writes=["Sall%d" % s], slot="Sall%d" % s)
                    bcast_row(cw[:], gdn_conv_w[l:l + 1, 3 * 1536:4 * 1536], 1536, "cwC")
                    V(lambda e: e.tensor_tensor(out=xa, in0=xa, in1=cw[:], op=ALU.mult), ["pjC", "cwC"], ["pjC"])
                    for i in range(3):
                        bcast_row(cw[:], gdn_conv_w[l:l + 1, i * 1536:(i + 1) * 1536], 1536, "cwC")
                        S.dma("sp", cs_[:], st_gconv[l, :, i, :], writes=["csC"], slot="csC")
                        if i >= 1:
                            S.dma("sp", gconv_s[l][:, i - 1, :], cs_[:], reads=["csC"], slot="gcs0")
                        V(lambda e: e.tensor_tensor(out=cs_[:], in0=cs_[:], in1=cw[:], op=ALU.mult), ["csC", "cwC"], ["csC"])
                        V(lambda e: e.tensor_tensor(out=xa, in0=xa, in1=cs_[:], op=ALU.add), ["pjC", "csC"], ["pjC"])
                    A(lambda e: e.activation(out=xa, in_=xa, func=AF.Silu), ["pjC"], ["pjC"])
                    V(lambda e: e.tensor_tensor(out=cs_[:, 0:1024], in0=xa[:, 0:1024], in1=xa[:, 0:1024], op=ALU.mult), ["pjC", "csC"], ["csC"])
                    V(lambda e: e.tensor_reduce(out=sv[:, 0:8], in_=cs_[:, 0:1024].rearrange("s (g d) -> s g d", d=128), axis=AX.X, op=ALU.add), ["csC"], ["svC"])
                    A(lambda e: e.activation(out=sv[:, 0:8], in_=sv[:, 0:8], func=AF.Sqrt, bias=epsc[0:NO, 0:1]), ["svC", "epsc"], ["svC"])
                    V(lambda e: e.reciprocal(out=sv[:, 0:8], in_=sv[:, 0:8]), ["svC"], ["svC"])
                    V(lambda e: e.tensor_scalar(out=sv[:, 0:4], in0=sv[:, 0:4], scalar1=float(128 ** -0.5), scalar2=None, op0=ALU.mult), ["svC"], ["svC"])
                    V(lambda e: e.tensor_tensor(out=xa[:, 0:1024].rearrange("s (g d) -> s g d", d=128), in0=xa[:, 0:1024].rearrange("s (g d) -> s g d", d=128),
                                                in1=sv[:, 0:8].unsqueeze(2).to_broadcast([NO, 8, 128]), op=ALU.mult), ["pjC", "svC"], ["pjC"])
                    A(lambda e: e.activation(out=sv[:, 8:12], in_=pj[:, 1536:1540], func=AF.Sigmoid), ["pjC", "svC"], ["svC"])
                    bcast_row(rowp[:, 0:4], gdn_dt_bias[l:l + 1, :], 4, "rowpC_a")
                    bcast_row(rowp[:, 4:8], gdn_A_log[l:l + 1, :], 4, "rowpC_b")
                    RQ = ["rowpC_a", "rowpC_b"]
                    V(lambda e: e.tensor_tensor(out=sv[:, 12:16], in0=pj[:, 1540:1544], in1=rowp[:, 0:4], op=ALU.add), ["pjC", "svC"] + RQ, ["svC"])
                    softplus(sv[:, 12:16], sv[:, 12:16], ["svC"], "svC")
                    A(lambda e: e.activation(out=sv[:, 16:20], in_=rowp[:, 4:8], func=AF.Exp), RQ + ["svC"], ["svC"])
                    V(lambda e: e.scalar_tensor_tensor(out=sv[:, 12:16], in0=sv[:, 16:20], scalar=-1.0, in1=sv[:, 12:16], op0=ALU.mult, op1=ALU.mult), ["svC"], ["svC"])
                    A(lambda e: e.activation(out=sv[:, 12:16], in_=sv[:, 12:16], func=AF.Exp), ["svC"], ["svC"])
                    def f(e):
                        for j in range(8):
                            ins = e.transpose(out=bank[0][:, j * NO:(j + 1) * NO], in_=xa[:, j * 128:(j + 1) * 128], identity=cm[0:NO, 0, 0:NO])
                        return ins
                    PE(f, ["pjC", "cm"], [BK[0]])
                    V(lambda e: e.tensor_copy(out=qkT[:].rearrange("p j s -> p (j s)"), in_=bank[0][:, 0:8 * NO]), [BK[0]], ["qkT"])
                    V(lambda e: e.tensor_tensor(out=qkTm[:], in0=qkT[:].unsqueeze(2).to_broadcast([128, 8, NO, NO]),
                                                in1=eye4[:].unsqueeze(1).to_broadcast([128, 8, NO, NO]), op=ALU.mult), ["qkT", "eye4"], ["qkTm"])
                    for s in range(NO):
                        PE(lambda e: e.matmul(bank[1][:, s * 4:(s + 1) * 4], lhsT=sel4[:, s, :], rhs=sv[:, 12:16], start=True, stop=True), ["sel4", "svC"], [BK[1]])
                    V(lambda e: e.tensor_copy(out=egb[:].rearrange("p s h -> p (s h)"), in_=bank[1][:, 0:16]), [BK[1]], ["egb"])
                    SA = ["Sall%d" % s for s in range(NO)]
                    for h in range(4):
                        def f(e):
                            for s in range(NO):
                                ins = e.matmul(bank[2][0:NO, h * 128:(h + 1) * 128], lhsT=qkTm[:, 4 + h, s, :], rhs=Sall[:, s, h, :], start=(s == 0), stop=(s == NO - 1))
                            return ins
                        PE(f, ["qkTm"] + SA, [BK[2]])
                    V(lambda e: e.tensor_tensor(out=vn[:].rearrange("s (h d) -> s h d", d=128), in0=bank[2][0:NO, :].rearrange("s (h d) -> s h d", d=128),
                                                in1=sv[:, 12:16].unsqueeze(2).to_broadcast([NO, 4, 128]), op=ALU.mult), [BK[2], "svC"], ["vn"])
                    V(lambda e: e.tensor_tensor(out=vn[:], in0=xa[:, 1024:1536], in1=vn[:], op=ALU.subtract), ["pjC", "vn"], ["vn"])
                    V(lambda e: e.tensor_tensor(out=vn[:].rearrange("s (h d) -> s h d", d=128), in0=vn[:].rearrange("s (h d) -> s h d", d=128),
                                                in1=sv[:, 8:12].unsqueeze(2).to_broadcast([NO, 4, 128]), op=ALU.mult), ["vn", "svC"], ["vn"])
                    for s in range(NO):
                        V(lambda e: e.tensor_scalar(out=kmr[:], in0=xa[:, 512:1024], scalar1=cm[0:NO, 0, s:s + 1], scalar2=None, op0=ALU.mult), ["pjC", "cm"], ["kmr"])
                        for h in range(4):
                            PE(lambda e: e.matmul(bank[3][:, h * 128:(h + 1) * 128], lhsT=kmr[:, h * 128:(h + 1) * 128], rhs=vn[:, h * 128:(h + 1) * 128], start=True, stop=True),
                               ["kmr", "vn"], [BK[3]])
                            V(lambda e: e.scalar_tensor_tensor(out=Sall[:, s, h, :], in0=Sall[:, s, h, :], scalar=egb[:, s, h:h + 1], in1=bank[3][:, h * 128:(h + 1) * 128],
                                                               op0=ALU.mult, op1=ALU.add), ["Sall%d" % s, "egb", BK[3]], ["Sall%d" % s])
                        S.dma("sp", gdn_s[l, s].rearrange("h k v -> k h v"), Sall[:, s, :, :], reads=["Sall%d" % s], slot="gdns%d" % s)
                    for h in range(4):
                        def f(e):
                            for s in range(NO):
                                ins = e.matmul(bank[4][0:NO, h * 128:(h + 1) * 128], lhsT=qkTm[:, h, s, :], rhs=Sall[:, s, h, :], start=(s == 0), stop=(s == NO - 1))
                            return ins
                        PE(f, ["qkTm"] + SA, [BK[4]])
                    og = vn
                    V(lambda e: e.tensor_copy(out=og[:], in_=bank[4][0:NO, :]), [BK[4], "vn"], ["vn"])
                    A(lambda e: e.activation(out=zs[:], in_=pj[:, 1544:2056], func=AF.Silu), ["pjC"], ["zsC"])
                    bcast_row(gC[:], gdn_norm_g[l:l + 1, :], 128, "gC")
                    gated_norm_rows(ss, og[:, :].rearrange("s (h d) -> s h d", d=128), "vn", 4, 128, gC[:, :].unsqueeze(1).to_broadcast([NO, 4, 128]), "gC", zs[:], "zsC", 2)
                S.barrier()

            def sample_finish():
                with ExitStack() as gstack:
                    outproj_sample(l, gstack)
                    S.barrier()
                S.barrier()

            if "A" in do_branches or do_sample:
                load_w(l, [(0, 1536), (OZ, 512)])
                if do_sample:
                    sample_attn()
            if "A" in do_branches:
                with ExitStack() as bs:
                    kT = T(bs, "kT", [128, 4, L], BF16)
                    vbf = T(bs, "vbf", [128, NT, 4, 130], BF16)
                    kvf = T(bs, "kvf", [128, 1, 512])
                    qT = T(bs, "qT", [128, 2, 4, 128], BF16)
                    pT = [T(bs, "pT%d" % i, [128, 4, 128], BF16) for i in range(2)]
                    sdg = T(bs, "sdg", [128, 2, 128])
                    za = T(bs, "za", [128, 512])
                    oh_ = T(bs, "oh_", [128, 512])
                    ob = T(bs, "ob", [128, 512], BF16)
                    r8 = T(bs, "r8", [128, 8])
                    s4 = T(bs, "s4", [128, 8])
                    jk = T(bs, "jk", [128, 128])
                    load_wout(l, 0)
                    V(lambda e: e.memset(vbf[:], 1.0), [], ["vbf"])
                    V(lambda e: e.memset(qT[:], 0.0), [], ["qT"])
                    for i in range(NT):
                        for h in range(4):
                            proj_feat(bank[0][:, h * 128:(h + 1) * 128], BK[0], 512 + h * 128, lambda k: hT[:, k, i * 128:(i + 1) * 128], ["hT%d" % i])
                        A(lambda e: e.activation(out=kT[:, :, i * 128:(i + 1) * 128], in_=bank[0][:, :].rearrange("p (h t) -> p h t", t=128), func=AF.Copy), [BK[0]], ["kT%d" % i])
                        for (j, c0, dst) in ((0, 512, k_p), (1, 1024, v_p)):
                            proj_tok(bank[1 + j][:, :], BK[1 + j], lambda k: hT[:, k, i * 128:(i + 1) * 128], ["hT%d" % i], c0, 512)
                            V(lambda e: e.tensor_copy(out=kvf[:, 0, :], in_=bank[1 + j][:, :]), [BK[1 + j]], ["kvf"])
                            S.dma("sp", dst[l, i * 128:(i + 1) * 128, :], kvf[:, 0, :], reads=["kvf"], slot="kvout")
                            if j == 1:
                                A(lambda e: e.activation(out=vbf[:, i, :, 0:128], in_=bank[2][:, :].rearrange("p (h d) -> p h d", d=128), func=AF.Copy), [BK[2]], ["vbf%d" % i, "vbf"])
                    for i in range((NT if DBG_A2 is None else DBG_A2) if DBG_STOP != 1 else 0):
                        for h in range(4):
                            proj_feat(bank[0][:, h * 128:(h + 1) * 128], BK[0], h * 128, lambda k: hT[:, k, i * 128:(i + 1) * 128], ["hT%d" % i])
                        A(lambda e: e.activation(out=qT[0:64, 0, :, :], in_=bank[0][0:64, :].rearrange("p (h t) -> p h t", t=128), func=AF.Copy, scale=0.125), [BK[0], "qT"], ["qT"])
                        A(lambda e: e.activation(out=qT[64:128, 1, :, :], in_=bank[0][64:128, :].rearrange("p (h t) -> p h t", t=128), func=AF.Copy, scale=0.125), [BK[0], "qT"], ["qT"])
                        proj_tok(bank[1][:, :], BK[1], lambda k: hT[:, k, i * 128:(i + 1) * 128], ["hT%d" % i], 1536, 512)
                        A(lambda e: e.activation(out=za[:], in_=bank[1][:, :], func=AF.Silu), [BK[1]], ["za"])
                        cnt = 0
                        if DBG_STOP == 2:
                            continue
                        for h in range(4):
                            for m in range(2 if DBG_STOP != 5 else 1):
                                hm = h * 2 + m
                                ob_bank = 4 + hm // 3
                                ocol = (hm % 3) * 130
                                ngroups = (i + 1 + 3) // 4
                                for g in range(ngroups):
                                    j0 = g * 4
                                    nb = min(4, i + 1 - j0)
                                    sb = 2 + (cnt % 2); pb = cnt % 2; cnt += 1
                                    def f(e):
                                        for jj in range(nb):
                                            j = j0 + jj
                                            ins = e.matmul(bank[sb][:, jj * 128:(jj + 1) * 128], lhsT=kT[:, h, j * 128:(j + 1) * 128],
                                                           rhs=qT[:, m, h, :], start=True, stop=True)
                                        return ins
                                    PE(f, ["qT"] + ["kT%d" % (j0 + jj) for jj in range(nb)], [BK[sb]])
                                    nfar = max(0, min(nb, (i - 1) - j0))
                                    if nfar > 0 and DBG_VAR != 1:
                                        A(lambda e: e.activation(out=pT[pb][:, 0:nfar, :], in_=bank[sb][:, 0:nfar * 128].rearrange("p (j t) -> p j t", t=128), func=AF.Exp,
                                                                 bias=c31[:, h:h + 1]), [BK[sb], "c31"], ["pT%d" % pb])
                                    for jj in range(nfar, nb if DBG_VAR != 2 else nfar):
                                        j = j0 + jj
                                        x_ = i - j
                                        V(lambda e: e.tensor_tensor(out=sdg[:, x_, :], in0=bank[sb][:, jj * 128:(jj + 1) * 128], in1=BT[:, h, x_ * 128:(x_ + 1) * 128], op=ALU.add),
                                          [BK[sb], "BT"], ["sdg%d" % x_])
                                        A(lambda e: e.activation(out=pT[pb][:, jj, :], in_=sdg[:, x_, :], func=AF.Exp), ["sdg%d" % x_], ["pT%d" % pb])
                                    if DBG_STOP == 4:
                                        continue
                                    def f2(e):
                                        for jj in range(nb):
                                            j = j0 + jj
                                            ins = e.matmul(bank[ob_bank][:, ocol:ocol + 129], lhsT=pT[pb][:, jj, :], rhs=vbf[:, j, h, 0:129], start=(j == 0), stop=(j == i))
                                        return ins
                                    PE(f2, ["pT%d" % pb] + ["vbf%d" % (j0 + jj) for jj in range(nb)], [BK[ob_bank]])
                        if DBG_STOP in (3, 4, 5):
                            continue
                        for hm in range(8):
                            ob_bank = 4 + hm // 3; ocol = (hm % 3) * 130
                            V(lambda e: e.reciprocal(out=r8[:, hm:hm + 1], in_=bank[ob_bank][:, ocol + 128:ocol + 129]), [BK[ob_bank]], ["r8_%d" % hm])
                            if hm % 2 == 1:
                                V(lambda e: e.tensor_scalar(out=r8[:, hm:hm + 1], in0=r8[:, hm:hm + 1], scalar1=lamv[:, 1:2], scalar2=None, op0=ALU.mult), ["r8_%d" % hm, "lamv"], ["r8_%d" % hm])
                        for h in range(4):
                            b0 = 4 + (2 * h) // 3; c0 = ((2 * h) % 3) * 130
                            b1 = 4 + (2 * h + 1) // 3; c1 = ((2 * h + 1) % 3) * 130
                            V(lambda e: e.tensor_scalar(out=oh_[:, h * 128:(h + 1) * 128], in0=bank[b0][:, c0:c0 + 128], scalar1=r8[:, 2 * h:2 * h + 1], scalar2=None, op0=ALU.mult),
                              [BK[b0], "r8_%d" % (2 * h)], ["oh_%d" % h])
                            V(lambda e: e.scalar_tensor_tensor(out=oh_[:, h * 128:(h + 1) * 128], in0=bank[b1][:, c1:c1 + 128], scalar=r8[:, 2 * h + 1:2 * h + 2],
                                                               in1=oh_[:, h * 128:(h + 1) * 128], op0=ALU.mult, op1=ALU.add), [BK[b1], "r8_%d" % (2 * h + 1), "oh_%d" % h], ["oh_%d" % h])
                            A(lambda e: e.activation(out=jk[:], in_=oh_[:, h * 128:(h + 1) * 128], func=AF.Square, accum_out=s4[:, h:h + 1]), ["oh_%d" % h], ["jk", "s4_%d" % h])
                        rstd_from_ss(s4[:, 4:8], s4[:, 0:4], 128, ["s4_%d" % h for h in range(4)], "s4r")
                        for h in range(4):
                            V(lambda e: e.scalar_tensor_tensor(out=oh_[:, h * 128:(h + 1) * 128], in0=oh_[:, h * 128:(h + 1) * 128], scalar=s4[:, 4 + h:5 + h], in1=sgl[:],
                                                               op0=ALU.mult, op1=ALU.mult), ["oh_%d" % h, "s4r", "sgl"], ["oh_%d" % h])
                        V(lambda e: e.tensor_tensor(out=ob[:], in0=oh_[:], in1=za[:], op=ALU.mult), ["oh_%d" % h for h in range(4)] + ["za"], ["ob"])
                        o_to_T(ob[:], "ob", oTt, "oTt")
                        outproj_update_prompt(i, oTt, "oTt", banks=(0, 1))
                    S.barrier()

            if "B" in do_branches or do_sample:
                load_w(l, [(OXBC, 1032), (OZ + 512, 512)])
                if do_sample:
                    sample_ssd()
            if "B" in do_branches:
                with ExitStack() as bs:
                    NCH = 8
                    xc = T(bs, "xc", [128, NCH, 131])
                    cwT = T(bs, "cwT", [128, 40])
                    cwr = T(bs, "cwr", [40, 128])
                    xact = T(bs, "xact", [128, NCH, 128])
                    acc = T(bs, "acc", [128, 128])
                    xact_b = T(bs, "xact_b", [128, 4, 128], BF16)
                    xs_tok = T(bs, "xs_tok", [128, 512])
                    Btok = T(bs, "Btok", [128, 2, 128], BF16)
                    rows = T(bs, "rows", [128, 32])
                    dtv = T(bs, "dtv", [128, 48])
                    aU = T(bs, "aU", [128, 128])
                    Es = T(bs, "Es", [128, 8, 128])
                    MT = T(bs, "MT", [128, 8, 128], BF16)
                    xdt = T(bs, "xdt", [128, 8, 64], BF16)
                    xdtw = T(bs, "xdtw", [128, 8, 64], BF16)
                    hS = T(bs, "hS", [128, 8, 64])
                    hSb = T(bs, "hSb", [128, 8, 64], BF16)
                    ysb = T(bs, "ysb", [128, 512])
                    zb = T(bs, "zb", [128, 512])
                    ng = T(bs, "ng", [128, 512])
                    ob = T(bs, "obB", [128, 512], BF16)
                    s2 = T(bs, "s2", [128, 4])
                    jk = T(bs, "jkB", [128, 256])
                    st3 = T(bs, "st3", [3, 1024])
                    hout = T(bs, "hout", [64, 8, 128])
                    load_wout(l, 512)
                    S.dma("sp", cwr[0:32, :], ssm_conv_w[l:l + 1, :].rearrange("o (a p) -> (o a) p", p=128), writes=["cwr"], slot="cwr")
                    S.dma("sp", cwr[32:40, :], ssm_conv_b[l:l + 1, :].rearrange("o (a p) -> (o a) p", p=128), writes=["cwr"], slot="cwr")
                    transpose_f32(bank[0][:, 0:40], cwr[:], ["cwr"], [BK[0]], rows=40)
                    V(lambda e: e.tensor_copy(out=cwT[:], in_=bank[0][:, 0:40]), [BK[0]], ["cwT"])
                    bcast_row(rows[:, 0:8], ssm_dt_bias[l:l + 1, :], 8, "rows_a")
                    bcast_row(rows[:, 8:16], ssm_A_log[l:l + 1, :], 8, "rows_b")
                    bcast_row(rows[:, 16:24], ssm_D[l:l + 1, :], 8, "rows_c")
                    RW = ["rows_a", "rows_b", "rows_c"]
                    A(lambda e: e.activation(out=rows[:, 24:32], in_=rows[:, 8:16], func=AF.Exp), RW, ["rows_d"])
                    RW = RW + ["rows_d"]
                    bcast_row(ng[:], ssm_norm_g[l:l + 1, :], 512, "ngB")
                    V(lambda e: e.memset(xc[:], 0.0), [], ["xc"])
                    V(lambda e: e.memset(hS[:], 0.0), [], ["hS"])
                    V(lambda e: e.memset(hSb[:], 0.0), [], ["hSb"])
                    for i in range(NT):
                        hk_ = ["hT%d" % i]
                        rhs_fn = lambda k: hT[:, k, i * 128:(i + 1) * 128]
                        for half in range(2):
                            for c in range(4):
                                proj_feat(bank[half][:, c * 128:(c + 1) * 128], BK[half], (half * 4 + c) * 128, rhs_fn, hk_)
                            A(lambda e: e.activation(out=xc[:, half * 4:(half + 1) * 4, 3:131], in_=bank[half][:, :].rearrange("p (c t) -> p c t", t=128), func=AF.Copy),
                              [BK[half]], ["xc"])
                        for c in range(NCH):
                            G(lambda e: e.tensor_scalar(out=acc[:], in0=xc[:, c, 3:131], scalar1=cwT[:, 24 + c:25 + c], scalar2=None, op0=ALU.mult), ["xc", "cwT"], ["acc"])
                            for tap in range(3):
                                V(lambda e: e.scalar_tensor_tensor(out=acc[:], in0=xc[:, c, tap:tap + 128], scalar=cwT[:, tap * 8 + c:tap * 8 + c + 1], in1=acc[:],
                                                                   op0=ALU.mult, op1=ALU.add), ["xc", "cwT", "acc"], ["acc"])
                            A(lambda e: e.activation(out=xact[:, c, :], in_=acc[:], func=AF.Silu, bias=cwT[:, 32 + c:33 + c]), ["acc", "cwT"], ["xact%d" % c])
                        if i == NT - 1:
                            for c in range(NCH):
                                transpose_f32(bank[6][0:3, c * 128:(c + 1) * 128] if c < 4 else bank[5][0:3, (c - 4) * 128:(c - 3) * 128], xc[:, c, 128:131], ["xc"], [BK[6] if c < 4 else BK[5]])
                            V(lambda e: e.tensor_copy(out=st3[:, 0:512], in_=bank[6][0:3, :]), [BK[6]], ["st3"])
                            V(lambda e: e.tensor_copy(out=st3[:, 512:1024], in_=bank[5][0:3, :]), [BK[5], "st3"], ["st3"])
                            S.dma("sp", sconv_p[l], st3[:], reads=["st3"], slot="sconvp")
                        V(lambda e: e.tensor_copy(out=xc[:, :, 0:3], in_=xc[:, :, 128:131]), ["xc"], ["xc"])
                        XA = ["xact%d" % c for c in range(NCH)]
                        proj_tok(bank[2][:, 0:8], BK[2], lambda k: hT[:, k, i * 128:(i + 1) * 128], hk_, 1024, 8)
                        V(lambda e: e.tensor_tensor(out=dtv[:, 0:8], in0=bank[2][:, 0:8], in1=rows[:, 0:8], op=ALU.add), [BK[2]] + RW, ["dtv"])
                        softplus(dtv[:, 0:8], dtv[:, 0:8], ["dtv"], "dtv")
                        V(lambda e: e.scalar_tensor_tensor(out=dtv[:, 8:16], in0=dtv[:, 0:8], scalar=-1.0, in1=rows[:, 24:32], op0=ALU.mult, op1=ALU.mult), ["dtv"] + RW, ["dtv"])
                        proj_tok(bank[3][:, :], BK[3], lambda k: hT[:, k, i * 128:(i + 1) * 128], hk_, 1032, 512)
                        A(lambda e: e.activation(out=zb[:], in_=bank[3][:, :], func=AF.Silu), [BK[3]], ["zb"])
                        for c in range(4):
                            transpose_f32(bank[4][:, c * 128:(c + 1) * 128], xact[:, c, :], ["xact%d" % c], [BK[4]])
                        V(lambda e: e.tensor_copy(out=xs_tok[:], in_=bank[4][:, :]), [BK[4]], ["xs_tok"])
                        for c in range(2):
                            transpose_f32(bank[5][:, c * 128:(c + 1) * 128], xact[:, 4 + c, :], ["xact%d" % (4 + c)], [BK[5]])
                        A(lambda e: e.activation(out=Btok[:], in_=bank[5][:, 0:256].rearrange("p (g n) -> p g n", n=128), func=AF.Copy), [BK[5]], ["Btok"])
                        A(lambda e: e.activation(out=xact_b[:], in_=xact[:, 4:8, :], func=AF.Copy), XA, ["xact_b"])
                        V(lambda e: e.tensor_tensor(out=xdt[:], in0=xs_tok[:].rearrange("p (h q) -> p h q", q=64), in1=dtv[:, 0:8].unsqueeze(2).to_broadcast([128, 8, 64]), op=ALU.mult),
                          ["xs_tok", "dtv"], ["xdt"])
                        PE(lambda e: e.matmul(bank[2][:, 16:24], lhsT=Umat, rhs=dtv[:, 8:16], start=True, stop=True), ["cm", "dtv"], [BK[2]])
                        PE(lambda e: e.matmul(bank[2][:, 32:40], lhsT=onesf[:], rhs=dtv[:, 8:16], start=True, stop=True), ["onesf", "dtv"], [BK[2]])
                        V(lambda e: e.tensor_copy(out=dtv[:, 16:24], in_=bank[2][:, 16:24]), [BK[2], "dtv"], ["dtv"])
                        V(lambda e: e.tensor_scalar(out=dtv[:, 24:32], in0=bank[2][:, 16:24], scalar1=-1.0, scalar2=None, op0=ALU.mult), [BK[2], "dtv"], ["dtv"])
                        A(lambda e: e.activation(out=dtv[:, 32:40], in_=bank[2][:, 16:24], func=AF.Exp), [BK[2], "dtv"], ["dtv"])
                        A(lambda e: e.activation(out=dtv[:, 40:48], in_=bank[2][:, 32:40], func=AF.Exp), [BK[2], "dtv"], ["dtv"])
                        for g in range(2):
                            PE(lambda e: e.matmul(bank[3][:, g * 128:(g + 1) * 128], lhsT=xact_b[:, g, :], rhs=xact_b[:, 2 + g, :], start=True, stop=True), ["xact_b"], [BK[3]])
                        for h in range(8):
                            g = h // 4
                            sb_ = h % 2
                            G(lambda e: e.tensor_scalar(out=aU[:], in0=Umat, scalar1=dtv[:, 8 + h:9 + h], scalar2=None, op0=ALU.mult), ["cm", "dtv"], ["aU"])
                            def f(e):
                                e.matmul(bank[sb_][:, 0:128], lhsT=onesf[:], rhs=aU[:], start=True, stop=False)
                                return e.matmul(bank[sb_][:, 0:128], lhsT=ident, rhs=negmask, start=False, stop=True)
                            PE(f, ["onesf", "aU", "cm"], [BK[sb_]])
                            A(lambda e: e.activation(out=Es[:, h, :], in_=bank[sb_][:, 0:128], func=AF.Exp, bias=dtv[:, 24 + h:25 + h]), [BK[sb_], "dtv"], ["Es%d" % h])
                            V(lambda e: e.tensor_tensor(out=MT[:, h, :], in0=Es[:, h, :], in1=bank[3][:, g * 128:(g + 1) * 128], op=ALU.mult), ["Es%d" % h, BK[3]], ["MT%d" % h])
                        ES = ["Es%d" % h for h in range(8)]
                        V(lambda e: e.tensor_tensor(out=xdtw[:], in0=xdt[:], in1=Es[:, :, 127:128].to_broadcast([128, 8, 64]), op=ALU.mult), ["xdt"] + ES, ["xdtw"])
                        def f(e):
                            for h in range(8):
                                ins = e.matmul(bank[4][:, h * 64:(h + 1) * 64], lhsT=MT[:, h, :], rhs=xdt[:, h, :], start=True, stop=True)
                            return ins
                        PE(f, ["MT%d" % h for h in range(8)] + ["xdt"], [BK[4]])
                        def f(e):
                            for h in range(8):
                                ins = e.matmul(bank[5][:, h * 64:(h + 1) * 64], lhsT=xact_b[:, 2 + h // 4, :], rhs=hSb[:, h, :], start=True, stop=True)
                            return ins
                        PE(f, ["xact_b", "hSb"], [BK[5]])
                        V(lambda e: e.tensor_tensor(out=ysb[:].rearrange("p (h q) -> p h q", q=64), in0=bank[5][:, :].rearrange("p (h q) -> p h q", q=64),
                                                    in1=dtv[:, 32:40].unsqueeze(2).to_broadcast([128, 8, 64]), op=ALU.mult), [BK[5], "dtv"], ["ysb"])
                        V(lambda e: e.tensor_tensor(out=ysb[:], in0=ysb[:], in1=bank[4][:, :], op=ALU.add), ["ysb", BK[4]], ["ysb"])
                        def f(e):
                            for h in range(8):
                                ins = e.matmul(bank[6][:, h * 64:(h + 1) * 64], lhsT=Btok[:, h // 4, :], rhs=xdtw[:, h, :], start=True, stop=True)
                            return ins
                        PE(f, ["Btok", "xdtw"], [BK[6]])
                        V(lambda e: e.tensor_tensor(out=hS[:], in0=hS[:], in1=dtv[:, 40:48].unsqueeze(2).to_broadcast([128, 8, 64]), op=ALU.mult), ["hS", "dtv"], ["hS"])
                        V(lambda e: e.tensor_tensor(out=hS[:], in0=hS[:], in1=bank[6][:, :].rearrange("p (h q) -> p h q", q=64), op=ALU.add), ["hS", BK[6]], ["hS"])
                        A(lambda e: e.activation(out=hSb[:], in_=hS[:], func=AF.Copy), ["hS"], ["hSb"])
                        V(lambda e: e.tensor_tensor(out=xs_tok[:].rearrange("p (h q) -> p h q", q=64), in0=xs_tok[:].rearrange("p (h q) -> p h q", q=64),
                                                    in1=rows[:, 16:24].unsqueeze(2).to_broadcast([128, 8, 64]), op=ALU.mult), ["xs_tok", "xdt"] + RW, ["xs_tok"])
                        V(lambda e: e.tensor_tensor(out=ysb[:], in0=ysb[:], in1=xs_tok[:], op=ALU.add), ["ysb", "xs_tok"], ["ysb"])
                        V(lambda e: e.tensor_tensor(out=ysb[:], in0=ysb[:], in1=zb[:], op=ALU.mult), ["ysb", "zb"], ["ysb"])
                        for g in range(2):
                            A(lambda e: e.activation(out=jk[:], in_=ysb[:, g * 256:(g + 1) * 256], func=AF.Square, accum_out=s2[:, g:g + 1]), ["ysb"], ["jkB", "s2_%d" % g])
                        rstd_from_ss(s2[:, 2:4], s2[:, 0:2], 256, ["s2_0", "s2_1"], "s2r")
                        for g in range(2):
                            V(lambda e: e.scalar_tensor_tensor(out=ob[:, g * 256:(g + 1) * 256], in0=ysb[:, g * 256:(g + 1) * 256], scalar=s2[:, 2 + g:3 + g], in1=ng[:, g * 256:(g + 1) * 256],
                                                               op0=ALU.mult, op1=ALU.mult), ["ysb", "s2r", "ngB"], ["obB"])
                        o_to_T(ob[:], "obB", oTt, "oTt")
                        outproj_update_prompt(i, oTt, "oTt")
                    for h in range(8):
                        transpose_f32(bank[h % 2][0:64, 0:128], hS[:, h, :], ["hS"], [BK[h % 2]])
                        V(lambda e: e.tensor_copy(out=hout[:, h, :], in_=bank[h % 2][0:64, 0:128]), [BK[h % 2]], ["hout"])
                    S.dma("sp", ssm_p[l].rearrange("h p n -> p h n"), hout[:], reads=["hout"], slot="ssmp")
                    S.barrier()

            if "C" in do_branches or do_sample:
                load_w(l, [(OQKV, 1544), (OZ + 1024, 512)])
                if do_sample:
                    sample_gdn()
            if "C" in do_branches:
                with ExitStack() as bs:
                    NCH = 12
                    xc = T(bs, "xcC", [128, 4, 131])
                    hist = T(bs, "histC", [128, NCH, 3])
                    cwT = T(bs, "cwTC", [128, 48])
                    cwr = T(bs, "cwrC", [48, 128])
                    xact = T(bs, "xactC", [128, 4, 128])
                    acc = T(bs, "accC", [128, 128])
                    sq = T(bs, "sqC", [128, 4, 128])
                    qkb = T(bs, "qkb", [128, 8, 128], BF16)
                    ktok = T(bs, "ktok", [128, 4, 128], BF16)
                    kd = T(bs, "kd", [128, 4, 128], BF16)
                    vtok = T(bs, "vtok", [128, 4, 128])
                    rows = T(bs, "rowsC", [128, 16])
                    gv = T(bs, "gv", [128, 40])
                    gU = T(bs, "gU", [128, 128])
                    DTm = T(bs, "DTm", [128, 4, 128])
                    Dm = T(bs, "Dm", [128, 128])
                    NX = T(bs, "NX", [128, 4, 128], BF16)
                    NY = T(bs, "NY", [128, 4, 128], BF16)
                    XY = [T(bs, "XY%d" % i, [128, 4, 128], BF16) for i in range(4)]
                    Pn = T(bs, "Pn", [128, 4, 128], BF16)
                    Pt = T(bs, "Pt", [128, 4, 128], BF16)
                    gmask = T(bs, "gmask", [128, 4, 128], BF16)
                    S.dma("pool", gmask[:], gmask_d, writes=["gmask"], slot="gmask")
                    attT = T(bs, "attT", [128, 4, 128], BF16)
                    Sg = T(bs, "Sg", [128, 4, 128])
                    Sgb = T(bs, "Sgb", [128, 4, 128], BF16)
                    rr = T(bs, "rr", [128, 4, 128])
                    rb = T(bs, "rb", [128, 4, 128], BF16)
                    vnb = T(bs, "vnb", [128, 4, 128], BF16)
                    o1 = rr
                    zc = T(bs, "zc", [128, 512])
                    ngc = T(bs, "ngc", [128, 128])
                    ob = T(bs, "obC", [128, 512], BF16)
                    s4 = T(bs, "s4C", [128, 8])
                    jk = T(bs, "jkC", [128, 128])
                    st3 = T(bs, "st3C", [3, 512])
                    load_wout(l, 1024)
                    S.dma("sp", cwr[:], gdn_conv_w[l:l + 1, :].rearrange("o (a p) -> (o a) p", p=128), writes=["cwrC"], slot="cwrC")
                    transpose_f32(bank[0][:, 0:48], cwr[:], ["cwrC"], [BK[0]], rows=48)
                    V(lambda e: e.tensor_copy(out=cwT[:], in_=bank[0][:, 0:48]), [BK[0]], ["cwTC"])
                    bcast_row(rows[:, 0:4], gdn_dt_bias[l:l + 1, :], 4, "rowsC_a")
                    bcast_row(rows[:, 4:8], gdn_A_log[l:l + 1, :], 4, "rowsC_b")
                    RW = ["rowsC_a", "rowsC_b"]
                    A(lambda e: e.activation(out=rows[:, 8:12], in_=rows[:, 4:8], func=AF.Exp), RW, ["rowsC_c"])
                    RW = RW + ["rowsC_c"]
                    bcast_row(ngc[:], gdn_norm_g[l:l + 1, :], 128, "ngc")
                    V(lambda e: e.memset(hist[:], 0.0), [], ["histC"])
                    V(lambda e: e.memset(Sg[:], 0.0), [], ["Sg"])
                    V(lambda e: e.memset(Sgb[:], 0.0), [], ["Sgb"])
                    for i in range(NT if DBG_TILES is None else DBG_TILES):
                        hk_ = ["hT%d" % i]
                        rhs_fn = lambda k: hT[:, k, i * 128:(i + 1) * 128]
                        for q4 in range(3):
                            bk_ = q4 % 2
                            for c in range(4):
                                proj_feat(bank[bk_][:, c * 128:(c + 1) * 128], BK[bk_], (q4 * 4 + c) * 128, rhs_fn, hk_)
                            A(lambda e: e.activation(out=xc[:, :, 3:131], in_=bank[bk_][:, :].rearrange("p (c t) -> p c t", t=128), func=AF.Copy), [BK[bk_]], ["xcC"])
                            V(lambda e: e.tensor_copy(out=xc[:, :, 0:3], in_=hist[:, q4 * 4:(q4 + 1) * 4, :]), ["histC", "xcC"], ["xcC"])
                            for c in range(4):
                                cc = q4 * 4 + c
                                G(lambda e: e.tensor_scalar(out=acc[:], in0=xc[:, c, 3:131], scalar1=cwT[:, 36 + cc:37 + cc], scalar2=None, op0=ALU.mult), ["xcC", "cwTC"], ["accC"])
                                for tap in range(3):
                                    V(lambda e: e.scalar_tensor_tensor(out=acc[:], in0=xc[:, c, tap:tap + 128], scalar=cwT[:, tap * 12 + cc:tap * 12 + cc + 1], in1=acc[:],
                                                                       op0=ALU.mult, op1=ALU.add), ["xcC", "cwTC", "accC"], ["accC"])
                                A(lambda e: e.activation(out=xact[:, c, :], in_=acc[:], func=AF.Silu), ["accC"], ["xactC"])
                            if i == NT - 1:
                                for c in range(4):
                                    transpose_f32(bank[4][0:3, c * 128:(c + 1) * 128], xc[:, c, 128:131], ["xcC"], [BK[4]])
                                V(lambda e: e.tensor_copy(out=st3[:], in_=bank[4][0:3, :]), [BK[4]], ["st3C"])
                                S.dma("sp", gconv_p[l][:, q4 * 512:(q4 + 1) * 512], st3[:], reads=["st3C"], slot="gconvp")
                            V(lambda e: e.tensor_copy(out=hist[:, q4 * 4:(q4 + 1) * 4, :], in_=xc[:, :, 128:131]), ["xcC"], ["histC"])
                            if q4 < 2:
                                A(lambda e: e.activation(out=sq[:], in_=xact[:], func=AF.Square), ["xactC"], ["sqC"])
                                PE(lambda e: e.matmul(bank[2 + q4][:, :], lhsT=onesf[:], rhs=sq[:].rearrange("p c t -> p (c t)"), start=True, stop=True), ["onesf", "sqC"], [BK[2 + q4]])
                                A(lambda e: e.activation(out=sq[:], in_=bank[2 + q4][:, :].rearrange("p (c t) -> p c t", t=128), func=AF.Sqrt,
                                                         bias=(epsc[:, 2:3] if q4 == 0 else epsc[:, 0:1]), scale=(128.0 if q4 == 0 else 1.0)), [BK[2 + q4], "epsc"], ["sqC"])
                                V(lambda e: e.reciprocal(out=sq[:], in_=sq[:]), ["sqC"], ["sqC"])
                                V(lambda e: e.tensor_tensor(out=qkb[:, q4 * 4:(q4 + 1) * 4, :], in0=xact[:], in1=sq[:], op=ALU.mult), ["xactC", "sqC"], ["qkb"])
                            else:
                                for h in range(4):
                                    transpose_f32(bank[4][:, h * 128:(h + 1) * 128], xact[:, h, :], ["xactC"], [BK[4]])
                                V(lambda e: e.tensor_copy(out=vtok[:], in_=bank[4][:, :].rearrange("p (h d) -> p h d", d=128)), [BK[4]], ["vtok"])
                        def f(e):
                            for h in range(4):
                                ins = e.transpose(out=trb[0][:, h * 128:(h + 1) * 128], in_=qkb[:, 4 + h, :], identity=identb[:])
                            return ins
                        PE(f, ["qkb", "identb"], [TRB[0]])
                        A(lambda e: e.activation(out=ktok[:], in_=trb[0][:, :].rearrange("p (h d) -> p h d", d=128), func=AF.Copy), [TRB[0]], ["ktok"])
                        proj_tok(bank[5][:, 0:8], BK[5], lambda k: hT[:, k, i * 128:(i + 1) * 128], hk_, 1536, 8)
                        A(lambda e: e.activation(out=gv[:, 0:4], in_=bank[5][:, 0:4], func=AF.Sigmoid), [BK[5], "gv"], ["gv"])
                        V(lambda e: e.tensor_tensor(out=gv[:, 4:8], in0=bank[5][:, 4:8], in1=rows[:, 0:4], op=ALU.add), [BK[5], "gv"] + RW, ["gv"])
                        softplus(gv[:, 4:8], gv[:, 4:8], ["gv"], "gv")
                        V(lambda e: e.scalar_tensor_tensor(out=gv[:, 4:8], in0=gv[:, 4:8], scalar=-1.0, in1=rows[:, 8:12], op0=ALU.mult, op1=ALU.mult), ["gv"] + RW, ["gv"])
                        V(lambda e: e.tensor_scalar(out=gv[:, 32:36], in0=gv[:, 0:4], scalar1=-1.0, scalar2=None, op0=ALU.mult), ["gv"], ["gv"])
                        proj_tok(bank[6][:, :], BK[6], lambda k: hT[:, k, i * 128:(i + 1) * 128], hk_, 1544, 512)
                        A(lambda e: e.activation(out=zc[:], in_=bank[6][:, :], func=AF.Silu), [BK[6]], ["zc"])
                        PE(lambda e: e.matmul(bank[5][:, 16:20], lhsT=Umat, rhs=gv[:, 4:8], start=True, stop=True), ["cm", "gv"], [BK[5]])
                        PE(lambda e: e.matmul(bank[5][:, 32:36], lhsT=onesf[:], rhs=gv[:, 4:8], start=True, stop=True), ["onesf", "gv"], [BK[5]])
                        V(lambda e: e.tensor_copy(out=gv[:, 8:12], in_=bank[5][:, 16:20]), [BK[5], "gv"], ["gv"])
                        V(lambda e: e.tensor_scalar(out=gv[:, 12:16], in0=bank[5][:, 16:20], scalar1=-1.0, scalar2=None, op0=ALU.mult), [BK[5], "gv"], ["gv"])
                        A(lambda e: e.activation(out=gv[:, 16:20], in_=bank[5][:, 16:20], func=AF.Exp), [BK[5], "gv"], ["gv"])
                        V(lambda e: e.tensor_scalar(out=gv[:, 20:24], in0=gv[:, 16:20], scalar1=-1.0, scalar2=None, op0=ALU.mult), ["gv"], ["gv"])
                        A(lambda e: e.activation(out=gv[:, 24:28], in_=bank[5][:, 32:36], func=AF.Exp), [BK[5], "gv"], ["gv"])
                        V(lambda e: e.tensor_tensor(out=gv[:, 28:32], in0=bank[5][:, 32:36], in1=gv[:, 8:12], op=ALU.subtract), [BK[5], "gv"], ["gv"])
                        A(lambda e: e.activation(out=gv[:, 28:32], in_=gv[:, 28:32], func=AF.Exp), ["gv"], ["gv"])
                        V(lambda e: e.tensor_tensor(out=kd[:], in0=ktok[:], in1=gv[:, 28:32].unsqueeze(2).to_broadcast([128, 4, 128]), op=ALU.mult), ["ktok", "gv"], ["kd"])
                        for h in range(4):
                            sb_ = 2 + (h % 2)
                            G(lambda e: e.tensor_scalar(out=gU[:], in0=Umat, scalar1=gv[:, 4 + h:5 + h], scalar2=None, op0=ALU.mult), ["cm", "gv"], ["gU"])
                            def f(e):
                                e.matmul(bank[sb_][:, 0:128], lhsT=onesf[:], rhs=gU[:], start=True, stop=False)
                                return e.matmul(bank[sb_][:, 0:128], lhsT=ident, rhs=negmask, start=False, stop=True)
                            PE(f, ["onesf", "gU", "cm"], [BK[sb_]])
                            A(lambda e: e.activation(out=DTm[:, h, :], in_=bank[sb_][:, 0:128], func=AF.Exp, bias=gv[:, 12 + h:13 + h]), [BK[sb_], "gv"], ["DTm%d" % h])
                            transpose_f32(bank[sb_][:, 128:256], DTm[:, h, :], ["DTm%d" % h], [BK[sb_]])
                            V(lambda e: e.tensor_tensor(out=Dm[:], in0=bank[sb_][:, 128:256], in1=strictL, op=ALU.mult), [BK[sb_], "cm"], ["Dm"])
                            PE(lambda e: e.matmul(bank[sb_][:, 256:384], lhsT=qkb[:, 4 + h, :], rhs=qkb[:, 4 + h, :], start=True, stop=True), ["qkb"], [BK[sb_]])
                            V(lambda e: e.scalar_tensor_tensor(out=NX[:, h, :], in0=bank[sb_][:, 256:384], scalar=gv[:, 32 + h:33 + h], in1=Dm[:], op0=ALU.mult, op1=ALU.mult),
                              [BK[sb_], "gv", "Dm"], ["NX"])
                            PE(lambda e: e.matmul(bank[sb_][:, 384:512], lhsT=qkb[:, 4 + h, :], rhs=qkb[:, h, :], start=True, stop=True), ["qkb"], [BK[sb_]])
                            V(lambda e: e.tensor_tensor(out=attT[:, h, :], in0=bank[sb_][:, 384:512], in1=DTm[:, h, :], op=ALU.mult), [BK[sb_], "DTm%d" % h], ["attT"])
                        def f(e):
                            for h in range(4):
                                ins = e.transpose(out=trb[1][:, h * 128:(h + 1) * 128], in_=NX[:, h, :], identity=identb[:])
                            return ins
                        PE(f, ["NX", "identb"], [TRB[1]])
                        A(lambda e: e.activation(out=NY[:], in_=trb[1][:, :].rearrange("p (h d) -> p h d", d=128), func=AF.Copy), [TRB[1]], ["NY"])
                        mk = lambda j: gmask[:, j, :].unsqueeze(1).to_broadcast([128, 4, 128])
                        idb4 = identb[:].unsqueeze(1).to_broadcast([128, 4, 128])
                        V(lambda e: e.tensor_tensor(out=XY[0][:], in0=NX[:], in1=mk(0), op=ALU.mult), ["NX", "gmask"], ["XY0"])
                        G(lambda e: e.tensor_tensor(out=XY[1][:], in0=NY[:], in1=mk(0), op=ALU.mult), ["NY", "gmask"], ["XY1"])
                        V(lambda e: e.tensor_tensor(out=Pn[:], in0=XY[0][:], in1=idb4, op=ALU.add), ["XY0", "identb"], ["Pn"])
                        G(lambda e: e.tensor_tensor(out=Pt[:], in0=XY[1][:], in1=idb4, op=ALU.add), ["XY1", "identb"], ["Pt"])
                        cx, cy = 0, 1
                        def mm4(bk, lh, rh):
                            def f(e):
                                for h in range(4):
                                    ins = e.matmul(bank[bk][:, h * 128:(h + 1) * 128], lhsT=lh[:, h, :], rhs=rh[:, h, :], start=True, stop=True)
                                return ins
                            return f
                        b4 = lambda bk: bank[bk][:, :].rearrange("p (h d) -> p h d", d=128)
                        for stp in range(3):
                            nx_, ny_ = cx + 2, cy + 2
                            if nx_ > 3:
                                nx_, ny_ = nx_ - 4, ny_ - 4
                            PE(mm4(2, XY[cy], XY[cx]), ["XY%d" % cy, "XY%d" % cx], [BK[2]])
                            A(lambda e: e.activation(out=XY[nx_][:], in_=b4(2), func=AF.Copy), [BK[2]], ["XY%d" % nx_])
                            PE(mm4(3, XY[cx], XY[cy]), ["XY%d" % cy, "XY%d" % cx], [BK[3]])
                            A(lambda e: e.activation(out=XY[ny_][:], in_=b4(3), func=AF.Copy), [BK[3]], ["XY%d" % ny_])
                            PE(mm4(4, XY[ny_], Pn), ["XY%d" % ny_, "Pn"], [BK[4]])
                            PE(mm4(1, XY[nx_], Pt), ["XY%d" % nx_, "Pt"], [BK[1]])
                            V(lambda e: e.tensor_tensor(out=Pn[:], in0=b4(4), in1=Pn[:], op=ALU.add), [BK[4], "Pn"], ["Pn"])
                            V(lambda e: e.tensor_tensor(out=Pt[:], in0=b4(1), in1=Pt[:], op=ALU.add), [BK[1], "Pt"], ["Pt"])
                            cx, cy = nx_, ny_
                        YE = XY[0]; Zb = XY[1]
                        for lvl in range(3):
                            G(lambda e: e.tensor_tensor(out=YE[:], in0=NY[:], in1=mk(1 + lvl), op=ALU.mult), ["NY", "gmask"], ["XY0"])
                            PE(mm4(2, YE, Pn), ["XY0", "Pn"], [BK[2]])
                            A(lambda e: e.activation(out=Zb[:], in_=b4(2), func=AF.Copy), [BK[2]], ["XY1"])
                            if lvl < 2:
                                PE(mm4(3, Pt, Zb), ["Pt", "XY1"], [BK[3]])
                            PE(mm4(4, Zb, Pt), ["Pt", "XY1"], [BK[4]])
                            if lvl < 2:
                                V(lambda e: e.tensor_tensor(out=Pn[:], in0=b4(3), in1=Pn[:], op=ALU.add), [BK[3], "Pn"], ["Pn"])
                            V(lambda e: e.tensor_tensor(out=Pt[:], in0=b4(4), in1=Pt[:], op=ALU.add), [BK[4], "Pt"], ["Pt"])
                        TT = Pt; TTk = "Pt"
                        def f(e):
                            for h in range(4):
                                ins = e.matmul(bank[5][:, h * 128:(h + 1) * 128], lhsT=qkb[:, 4 + h, :], rhs=Sgb[:, h, :], start=True, stop=True)
                            return ins
                        PE(f, ["qkb", "Sgb"], [BK[5]])
                        V(lambda e: e.tensor_tensor(out=rr[:], in0=bank[5][:, :].rearrange("p (h d) -> p h d", d=128), in1=gv[:, 20:24].unsqueeze(2).to_broadcast([128, 4, 128]), op=ALU.mult),
                          [BK[5], "gv"], ["rr"])
                        V(lambda e: e.tensor_tensor(out=rr[:], in0=rr[:], in1=vtok[:], op=ALU.add), ["rr", "vtok"], ["rr"])
                        V(lambda e: e.tensor_tensor(out=rb[:], in0=rr[:], in1=gv[:, 0:4].unsqueeze(2).to_broadcast([128, 4, 128]), op=ALU.mult), ["rr", "gv"], ["rb"])
                        def f(e):
                            for h in range(4):
                                ins = e.matmul(bank[6][:, h * 128:(h + 1) * 128], lhsT=TT[:, h, :], rhs=rb[:, h, :], start=True, stop=True)
                            return ins
                        PE(f, [TTk, "rb"], [BK[6]])
                        A(lambda e: e.activation(out=vnb[:], in_=bank[6][:, :].rearrange("p (h d) -> p h d", d=128), func=AF.Copy), [BK[6]], ["vnb"])
                        def f(e):
                            for h in range(4):
                                ins = e.matmul(bank[5][:, h * 128:(h + 1) * 128], lhsT=qkb[:, h, :], rhs=Sgb[:, h, :], start=True, stop=True)
                            return ins
                        PE(f, ["qkb", "Sgb"], [BK[5]])
                        V(lambda e: e.tensor_tensor(out=o1[:], in0=bank[5][:, :].rearrange("p (h d) -> p h d", d=128), in1=gv[:, 16:20].unsqueeze(2).to_broadcast([128, 4, 128]), op=ALU.mult),
                          [BK[5], "gv"], ["rr"])
                        def f(e):
                            for h in range(4):
                                ins = e.matmul(bank[6][:, h * 128:(h + 1) * 128], lhsT=attT[:, h, :], rhs=vnb[:, h, :], start=True, stop=True)
                            return ins
                        PE(f, ["attT", "vnb"], [BK[6]])
                        V(lambda e: e.tensor_tensor(out=o1[:], in0=o1[:], in1=bank[6][:, :].rearrange("p (h d) -> p h d", d=128), op=ALU.add), ["rr", BK[6]], ["rr"])
                        def f(e):
                            for h in range(4):
                                ins = e.matmul(bank[5][:, h * 128:(h + 1) * 128], lhsT=kd[:, h, :], rhs=vnb[:, h, :], start=True, stop=True)
                            return ins
                        PE(f, ["kd", "vnb"], [BK[5]])
                        V(lambda e: e.tensor_tensor(out=Sg[:], in0=Sg[:], in1=gv[:, 24:28].unsqueeze(2).to_broadcast([128, 4, 128]), op=ALU.mult), ["Sg", "gv"], ["Sg"])
                        V(lambda e: e.tensor_tensor(out=Sg[:], in0=Sg[:], in1=bank[5][:, :].rearrange("p (h d) -> p h d", d=128), op=ALU.add), ["Sg", BK[5]], ["Sg"])
                        A(lambda e: e.activation(out=Sgb[:], in_=Sg[:], func=AF.Copy), ["Sg"], ["Sgb"])
                        for h in range(4):
                            A(lambda e: e.activation(out=jk[:], in_=o1[:, h, :], func=AF.Square, accum_out=s4[:, h:h + 1]), ["rr"], ["jkC", "s4C_%d" % h])
                        rstd_from_ss(s4[:, 4:8], s4[:, 0:4], 128, ["s4C_%d" % h for h in range(4)], "s4Cr")
                        for h in range(4):
                            V(lambda e: e.scalar_tensor_tensor(out=o1[:, h, :], in0=o1[:, h, :], scalar=s4[:, 4 + h:5 + h], in1=ngc[:], op0=ALU.mult, op1=ALU.mult), ["rr", "s4Cr", "ngc"], ["rr"])
                        V(lambda e: e.tensor_tensor(out=ob[:], in0=o1[:].rearrange("p h d -> p (h d)"), in1=zc[:], op=ALU.mult), ["rr", "zc"], ["obC"])
                        o_to_T(ob[:], "obC", oTt, "oTt")
                        outproj_update_prompt(i, oTt, "oTt")
                    S.dma("sp", gdn_p[l].rearrange("h k v -> k h v"), Sg[:], reads=["Sg"], slot="gdnp")
                    S.barrier()
            if do_sample:
                sample_finish()

        with ExitStack() as fs:
            junk = T(fs, "junkF", [128, D])
            ssq = T(fs, "ssqF", [128, 2])
            yt = [T(fs, "yt%d" % i, [128, D]) for i in range(2)]
            fin_g = T(fs, "fin_g", [128, D])
            bcast_row(fin_g[:], final_g[0:1, :], D, "fin_g")
            for i in range(NT):
                b = i % 2
                A(lambda e: e.activation(out=junk[:], in_=x_sb[:, i, :], func=AF.Square, accum_out=ssq[:, 0:1]), ["x%d" % i], ["junkF", "ssqF"])
                rstd_from_ss(ssq[:, 1:2], ssq[:, 0:1], D, ["ssqF"], "ssqF")
                V(lambda e: e.scalar_tensor_tensor(out=yt[b][:], in0=x_sb[:, i, :], scalar=ssq[:, 1:2], in1=fin_g[:], op0=ALU.mult, op1=ALU.mult), ["x%d" % i, "ssqF", "fin_g"], ["yt%d" % b])
                S.dma("sp", y_p[i * 128:(i + 1) * 128, :], yt[b][:], reads=["yt%d" % b], slot="yp%d" % b)
            A(lambda e: e.activation(out=junk[0:NO, :], in_=xo[:], func=AF.Square, accum_out=ssq[0:NO, 0:1]), ["xo"], ["junkF", "ssqF"])
            rstd_from_ss(ssq[0:NO, 1:2], ssq[0:NO, 0:1], D, ["ssqF"], "ssqF")
            V(lambda e: e.scalar_tensor_tensor(out=yt[0][0:NO, :], in0=xo[:], scalar=ssq[0:NO, 1:2], in1=fin_g[0:NO, :], op0=ALU.mult, op1=ALU.mult), ["xo", "ssqF", "fin_g"], ["yt0"])
            S.dma("sp", y_s, yt[0][0:NO, :], reads=["yt0"], slot="ys")
            S.finish("sp")
        print("program built: ops=%d waits=%d" % (S.nops, S.nwaits), flush=True)
    return nc


def _bucket(n):
    n = np.asarray(n)
    nn = np.maximum(n, 0)
    exact = 16
    large = exact + (np.log(np.maximum(nn, 1).astype(np.float32) / np.float32(exact)).astype(np.float32)
                     / np.float32(math.log(128 / exact)) * np.float32(32 - exact)).astype(np.int32)
    return np.where(nn < exact, nn, np.minimum(large, 31))


def make_consts(core):
    cm = np.zeros((128, 5, 128), np.float32)
    r = np.arange(128)
    cm[:, 0, :] = np.eye(128)
    cm[:, 1, :] = (r[:, None] <= r[None, :])
    cm[:, 2, :] = np.where(r[:, None] <= r[None, :], 0.0, -1.0e6)
    cm[:, 3, :] = (r[None, :] < r[:, None])
    cm[:, 4, :] = np.eye(128)[::-1]
    oh = np.zeros((33, 512), np.float32)
    n = np.arange(512) - 127
    bk = _bucket(n)
    for i in range(512):
        if n[i] < 0:
            oh[32, i] = 1.0
        else:
            oh[bk[i], i] = 1.0
    p = np.arange(128)
    ohs = np.zeros((32, 128), np.float32)
    ohs[_bucket(128 - p), p] = 1.0
    cv = np.zeros((128, 16), np.float32)
    cv[:, 0] = p
    cv[:, 1] = p + NPOOL * 128
    for g in range(8):
        cv[:, 8 + g] = (p // 16 == g)
    return cm, oh, ohs, cv


def make_selown(core):
    so = np.zeros((NO, NS), np.float32)
    for i in range(NO):
        so[i, (NO * core + i) % NS] = 1.0
    return so


def make_gmask():
    r = np.arange(128)
    bd = lambda b: (r[:, None] // b == r[None, :] // b).astype(np.float32)
    g = np.zeros((128, 4, 128), np.float32)
    g[:, 0] = bd(16); g[:, 1] = bd(32) - bd(16); g[:, 2] = bd(64) - bd(32); g[:, 3] = 1.0 - bd(64)
    return g


_CACHE = {}
_RUN_KW = {}


def kernel(x_prompt, x_sample, c_prompt, c_sample, cache_k, cache_v, state_ssm, state_ssm_conv,
           state_gdn, state_gdn_conv, page_table, w_ada, b_ada, w_in, w_out, rel_bias, attn_lambda,
           attn_subln_g, ssm_conv_w, ssm_conv_b, ssm_dt_bias, ssm_A_log, ssm_D, ssm_norm_g,
           gdn_conv_w, gdn_dt_bias, gdn_A_log, gdn_norm_g, final_norm_g):
    f = lambda a: np.ascontiguousarray(np.asarray(a, dtype=np.float32))
    if "nc" not in _CACHE:
        _CACHE["nc"] = build_program()
    nc = _CACHE["nc"]
    x_prompt = f(x_prompt); x_sample = f(x_sample).reshape(NS, D); c_prompt = f(c_prompt); c_sample = f(c_sample)
    cache_k = np.asarray(cache_k); cache_v = np.asarray(cache_v)
    lamc = np.array([[0.8 - 0.6 * math.exp(-0.3 * 0), 0.8 - 0.6 * math.exp(-0.3 * 1), 0, 0]], np.float32)
    shared = {
        "xs_all": x_sample, "cs_all": c_sample,
        "w_ada": f(w_ada), "b_ada": f(b_ada), "w_in": f(w_in), "w_out": f(w_out), "rel_bias": f(rel_bias),
        "attn_lambda": f(attn_lambda).reshape(2, 256), "subln": f(attn_subln_g),
        "ssm_conv_w": f(ssm_conv_w).reshape(2, 4096), "ssm_conv_b": f(ssm_conv_b), "ssm_dt_bias": f(ssm_dt_bias),
        "ssm_A_log": f(ssm_A_log), "ssm_D": f(ssm_D), "ssm_norm_g": f(ssm_norm_g),
        "gdn_conv_w": f(gdn_conv_w).reshape(2, 6144), "gdn_dt_bias": f(gdn_dt_bias), "gdn_A_log": f(gdn_A_log),
        "gdn_norm_g": f(gdn_norm_g), "final_g": f(final_norm_g).reshape(1, D), "lamc": lamc, "gmask": make_gmask(),
    }
    in_maps = []
    poolk_full = np.ascontiguousarray(cache_k, dtype=np.float32).reshape(-1, 512)
    poolv_full = np.ascontiguousarray(cache_v, dtype=np.float32).reshape(-1, 512)
    for c in range(NCORE):
        cm, oh, ohs, cv = make_consts(c)
        m = dict(shared)
        m.update({
            "xp": x_prompt[c], "cp": c_prompt[c:c + 1],
            "xs_own": x_sample[NO * c:NO * (c + 1)], "cs_own": c_sample[NO * c:NO * (c + 1)],
            "poolk": poolk_full, "poolv": poolv_full,
            "pt": np.ascontiguousarray(np.asarray(page_table, dtype=np.int32)[NO * c:NO * (c + 1)].reshape(1, NO * NPAGE)),
            "st_ssm": f(state_ssm[:, NO * c:NO * (c + 1)]), "st_sconv": f(state_ssm_conv[:, NO * c:NO * (c + 1)]),
            "st_gdn": f(state_gdn[:, NO * c:NO * (c + 1)]), "st_gconv": f(state_gdn_conv[:, NO * c:NO * (c + 1)]),
            "cmat": cm, "oh1d": oh, "ohs": ohs, "cvec": cv, "selown": make_selown(c),
        })
        in_maps.append(m)
    res = run_bass_kernel_spmd(nc, in_maps, core_ids=list(range(NCORE)), **_RUN_KW)
    _CACHE["last_exec_time_ns"] = getattr(res, "exec_time_ns", None)
    R = res.results
    cat = lambda name, ax: np.concatenate([np.asarray(R[c][name])[None] if ax is None else np.asarray(R[c][name]) for c in range(NCORE)], axis=0 if ax is None else ax)
    y_prompt = np.stack([R[c]["y_p"] for c in range(NCORE)], 0)
    y_sample = np.concatenate([R[c]["y_s"] for c in range(NCORE)], 0).reshape(-1, 1, D)
    k_prompt = np.stack([R[c]["k_p"] for c in range(NCORE)], 1).reshape(2, NCORE, L, 4, 128)
    v_prompt = np.stack([R[c]["v_p"] for c in range(NCORE)], 1).reshape(2, NCORE, L, 4, 128)
    ssm_prompt = np.stack([R[c]["ssm_p"] for c in range(NCORE)], 1)
    sconv_prompt = np.stack([R[c]["sconv_p"] for c in range(NCORE)], 1)
    gdn_prompt = np.stack([R[c]["gdn_p"] for c in range(NCORE)], 1)
    gconv_prompt = np.stack([R[c]["gconv_p"] for c in range(NCORE)], 1)
    k_sample = np.concatenate([R[c]["k_s"] for c in range(NCORE)], 1).reshape(2, -1, 1, 4, 128)
    v_sample = np.concatenate([R[c]["v_s"] for c in range(NCORE)], 1).reshape(2, -1, 1, 4, 128)
    ssm_sample = np.concatenate([R[c]["ssm_s"] for c in range(NCORE)], 1)
    sconv_sample = np.concatenate([R[c]["sconv_s"] for c in range(NCORE)], 1)
    gdn_sample = np.concatenate([R[c]["gdn_s"] for c in range(NCORE)], 1)
    gconv_sample = np.concatenate([R[c]["gconv_s"] for c in range(NCORE)], 1)
    outs = (y_prompt, y_sample, k_prompt, v_prompt, ssm_prompt, sconv_prompt, gdn_prompt, gconv_prompt,
            k_sample, v_sample, ssm_sample, sconv_sample, gdn_sample, gconv_sample)
    return tuple(np.ascontiguousarray(o, dtype=np.float32) for o in outs)
```
